# Optimizing a Trainium2 kernel written in Bass

```python
import math
import jax, jax.numpy as jnp
from jax import lax
import numpy as np

D_MODEL = 2048
BATCH = 4
SEQ = 2048
DEPTH = 4

GRID_W = 64
CTX_LEN = 256
N_MIXERS = 3
N_HEADS = 16
HEAD_DIM = 64
V_HEAD_DIM = 2 * HEAD_DIM
ROPE_BASE = 10000.0
AXIS_ROT = HEAD_DIM // 2
Q_BLOCK = 128
SUBLN_EPS = 1e-5
N_FFT_GROUPS = 4
POOL_WINDOWS = (2, 4, 8, 16)
N_POOL_GROUPS = len(POOL_WINDOWS)
D_FF = 4 * D_MODEL
NORM_EPS = 1e-6
N_MOD = 6
N_A = len(range(0, DEPTH, N_MIXERS))
N_B = len(range(1, DEPTH, N_MIXERS))
N_C = len(range(2, DEPTH, N_MIXERS))

kernel_name = 'hybrid_diffattn_fourier_pool_dit'

F32 = jnp.float32


def rmsnorm(x, g, eps=NORM_EPS):
    x32 = x.astype(F32)
    y = x32 * lax.rsqrt(jnp.mean(x32 * x32, axis=-1, keepdims=True) + eps)
    return y.astype(x.dtype) * g


def ada_mod(cond, w, b):
    m = jax.nn.silu(cond) @ w + b
    m = m.reshape(m.shape[:-1] + (N_MOD, 1, D_MODEL))
    return [m[..., k, :, :] for k in range(N_MOD)]


def axial_rope_tables(seq_len):
    rows = seq_len // GRID_W
    row = jnp.repeat(jnp.arange(rows), GRID_W).astype(F32)
    col = jnp.tile(jnp.arange(GRID_W), rows).astype(F32)
    n_freq = AXIS_ROT // 2
    inv = 1.0 / (ROPE_BASE ** (jnp.arange(n_freq, dtype=F32) / n_freq))
    ang = jnp.stack([row[:, None] * inv, col[:, None] * inv], axis=1)
    return jnp.cos(ang), jnp.sin(ang)


def apply_axial_rope(x, cos, sin):
    xs = x.reshape(x.shape[:-1] + (2, 2, AXIS_ROT // 2))
    a = xs[..., 0, :]
    b = xs[..., 1, :]
    cs = cos[None, :, None, None].astype(x.dtype)
    sn = sin[None, :, None, None].astype(x.dtype)
    out = jnp.stack([a * cs - b * sn, a * sn + b * cs], axis=-2)
    return out.reshape(x.shape)


def diff_attention(ul, uc, cos, sin, w_qkv, w_o, lq1, lk1, lq2, lk2, g_subln, lambda_init, with_ctx_out):
    B, L, _ = ul.shape

    def proj(u):
        q, k, v = jnp.split(u @ w_qkv, 3, axis=-1)
        sh = u.shape[:2]
        return (q.reshape(sh + (N_HEADS, 2, HEAD_DIM)),
                k.reshape(sh + (N_HEADS, 2, HEAD_DIM)),
                v.reshape(sh + (N_HEADS, V_HEAD_DIM)))

    ql, kl, vl = proj(ul)
    qc, kc, vc = proj(uc)
    ql = apply_axial_rope(ql, cos, sin)
    kl = apply_axial_rope(kl, cos, sin)
    lam = (jnp.exp(jnp.sum(lq1.astype(F32) * lk1.astype(F32)))
           - jnp.exp(jnp.sum(lq2.astype(F32) * lk2.astype(F32))) + lambda_init)
    scale = 1.0 / math.sqrt(HEAD_DIM)

    def attend(q, k, v):
        s = jnp.einsum('bqhcd,bkhcd->bhcqk', q, k).astype(F32) * scale
        p = jax.nn.softmax(s, axis=-1)
        a = (p[:, :, 0] - lam * p[:, :, 1]).astype(v.dtype)
        return jnp.einsum('bhqk,bkhe->bqhe', a, v)

    k_all = jnp.concatenate([kl, kc], axis=1)
    v_all = jnp.concatenate([vl, vc], axis=1)
    nb = L // Q_BLOCK
    qb = ql.reshape(B, nb, Q_BLOCK, N_HEADS, 2, HEAD_DIM).swapaxes(0, 1)
    ol = lax.map(lambda q: attend(q, k_all, v_all), qb)
    ol = ol.swapaxes(0, 1).reshape(B, L, N_HEADS, V_HEAD_DIM)

    def finish(o):
        o = rmsnorm(o, g_subln, SUBLN_EPS) * (1.0 - lambda_init)
        return o.reshape(o.shape[:2] + (N_HEADS * V_HEAD_DIM,)) @ w_o

    yl = finish(ol)
    yc = finish(attend(qc, kc, vc)) if with_ctx_out else None
    return yl, yc


def fourier_mix(u, w_out):
    B, L, D = u.shape
    ug = u.reshape(B, L, N_FFT_GROUPS, D // N_FFT_GROUPS).astype(F32)
    f = jnp.fft.fft2(ug, axes=(1, 3), norm='ortho').real
    return f.astype(u.dtype).reshape(B, L, D) @ w_out


def pool_mix(u, w_pool, pool_scale):
    L = u.shape[1]
    cg = D_MODEL // N_POOL_GROUPS
    t = jnp.arange(L)
    outs = []
    for g, w in enumerate(POOL_WINDOWS):
        ug = u[..., g * cg:(g + 1) * cg]
        cs = jnp.cumsum(ug.astype(F32), axis=1)
        cs = jnp.concatenate([jnp.zeros_like(cs[:, :1]), cs], axis=1)
        lo = jnp.clip(t - w // 2, 0, L)
        hi = jnp.clip(t - w // 2 + w, 0, L)
        cnt = (hi - lo).astype(F32)[None, :, None]
        mean = (jnp.take(cs, hi, axis=1) - jnp.take(cs, lo, axis=1)) / cnt
        outs.append((mean.astype(u.dtype) - ug) @ w_pool[g])
    return jnp.concatenate(outs, axis=-1) * pool_scale


def sq_relu_mlp(u, w1, w2):
    h = jax.nn.relu(u @ w1)
    return (h * h) @ w2


def setup_inputs(seed: int = 0) -> dict:
    key = jax.random.key(seed)
    ks = jax.random.split(key, 24)
    D = D_MODEL
    cg = D // N_POOL_GROUPS
    nrm = lambda k, shape, s: jax.random.normal(k, shape, F32) * s
    return {
        'x': nrm(ks[0], (BATCH, SEQ, D), 1.0),
        'c': nrm(ks[1], (BATCH, D), 1.0),
        'ctx': nrm(ks[2], (BATCH, CTX_LEN, D), 1.0),
        'c_ctx': nrm(ks[3], (D,), 1.0),
        'w_mod': nrm(ks[4], (DEPTH, D, N_MOD * D), 0.5 * D ** -0.5),
        'b_mod': nrm(ks[5], (DEPTH, N_MOD * D), 0.01),
        'g_mix_pre': 1.0 + nrm(ks[6], (DEPTH, D), 0.02),
        'g_mix_post': 1.0 + nrm(ks[7], (DEPTH, D), 0.02),
        'g_mlp_pre': 1.0 + nrm(ks[8], (DEPTH, D), 0.02),
        'g_mlp_post': 1.0 + nrm(ks[9], (DEPTH, D), 0.02),
        'w_qkv': nrm(ks[10], (N_A, D, 3 * D), D ** -0.5),
        'w_attn_out': nrm(ks[11], (N_A, N_HEADS * V_HEAD_DIM, D), (N_HEADS * V_HEAD_DIM) ** -0.5),
        'lambda_q1': nrm(ks[12], (N_A, HEAD_DIM), 0.1),
        'lambda_k1': nrm(ks[13], (N_A, HEAD_DIM), 0.1),
        'lambda_q2': nrm(ks[14], (N_A, HEAD_DIM), 0.1),
        'lambda_k2': nrm(ks[15], (N_A, HEAD_DIM), 0.1),
        'g_subln': 1.0 + nrm(ks[16], (N_A, V_HEAD_DIM), 0.02),
        'w_fourier_out': nrm(ks[17], (N_B, D, D), D ** -0.5),
        'w_pool': nrm(ks[18], (N_C, N_POOL_GROUPS, cg, cg), cg ** -0.5),
        'pool_scale': 1.0 + nrm(ks[19], (N_C, D), 0.1),
        'w_mlp_in': nrm(ks[20], (DEPTH, D, D_FF), D ** -0.5),
        'w_mlp_out': nrm(ks[21], (DEPTH, D_FF, D), D_FF ** -0.5),
    }


def reference(x, c, ctx, c_ctx, w_mod, b_mod, g_mix_pre, g_mix_post, g_mlp_pre, g_mlp_post,
              w_qkv, w_attn_out, lambda_q1, lambda_k1, lambda_q2, lambda_k2, g_subln,
              w_fourier_out, w_pool, pool_scale, w_mlp_in, w_mlp_out):
    xl, xc = x, ctx
    cos, sin = axial_rope_tables(xl.shape[1])
    ia = ib = ic = 0
    for i in range(DEPTH):
        last = i == DEPTH - 1
        kind = i % N_MIXERS
        need_ctx_in = (not last) or kind == 0
        sh1, sc1, gt1, sh2, sc2, gt2 = ada_mod(c, w_mod[i], b_mod[i])
        csh1, csc1, cgt1, csh2, csc2, cgt2 = ada_mod(c_ctx, w_mod[i], b_mod[i])

        ul = rmsnorm(xl, g_mix_pre[i]) * (1.0 + sc1) + sh1
        uc = rmsnorm(xc, g_mix_pre[i]) * (1.0 + csc1) + csh1 if need_ctx_in else None
        if kind == 0:
            lambda_init = 0.8 - 0.6 * math.exp(-0.3 * i)
            yl, yc = diff_attention(ul, uc, cos, sin, w_qkv[ia], w_attn_out[ia],
                                    lambda_q1[ia], lambda_k1[ia], lambda_q2[ia], lambda_k2[ia],
                                    g_subln[ia], lambda_init, not last)
            ia += 1
        elif kind == 1:
            yl = fourier_mix(ul, w_fourier_out[ib])
            yc = None if last else fourier_mix(uc, w_fourier_out[ib])
            ib += 1
        else:
            yl = pool_mix(ul, w_pool[ic], pool_scale[ic])
            yc = None if last else pool_mix(uc, w_pool[ic], pool_scale[ic])
            ic += 1
        xl = xl + gt1 * rmsnorm(yl, g_mix_post[i])

        vl = rmsnorm(xl, g_mlp_pre[i]) * (1.0 + sc2) + sh2
        xl = xl + gt2 * rmsnorm(sq_relu_mlp(vl, w_mlp_in[i], w_mlp_out[i]), g_mlp_post[i])

        if not last:
            xc = xc + cgt1 * rmsnorm(yc, g_mix_post[i])
            vc = rmsnorm(xc, g_mlp_pre[i]) * (1.0 + csc2) + csh2
            xc = xc + cgt2 * rmsnorm(sq_relu_mlp(vc, w_mlp_in[i], w_mlp_out[i]), g_mlp_post[i])
    return xl
```

```python
import math
from contextlib import ExitStack

import numpy as np
import ml_dtypes

import concourse.bass as bass
import concourse.mybir as mybir
from concourse.bass_utils import run_bass_kernel_spmd

F32 = mybir.dt.float32
BF16 = mybir.dt.bfloat16
AF = mybir.ActivationFunctionType
ALU = mybir.AluOpType

P = 128
D = 2048
KC = 16
L = 2048
LH = 1024
NCTX = 256
NKEY = L + NCTX
DFF = 8192
FC = DFF // P
NH = 16
DEPTH = 4
NORM_EPS = 1e-6
SUBLN_EPS = 1e-5
NCORES = 8
MODW = 6 * D
MSH = MODW // 2


class Buf:
    __slots__ = ("name", "last_w", "readers", "dsem", "dcnt")

    def __init__(self, name):
        self.name = name
        self.last_w = None
        self.readers = []
        self.dsem = None
        self.dcnt = 0


class Sched:
    ENG = ("pe", "act", "dve", "pool", "sp")

    def __init__(self, nc, stack):
        self.nc = nc
        self.stack = stack
        self.eng = {"pe": nc.tensor, "act": nc.scalar, "dve": nc.vector,
                    "pool": nc.gpsimd, "sp": nc.sync}
        self.esem = {e: stack.enter_context(nc.semaphore("es_" + e)) for e in self.ENG}
        self.ecnt = {e: 0 for e in self.ENG}
        self.known = {e: {} for e in self.ENG}
        self.pool = []
        self.live = []
        self.nsem = 0
        self.ccsem = None
        self.cccnt = 0

    def _need(self, e, ev):
        if ev is None:
            return
        sem, cnt = ev
        k = self.known[e]
        if k.get(id(sem), 0) >= cnt:
            return
        if e == "pe" and sem is self.esem["pe"]:
            return
        self.eng[e].wait_ge(sem, cnt)
        k[id(sem)] = cnt

    def _deps(self, e, reads, writes):
        for b in reads:
            self._need(e, b.last_w)
        for b in writes:
            self._need(e, b.last_w)
            for r in b.readers:
                self._need(e, r)

    @staticmethod
    def _commit(ev, reads, writes):
        for b in reads:
            b.readers.append(ev)
            if len(b.readers) > 64:
                b.readers = b.readers[-48:]
        for b in writes:
            b.last_w = ev
            b.readers = []

    def op(self, e, fn, reads=(), writes=()):
        self._deps(e, reads, writes)
        ins = fn(self.eng[e])
        self.ecnt[e] += 1
        ins.then_inc(self.esem[e], 1)
        ev = (self.esem[e], self.ecnt[e])
        self._commit(ev, reads, writes)
        return ev

    def dma(self, q, owner, fn, reads=(), writes=()):
        if owner.dsem is None:
            if self.pool:
                owner.dsem, owner.dcnt = self.pool.pop()
            else:
                owner.dsem = self.stack.enter_context(self.nc.semaphore("ds%d" % self.nsem))
                owner.dcnt = 0
                self.nsem += 1
            self.live.append(owner)
        self._deps(q, reads, writes)
        lst = fn(self.eng[q])
        for ins in lst:
            ins.then_inc(owner.dsem, 16)
            owner.dcnt += 16
        ev = (owner.dsem, owner.dcnt)
        self._commit(ev, reads, writes)
        return ev

    def coll(self, fns, reads, writes):
        if self.ccsem is None:
            self.ccsem = self.stack.enter_context(self.nc.semaphore("ccsem"))
        self._deps("pool", reads, writes)
        for fn in fns:
            ins = fn(self.eng["pool"])
            self.cccnt += 1
            ins.then_inc(self.ccsem, 1)
        ev = (self.ccsem, self.cccnt)
        self._commit(ev, reads, writes)
        return ev

    def barrier(self, engines=None):
        engines = engines or self.ENG
        for e in engines:
            for f in self.ENG:
                if f != e and self.ecnt[f] > 0:
                    self._need(e, (self.esem[f], self.ecnt[f]))
            for b in self.live:
                if b.dcnt > 0:
                    self._need(e, (b.dsem, b.dcnt))
            if self.cccnt > 0:
                self._need(e, (self.ccsem, self.cccnt))

    def release_dma_sems(self):
        for b in self.live:
            self.pool.append((b.dsem, b.dcnt))
            b.dsem = None
        self.live = []


class KB:
    def __init__(self, nc, st):
        self.nc = nc
        self.st = st
        self.S = Sched(nc, st)
        self.uid = 0
        self.ps = []
        self.pb = []
        self.psall = st.enter_context(nc.psum_tensor("psall", [P, 8, 512], F32))
        for i in range(8):
            self.ps.append(self.psall[:, i, :])
            self.pb.append(Buf("psb%d" % i))
        self.ones, self.b_ones = self.sb(st, "ones", [P, P], BF16)
        self.S.op("pool", lambda e: e.memset(self.ones[:], 1.0), writes=[self.b_ones])
        self.epsb, self.b_eps = self.sb(st, "epsb", [P, 2], F32)
        self.S.op("pool", lambda e: e.memset(self.epsb[:, 0:1], NORM_EPS), writes=[self.b_eps])
        self.S.op("pool", lambda e: e.memset(self.epsb[:, 1:2], SUBLN_EPS), writes=[self.b_eps])
        self.dr = {}
        self.wq = 0

    def sb(self, ph, name, shape, dt):
        self.uid += 1
        nm = "%s_%d" % (name, self.uid)
        t = ph.enter_context(self.nc.sbuf_tensor(nm, list(shape), dt))
        return t, Buf(nm)

    def rb(self, *key):
        b = self.dr.get(key)
        if b is None:
            b = Buf("dr_" + "_".join(str(k) for k in key))
            self.dr[key] = b
        return b

    def end_phase(self):
        self.S.barrier()
        self.S.release_dma_sems()

    def scratch(self, ph, nx=3):
        sc = {}
        sc["sq"] = [self.sb(ph, "sq", [P, 512], BF16) for _ in range(2)]
        sc["rt"] = self.sb(ph, "rt", [P, 512], F32)
        sc["rstd"] = [self.sb(ph, "rstd", [P, 512], F32) for _ in range(2)]
        sc["xk"] = [self.sb(ph, "xk", [P, 512], F32) for _ in range(nx)]
        sc["tmp"] = [self.sb(ph, "tmp", [P, 512], F32) for _ in range(2)]
        sc["ok"] = [self.sb(ph, "ok", [P, 512], F32) for _ in range(2)]
        sc["i"] = 0
        return sc

    def stats(self, sc, chunks, T, n, eps, bank, slot):
        S = self.S
        ps, pb = self.ps[bank], self.pb[bank]
        nk = len(chunks)
        for k, (ap, buf) in enumerate(chunks):
            sq, bsq = sc["sq"][k % 2]
            S.op("act", lambda e: e.activation(out=sq[:, :T], in_=ap, func=AF.Square),
                 reads=[buf], writes=[bsq])
            S.op("pe", lambda e: e.matmul(ps[:, :T], self.ones[:], sq[:, :T],
                                          start=(k == 0), stop=(k == nk - 1)),
                 reads=[bsq, self.b_ones], writes=[pb])
        rt, brt = sc["rt"]
        rstd, brstd = sc["rstd"][slot]
        S.op("act", lambda e: e.activation(out=rt[:, :T], in_=ps[:, :T], func=AF.Ln,
                                           bias=self.epsb[:, 0:1] if eps == NORM_EPS else self.epsb[:, 1:2], scale=1.0 / n),
             reads=[pb, self.b_eps], writes=[brt])
        S.op("act", lambda e: e.activation(out=rstd[:, :T], in_=rt[:, :T], func=AF.Exp, scale=-0.5),
             reads=[brt], writes=[brstd])
        return rstd, brstd

    def prenorm(self, sc, src, skey, t0, T, gsc, sh, bmod, uT, buT, ucol0, bank=7):
        S = self.S
        ps, pb = self.ps[bank], self.pb[bank]
        nx = len(sc["xk"])
        if not callable(src):
            src_ap = src
            src = lambda k, c0, c1: src_ap[k * P:(k + 1) * P, c0:c1]
        for k in range(KC):
            xk, bxk = sc["xk"][sc["i"] % nx]
            sc["i"] += 1
            S.dma("sp", bxk, lambda e: [e.dma_start(out=xk[:, :T], in_=src(k, t0, t0 + T))],
                  reads=[self.rb(skey, t0 // 512, k)], writes=[bxk])
            sq, bsq = sc["sq"][k % 2]
            S.op("act", lambda e: e.activation(out=sq[:, :T], in_=xk[:, :T], func=AF.Square),
                 reads=[bxk], writes=[bsq])
            S.op("pe", lambda e: e.matmul(ps[:, :T], self.ones[:], sq[:, :T],
                                          start=(k == 0), stop=(k == KC - 1)),
                 reads=[bsq, self.b_ones], writes=[pb])
        rt, brt = sc["rt"]
        rstd, brstd = sc["rstd"][0]
        S.op("act", lambda e: e.activation(out=rt[:, :T], in_=ps[:, :T], func=AF.Ln,
                                           bias=self.epsb[:, 0:1], scale=1.0 / D),
             reads=[pb, self.b_eps], writes=[brt])
        S.op("act", lambda e: e.activation(out=rstd[:, :T], in_=rt[:, :T], func=AF.Exp, scale=-0.5),
             reads=[brt], writes=[brstd])
        for k in range(KC):
            xk, bxk = sc["xk"][sc["i"] % nx]
            sc["i"] += 1
            S.dma("sp", bxk, lambda e: [e.dma_start(out=xk[:, :T], in_=src(k, t0, t0 + T))],
                  reads=[self.rb(skey, t0 // 512, k)], writes=[bxk])
            tmp, btmp = sc["tmp"][k % 2]
            S.op("dve", lambda e: e.scalar_tensor_tensor(out=tmp[:, :T], in0=xk[:, :T], scalar=gsc[:, k:k + 1],
                                                         in1=rstd[:, :T], op0=ALU.mult, op1=ALU.mult),
                 reads=[bxk, brstd, bmod], writes=[btmp])
            S.op("act", lambda e: e.activation(out=uT[:, k, ucol0:ucol0 + T], in_=tmp[:, :T], func=AF.Identity,
                                               bias=sh[:, k:k + 1], scale=1.0),
                 reads=[btmp, bmod], writes=[buT])

    def postnorm(self, sc, yT, byT, T, gate, bmod, src, skey, dst, dkey, t0, bank=7, store_q="sp"):
        S = self.S
        chunks = [(yT[:, k, :T], byT) for k in range(KC)]
        rstd, brstd = self.stats(sc, chunks, T, D, NORM_EPS, bank, 1)
        nx = len(sc["xk"])
        base = sc["i"]
        sc["i"] += KC
        LA = max(0, min(2, nx - 1))

        def ld(k):
            xk, bxk = sc["xk"][(base + k) % nx]
            S.dma("sp", bxk, lambda e: [e.dma_start(out=xk[:, :T], in_=src[k * P:(k + 1) * P, t0:t0 + T])],
                  reads=[self.rb(skey, t0 // 512, k)], writes=[bxk])

        for k in range(LA):
            ld(k)
        for k in range(KC):
            if k + LA < KC:
                ld(k + LA)
            xk, bxk = sc["xk"][(base + k) % nx]
            tmp, btmp = sc["tmp"][k % 2]
            S.op("dve", lambda e: e.scalar_tensor_tensor(out=tmp[:, :T], in0=yT[:, k, :T], scalar=gate[:, k:k + 1],
                                                         in1=rstd[:, :T], op0=ALU.mult, op1=ALU.mult),
                 reads=[byT, brstd, bmod], writes=[btmp])
            ok, bok = sc["ok"][k % 2]
            S.op("pool", lambda e: e.tensor_tensor(out=ok[:, :T], in0=tmp[:, :T], in1=xk[:, :T], op=ALU.add),
                 reads=[btmp, bxk], writes=[bok])
            S.dma(store_q, bok, lambda e: [e.dma_start(out=dst[k * P:(k + 1) * P, t0:t0 + T], in_=ok[:, :T])],
                  reads=[bok], writes=[self.rb(dkey, t0 // 512, k)])

    def wstream(self, wst, W, kcn, ncols, tiles, evac, banks, cb=None, act_stationary=False,
                prefetch_only=False, wkey=None, before_last=None):
        S = self.S
        if cb is None:
            cb = 8192 // kcn
        cb = min(cb, ncols)
        npiece = max(1, (kcn * cb) // 4096)
        blocks = list(range(0, ncols, cb))
        loaded = {}
        pre = wst.setdefault("pre", {})
        if prefetch_only:
            blocks = blocks[:1]

        def load_block(c0):
            wi = self.wq
            self.wq += 1
            wb, bwb = wst["wb"][wi % 2]
            wv = wb[:, 0:kcn * cb].rearrange("p (k m) -> p k m", k=kcn)
            for pc in range(npiece):
                si = wst["si"]
                wst["si"] += 1
                stg, bstg = wst["stg"][si % len(wst["stg"])]
                if kcn * cb <= 4096:
                    kk0, kk1, cc0, cc1 = 0, kcn, 0, cb
                else:
                    h = kcn // npiece
                    kk0, kk1, cc0, cc1 = pc * h, (pc + 1) * h, 0, cb
                nk, ncl = kk1 - kk0, cc1 - cc0
                sv = stg[:, 0:nk * ncl].rearrange("p (k m) -> p k m", k=nk)
                src = W[kk0 * P:kk1 * P, c0 + cc0:c0 + cc1].rearrange("(k p) m -> p k m", p=P)
                S.dma("sp", bstg, lambda e: [e.dma_start(out=sv, in_=src)], writes=[bstg])
                ce = ("dve", "act")[si % 2]
                if ce == "dve":
                    S.op("dve", lambda e: e.tensor_copy(out=wv[:, kk0:kk1, cc0:cc1], in_=sv),
                         reads=[bstg], writes=[bwb])
                else:
                    S.op("act", lambda e: e.activation(out=wv[:, kk0:kk1, cc0:cc1], in_=sv, func=AF.Copy),
                         reads=[bstg], writes=[bwb])
            loaded[c0] = (wv, bwb)

        def compute_block(c0):
            wv, bwb = loaded.pop(c0)
            if act_stationary:
                for (lhs_fn, rbuf, T, tag) in tiles:
                    bi = banks[wst["bi"] % len(banks)]
                    wst["bi"] += 1
                    ps, pb = self.ps[bi], self.pb[bi]
                    S._deps("pe", [bwb, rbuf], [pb])
                    for k in range(kcn - 1):
                        self.nc.tensor.matmul(ps[:, :cb], lhs_fn(k), wv[:, k, 0:cb], start=(k == 0), stop=False)
                    S.op("pe", lambda e: e.matmul(ps[:, :cb], lhs_fn(kcn - 1), wv[:, kcn - 1, 0:cb],
                                                  start=(kcn == 1), stop=True),
                         reads=[bwb, rbuf], writes=[pb])
                    evac(c0 // cb, tag, ps[:, :cb], pb)
                return
            for m in range(cb // P):
                for (rhs_fn, rbuf, T, tag) in tiles:
                    bi = banks[wst["bi"] % len(banks)]
                    wst["bi"] += 1
                    ps, pb = self.ps[bi], self.pb[bi]
                    S._deps("pe", [bwb, rbuf], [pb])
                    for k in range(kcn - 1):
                        self.nc.tensor.matmul(ps[:, :T], wv[:, k, m * P:(m + 1) * P], rhs_fn(k),
                                              start=(k == 0), stop=False)
                    S.op("pe", lambda e: e.matmul(ps[:, :T], wv[:, kcn - 1, m * P:(m + 1) * P], rhs_fn(kcn - 1),
                                                  start=(kcn == 1), stop=True),
                         reads=[bwb, rbuf], writes=[pb])
                    evac((c0 // P) + m, tag, ps[:, :T], pb)

        if prefetch_only:
            load_block(blocks[0])
            pre[wkey] = loaded.pop(blocks[0])
            return
        if wkey is not None and wkey in pre:
            loaded[blocks[0]] = pre.pop(wkey)
        else:
            load_block(blocks[0])
        for i, c0 in enumerate(blocks):
            if i + 1 < len(blocks):
                load_block(blocks[i + 1])
            elif before_last is not None:
                before_last()
            compute_block(c0)

    def wstream_kslab(self, wst, W, kct, ncols, rhs_fn, rbuf, T, evac, wkey=None, before_last=None,
                      prefetch_only=False):
        S = self.S
        SL = 16
        nsl = kct // SL
        pre = wst.setdefault("pre", {})
        blocks = [(c0, sl) for c0 in range(0, ncols, 512) for sl in range(nsl)]

        def load(bk):
            c0, sl = bk
            wi = self.wq
            self.wq += 1
            wb, bwb = wst["wb"][wi % 2]
            wv = wb[:, 0:SL * 512].rearrange("p (k m) -> p k m", k=SL)
            for pc in range(2):
                si = wst["si"]
                wst["si"] += 1
                stg, bstg = wst["stg"][si % len(wst["stg"])]
                sv = stg[:, 0:8 * 512].rearrange("p (k m) -> p k m", k=8)
                r0 = (sl * SL + pc * 8) * P
                src = W[r0:r0 + 8 * P, c0:c0 + 512].rearrange("(k p) m -> p k m", p=P)
                S.dma("sp", bstg, lambda e: [e.dma_start(out=sv, in_=src)], writes=[bstg])
                if si % 2 == 0:
                    S.op("dve", lambda e: e.tensor_copy(out=wv[:, pc * 8:(pc + 1) * 8, :], in_=sv),
                         reads=[bstg], writes=[bwb])
                else:
                    S.op("act", lambda e: e.activation(out=wv[:, pc * 8:(pc + 1) * 8, :], in_=sv, func=AF.Copy),
                         reads=[bstg], writes=[bwb])
            return (wv, bwb)

        if prefetch_only:
            pre[wkey] = load(blocks[0])
            return
        loaded = {}
        if wkey is not None and wkey in pre:
            loaded[blocks[0]] = pre.pop(wkey)
        else:
            loaded[blocks[0]] = load(blocks[0])
        for i, bk in enumerate(blocks):
            if i + 1 < len(blocks):
                loaded[blocks[i + 1]] = load(blocks[i + 1])
            elif before_last is not None:
                before_last()
            c0, sl = bk
            wv, bwb = loaded.pop(bk)
            g = c0 // 512
            banks = [0, 1, 2, 3] if g % 2 == 0 else [4, 5, 6, 7]
            pbs = [self.pb[b_] for b_ in banks]
            S._deps("pe", [bwb, rbuf], pbs)
            for m in range(4):
                ps = self.ps[banks[m]]
                for k in range(SL):
                    first = (sl == 0 and k == 0)
                    last = (sl == nsl - 1 and k == SL - 1)
                    if m == 3 and k == SL - 1:
                        S.op("pe", lambda e: e.matmul(ps[:, :T], wv[:, k, m * P:(m + 1) * P], rhs_fn(sl * SL + k),
                                                      start=first, stop=last), reads=[bwb, rbuf], writes=pbs)
                    else:
                        self.nc.tensor.matmul(ps[:, :T], wv[:, k, m * P:(m + 1) * P], rhs_fn(sl * SL + k),
                                              start=first, stop=last)
            if sl == nsl - 1:
                for m in range(4):
                    evac(c0 // P + m, 0, self.ps[banks[m]][:, :T], self.pb[banks[m]])

    def wstream_bufs(self, ph, nstg=2, small=False):
        n1, n2 = (2048, 2048) if small else (4096, 8192)
        return {"stg": [self.sb(ph, "stg", [P, n1], F32) for _ in range(nstg)],
                "wb": [self.sb(ph, "wb", [P, n2], BF16) for _ in range(2)],
                "si": 0, "bi": 0}

    def mlp_phase(self, tiles, w1, w2, mods, after_latent=None):
        S = self.S
        with ExitStack() as ph:
            sc = self.scratch(ph)
            wst = self.wstream_bufs(ph)
            uT, buT = self.sb(ph, "uT", [P, KC, 512], BF16)
            hT, bhT = self.sb(ph, "hT", [P, FC, 512], BF16)
            yT, byT = self.sb(ph, "yT", [P, KC, 512], F32)
            rr = [self.sb(ph, "rr", [P, 512], F32) for _ in range(2)]
            cnt = {"r": 0}
            ti = 0
            ntl = len(tiles)
            self.wstream(wst, w1, KC, DFF, None, None, None, prefetch_only=True, wkey=("w1", 0))
            for tix, (src, skey, dst, dkey, t0, T, which) in enumerate(tiles):
                self.prenorm(sc, src, skey, t0, T, mods["gsc2"][:, which, :], mods["sh2"][:, which, :],
                             mods["buf"], uT, buT, 0)

                def ev1(m, tag, ps, pb):
                    r, br = rr[cnt["r"] % 2]
                    cnt["r"] += 1
                    S.op("act", lambda e: e.activation(out=r[:, :T], in_=ps, func=AF.Relu),
                         reads=[pb], writes=[br])
                    S.op("pool", lambda e: e.tensor_tensor(out=hT[:, m, :T], in0=r[:, :T], in1=r[:, :T], op=ALU.mult),
                         reads=[br], writes=[bhT])

                self.wstream(wst, w1, KC, DFF, [(lambda k: uT[:, k, :T], buT, T, 0)], ev1, banks=[0, 1, 2, 3],
                             wkey=("w1", tix),
                             before_last=lambda: self.wstream_kslab(wst, w2, FC, D, None, None, None, None,
                                                                    prefetch_only=True, wkey=("w2", tix)))

                def ev2(m, tag, ps, pb):
                    S.op("act", lambda e: e.activation(out=yT[:, m, :T], in_=ps, func=AF.Copy),
                         reads=[pb], writes=[byT])

                nxt = None
                if tix + 1 < ntl:
                    nxt = lambda: self.wstream(wst, w1, KC, DFF, None, None, None, prefetch_only=True,
                                               wkey=("w1", tix + 1))
                self.wstream_kslab(wst, w2, FC, D, (lambda k: hT[:, k, :T]), bhT, T, ev2,
                                   wkey=("w2", tix), before_last=nxt)
                self.postnorm(sc, yT, byT, T, mods["gate2"][:, which, :], mods["buf"], src, skey, dst, dkey, t0)
                ti += 1
                if ti == 2 and after_latent is not None:
                    after_latent()
            self.end_phase()

    def mod_stage(self, cT, w_mod_sh, b_mod_sh, modsh):
        S = self.S
        with ExitStack() as ph:
            ct, bct = self.sb(ph, "ct", [P, KC, 2], F32)
            st_, bst = self.sb(ph, "sT", [P, KC, 2], F32)
            bbs = [self.sb(ph, "bb", [2, MSH], F32) for _ in range(2)]
            mos = [self.sb(ph, "mo", [2, MSH], F32) for _ in range(2)]
            wf = [self.sb(ph, "wf", [P, KC, 512], F32) for _ in range(3)]
            S.dma("sp", bct, lambda e: [e.dma_start(out=ct[:], in_=cT)], writes=[bct])
            S.op("act", lambda e: e.activation(out=st_[:], in_=ct[:], func=AF.Silu), reads=[bct], writes=[bst])
            n = 0
            for i in range(DEPTH):
                bb, bbb = bbs[i % 2]
                mo, bmo = mos[i % 2]
                S.dma("sp", bbb, lambda e: [e.dma_start(out=bb[:], in_=b_mod_sh[i * MSH:(i + 1) * MSH].partition_broadcast(2))],
                      writes=[bbb])
                for j in range(MSH // 512):
                    w, bw = wf[n % 3]
                    src = w_mod_sh[i, :, j * 512:(j + 1) * 512].rearrange("(k p) m -> p k m", p=P)
                    S.dma("sp", bw, lambda e: [e.dma_start(out=w[:], in_=src)], writes=[bw])
                    ps, pb = self.ps[n % 2], self.pb[n % 2]
                    S._deps("pe", [bst, bw], [pb])
                    for k in range(KC - 1):
                        self.nc.tensor.matmul(ps[0:2, :], st_[:, k, :], w[:, k, :], start=(k == 0), stop=False)
                    S.op("pe", lambda e: e.matmul(ps[0:2, :], st_[:, KC - 1, :], w[:, KC - 1, :], start=False, stop=True),
                         reads=[bst, bw], writes=[pb])
                    c0 = j * 512
                    S.op("dve", lambda e: e.tensor_tensor(out=mo[:, c0:c0 + 512], in0=ps[0:2, :],
                                                          in1=bb[:, c0:c0 + 512], op=ALU.add),
                         reads=[pb, bbb], writes=[bmo])
                    n += 1
                S.dma("sp", bmo, lambda e: [e.dma_start(out=modsh[:, i * MSH:(i + 1) * MSH], in_=mo[:])], reads=[bmo],
                      writes=[self.rb("modsh")])
            self.end_phase()

    def alloc_mods(self):
        st = self.st
        m = {}
        m["fm"], m["bfm"] = self.sb(st, "modfm", [P, 6, KC, 2], F32)
        m["gv"], m["bgv"] = self.sb(st, "gvec", [P, DEPTH, 5, KC], F32)
        m["sel"], m["bsel"] = self.sb(st, "sel", [2, 2], F32)
        for nm in ("gsc1", "sh1", "gate1", "gsc2", "sh2", "gate2"):
            m[nm], _ = self.sb(st, nm, [P, 2, KC], F32)
        m["buf"] = Buf("mods")
        return m

    def load_mod_consts(self, m, gvec, sel):
        S = self.S
        S.dma("sp", m["bgv"], lambda e: [e.dma_start(out=m["gv"][:], in_=gvec)], writes=[m["bgv"]])
        S.dma("sp", m["bsel"], lambda e: [e.dma_start(out=m["sel"][:], in_=sel)], writes=[m["bsel"]])

    def mod_prep(self, m, modall, mkey, li):
        S = self.S
        with ExitStack() as ph:
            mr, bmr = self.sb(ph, "mrow", [2, MODW], F32)
            S.dma("sp", bmr, lambda e: [e.dma_start(out=mr[:, h * MSH:(h + 1) * MSH],
                                                    in_=modall[h, :, li * MSH:(li + 1) * MSH]) for h in range(2)],
                  reads=[self.rb(mkey)], writes=[bmr])
            ps, pb = self.ps[0], self.pb[0]
            S._deps("pe", [bmr, m["bsel"]], [pb])
            for j in range(6):
                for k in range(KC):
                    c = j * 32 + k * 2
                    ins_fn = lambda e: e.matmul(ps[:, c:c + 2], mr[0:2, j * D + k * P: j * D + (k + 1) * P],
                                                m["sel"][0:2, 0:2], start=True, stop=True)
                    if j == 5 and k == KC - 1:
                        S.op("pe", ins_fn, reads=[bmr, m["bsel"]], writes=[pb])
                    else:
                        ins_fn(self.nc.tensor)
            fm = m["fm"]
            S.op("dve", lambda e: e.tensor_copy(out=fm[:].rearrange("p a k w -> p (a k w)"), in_=ps[:, 0:192]),
                 reads=[pb], writes=[m["bfm"]])
            gv = m["gv"]
            rd = [m["bfm"], m["bgv"]]
            wr = [m["buf"]]
            for w in range(2):
                S.op("dve", lambda e: e.scalar_tensor_tensor(out=m["gsc1"][:, w, :], in0=fm[:, 1, :, w], scalar=1.0,
                                                             in1=gv[:, li, 0, :], op0=ALU.add, op1=ALU.mult),
                     reads=rd, writes=wr)
                S.op("dve", lambda e: e.tensor_copy(out=m["sh1"][:, w, :], in_=fm[:, 0, :, w]), reads=rd, writes=wr)
                S.op("dve", lambda e: e.tensor_tensor(out=m["gate1"][:, w, :], in0=fm[:, 2, :, w],
                                                      in1=gv[:, li, 1, :], op=ALU.mult), reads=rd, writes=wr)
                S.op("dve", lambda e: e.scalar_tensor_tensor(out=m["gsc2"][:, w, :], in0=fm[:, 4, :, w], scalar=1.0,
                                                             in1=gv[:, li, 2, :], op0=ALU.add, op1=ALU.mult),
                     reads=rd, writes=wr)
                S.op("dve", lambda e: e.tensor_copy(out=m["sh2"][:, w, :], in_=fm[:, 3, :, w]), reads=rd, writes=wr)
                S.op("dve", lambda e: e.tensor_tensor(out=m["gate2"][:, w, :], in0=fm[:, 5, :, w],
                                                      in1=gv[:, li, 3, :], op=ALU.mult), reads=rd, writes=wr)
            self.end_phase()


    def proj_post(self, opT, bopT, W, tiles, gate_name, mods):
        S = self.S
        with ExitStack() as ph:
            sc = self.scratch(ph)
            wst = self.wstream_bufs(ph)
            yT, byT = self.sb(ph, "yT", [P, KC, 512], F32)
            for (col0, T, which, src, skey, dst, dkey, t0) in tiles:
                def ev(m, tag, ps, pb):
                    S.op("act", lambda e: e.activation(out=yT[:, m, :T], in_=ps, func=AF.Copy),
                         reads=[pb], writes=[byT])
                self.wstream(wst, W, KC, D, [(lambda k: opT[:, k, col0:col0 + T], bopT, T, 0)], ev,
                             banks=[0, 1, 2, 3])
                self.postnorm(sc, yT, byT, T, mods[gate_name][:, which, :], mods["buf"], src, skey, dst, dkey, t0)
            self.end_phase()

    def rope_evac(self, rp, ps, pb, T, cos_ap, sin_ap, btab, out_ap, bout):
        S = self.S
        i = rp["i"]
        rp["i"] += 1
        qf, bqf = rp["qf"][i % 2]
        t1, bt1 = rp["t1"][i % 2]
        t2, bt2 = rp["t2"][i % 2]
        bank = rp["banks"][i % len(rp["banks"])]
        ps2, pb2 = self.ps[bank], self.pb[bank]
        S.op("act", lambda e: e.activation(out=qf[:, :T], in_=ps, func=AF.Copy), reads=[pb], writes=[bqf])
        S.op("pe", lambda e: e.matmul(ps2[:, :T], rp["perm"][:], qf[:, :T], start=True, stop=True),
             reads=[bqf, rp["bperm"]], writes=[pb2])
        S.op("pool", lambda e: e.tensor_tensor(out=t1[:, :T], in0=qf[:, :T], in1=cos_ap, op=ALU.mult),
             reads=[bqf, btab], writes=[bt1])
        S.op("dve", lambda e: e.tensor_tensor(out=t2[:, :T], in0=ps2[:, :T], in1=sin_ap, op=ALU.mult),
             reads=[pb2, btab], writes=[bt2])
        S.op("dve", lambda e: e.tensor_tensor(out=out_ap, in0=t1[:, :T], in1=t2[:, :T], op=ALU.add),
             reads=[bt1, bt2], writes=[bout])

    def attn_layer(self, li, ai, io, w_qkv, w_o, lamv, gsub, tabs, scr, mods, with_ctx):
        S = self.S
        nc = self.nc
        lambda_init = 0.8 - 0.6 * math.exp(-0.3 * li)
        NQ = LH + (NCTX if with_ctx else 0)
        kT_s, v_s, qT_s = scr["kT"], scr["v"], scr["qT"]
        with ExitStack() as lay:
            cst, bcst = self.sb(lay, "acst", [P, 8], F32)
            with ExitStack() as ph:
                lv, blv = self.sb(ph, "lv", [P, 4, 64], F32)
                gs_, bgs = self.sb(ph, "gsl", [P, 1], F32)
                pr, bpr = self.sb(ph, "pr", [P, 2, 64], F32)
                S.dma("sp", blv, lambda e: [e.dma_start(out=lv[:, j, :], in_=lamv[j].partition_broadcast(P))
                                            for j in range(4)], writes=[blv])
                S.dma("sp", bgs, lambda e: [e.dma_start(out=gs_[:], in_=gsub.rearrange("(p o) -> p o", o=1))],
                      writes=[bgs])
                S.op("dve", lambda e: e.tensor_tensor(out=pr[:, 0, :], in0=lv[:, 0, :], in1=lv[:, 1, :], op=ALU.mult),
                     reads=[blv], writes=[bpr])
                S.op("dve", lambda e: e.tensor_tensor(out=pr[:, 1, :], in0=lv[:, 2, :], in1=lv[:, 3, :], op=ALU.mult),
                     reads=[blv], writes=[bpr])
                S.op("dve", lambda e: e.reduce_sum(out=cst[:, 0:1], in_=pr[:, 0, :], axis=mybir.AxisListType.X),
                     reads=[bpr], writes=[bcst])
                S.op("dve", lambda e: e.reduce_sum(out=cst[:, 1:2], in_=pr[:, 1, :], axis=mybir.AxisListType.X),
                     reads=[bpr], writes=[bcst])
                S.op("act", lambda e: e.activation(out=cst[:, 2:4], in_=cst[:, 0:2], func=AF.Exp),
                     reads=[bcst], writes=[bcst])
                S.op("dve", lambda e: e.tensor_tensor(out=cst[:, 4:5], in0=cst[:, 3:4], in1=cst[:, 2:3], op=ALU.subtract),
                     reads=[bcst], writes=[bcst])
                S.op("dve", lambda e: e.tensor_scalar(out=cst[:, 5:6], in0=cst[:, 4:5], scalar1=-float(lambda_init),
                                                      scalar2=None, op0=ALU.add), reads=[bcst], writes=[bcst])
                S.op("dve", lambda e: e.tensor_scalar(out=cst[:, 6:7], in0=gs_[:, 0:1], scalar1=float(1.0 - lambda_init),
                                                      scalar2=None, op0=ALU.mult), reads=[bgs, bcst], writes=[bcst])
                self.end_phase()
            neglam = cst[:, 5:6]
            gsl = cst[:, 6:7]

            with ExitStack() as ph:
                sc = self.scratch(ph)
                wst = self.wstream_bufs(ph)
                uT, buT = self.sb(ph, "uTall", [P, KC, NKEY], BF16)
                rp = self.rope_bufs(ph, tabs)
                ck, bck = self.sb(ph, "ropek", [P, 2, L], F32)
                S.dma("sp", bck, lambda e: [e.dma_start(out=ck[:, 0, :], in_=tabs["ropek_cos"]),
                                            e.dma_start(out=ck[:, 1, :], in_=tabs["ropek_sin"])], writes=[bck])
                kh = [self.sb(ph, "kh", [P, NKEY], BF16) for _ in range(2)]
                vo = [self.sb(ph, "vo", [P, 512], BF16) for _ in range(2)]
                ktiles = []
                for t in range(4):
                    self.prenorm(sc, io["x_full"][t // 2], ("xfull", io["kfull"], t // 2), (t % 2) * 512, 512,
                                 mods["gsc1"][:, 0, :], mods["sh1"][:, 0, :], mods["buf"], uT, buT, t * 512)
                    ktiles.append((lambda k, t=t: uT[:, k, t * 512:(t + 1) * 512], buT, 512, t))
                self.prenorm(sc, io["xc"], io["kxc"], 0, NCTX, mods["gsc1"][:, 1, :], mods["sh1"][:, 1, :],
                             mods["buf"], uT, buT, L)
                ktiles.append((lambda k: uT[:, k, L:L + NCTX], buT, NCTX, 4))

                def evK(m, tag, ps, pb):
                    kb_, bkb = kh[m % 2]
                    T = 512 if tag < 4 else NCTX
                    c0 = tag * 512
                    if tag < 4:
                        self.rope_evac(rp, ps, pb, T, ck[:, 0, c0:c0 + T], ck[:, 1, c0:c0 + T], bck,
                                       kb_[:, c0:c0 + T], bkb)
                    else:
                        S.op("act", lambda e: e.activation(out=kb_[:, c0:c0 + T], in_=ps, func=AF.Copy),
                             reads=[pb], writes=[bkb])
                        S.dma("sp", bkb, lambda e: [e.dma_start(out=kT_s[m], in_=kb_[:])], reads=[bkb],
                              writes=[self.rb("kT", m)])

                self.wstream(wst, w_qkv[:, D:2 * D], KC, D, ktiles, evK, banks=[0, 1, 2, 3])

                vtiles = [(lambda k, kc=kc: uT[:, k, kc * P:(kc + 1) * P], buT, P, kc) for kc in range(NKEY // P)]

                def evV(cblk, tag, ps, pb):
                    v_, bv = vo[tag % 2]
                    eng = ("act", "dve")[tag % 2]
                    if eng == "act":
                        S.op("act", lambda e: e.activation(out=v_[:], in_=ps, func=AF.Copy), reads=[pb], writes=[bv])
                    else:
                        S.op("dve", lambda e: e.tensor_copy(out=v_[:], in_=ps), reads=[pb], writes=[bv])
                    S.dma("sp", bv, lambda e: [e.dma_start(out=v_s[tag * P:(tag + 1) * P, cblk * 512:(cblk + 1) * 512],
                                                           in_=v_[:])], reads=[bv], writes=[self.rb("v", cblk, tag)])

                self.wstream(wst, w_qkv[:, 2 * D:3 * D], KC, D, vtiles, evV, banks=[4, 5, 6], act_stationary=True)
                self.end_phase()

            with ExitStack() as ph:
                sc = self.scratch(ph)
                wst = self.wstream_bufs(ph)
                uT, buT = self.sb(ph, "uTq", [P, KC, NQ], BF16)
                rp = self.rope_bufs(ph, tabs)
                cq, bcq = self.sb(ph, "ropeq", [P, 2, LH], F32)
                S.dma("sp", bcq, lambda e: [e.dma_start(out=cq[:, 0, :], in_=tabs["ropeq_cos"]),
                                            e.dma_start(out=cq[:, 1, :], in_=tabs["ropeq_sin"])], writes=[bcq])
                qh = [self.sb(ph, "qhb", [P, NQ], BF16) for _ in range(2)]
                qtiles = []
                for t in range(2):
                    self.prenorm(sc, io["x_own"], io["kown"], t * 512, 512,
                                 mods["gsc1"][:, 0, :], mods["sh1"][:, 0, :], mods["buf"], uT, buT, t * 512)
                    qtiles.append((lambda k, t=t: uT[:, k, t * 512:(t + 1) * 512], buT, 512, t))
                if with_ctx:
                    self.prenorm(sc, io["xc"], io["kxc"], 0, NCTX, mods["gsc1"][:, 1, :], mods["sh1"][:, 1, :],
                                 mods["buf"], uT, buT, LH)
                    qtiles.append((lambda k: uT[:, k, LH:LH + NCTX], buT, NCTX, 2))
                nqt = len(qtiles)

                def evQ(m, tag, ps, pb):
                    qb_, bqb = qh[m % 2]
                    T = 512 if tag < 2 else NCTX
                    c0 = tag * 512
                    if tag < 2:
                        self.rope_evac(rp, ps, pb, T, cq[:, 0, c0:c0 + T], cq[:, 1, c0:c0 + T], bcq,
                                       qb_[:, c0:c0 + T], bqb)
                    else:
                        S.op("act", lambda e: e.activation(out=qb_[:, c0:c0 + T], in_=ps, func=AF.Copy),
                             reads=[pb], writes=[bqb])
                    if tag == nqt - 1:
                        S.dma("sp", bqb, lambda e: [e.dma_start(out=qT_s[m, :, 0:NQ], in_=qb_[:])], reads=[bqb],
                              writes=[self.rb("qT", m)])

                self.wstream(wst, w_qkv[:, 0:D], KC, D, qtiles, evQ, banks=[0, 1, 2, 3])
                self.end_phase()

            aT, baT = self.sb(lay, "attnT", [P, KC, NQ], BF16)
            with ExitStack() as ph:
                sc = self.scratch(ph, nx=1)
                khb = [self.sb(ph, "khc", [P, NKEY], BF16) for _ in range(2)]
                vhb = [self.sb(ph, "vhc", [P, NKEY // P, P], BF16) for _ in range(2)]
                qhb = [self.sb(ph, "qhc", [P, NQ], BF16) for _ in range(2)]
                eb = [self.sb(ph, "eb", [P, 2, 512], BF16) for _ in range(4)]
                es = [self.sb(ph, "esum", [P, 2, 512], F32) for _ in range(2)]
                ones32, bones32 = self.sb(ph, "ones32", [P, P], F32)
                S.op("pool", lambda e: e.memset(ones32[:], 1.0), writes=[bones32])
                fr = [self.sb(ph, "fr", [P, 512], F32) for _ in range(4)]
                zsb = [self.sb(ph, "zs", [P, 2, 512], F32) for _ in range(2)]
                ei = 0
                si = 0
                qi = 0
                psall = self.psall
                for h in range(NH):
                    k_, bk = khb[h % 2]
                    v_, bv = vhb[h % 2]
                    q_, bq = qhb[h % 2]
                    S.dma("sp", bk, lambda e: [e.dma_start(out=k_[:], in_=kT_s[h])], reads=[self.rb("kT", h)],
                          writes=[bk])
                    S.dma("sp", bv, lambda e: [e.dma_start(out=v_[:], in_=v_s[:, h * P:(h + 1) * P].rearrange(
                        "(c p) e -> p c e", p=P))],
                          reads=[self.rb("v", h // 4, kc) for kc in range(NKEY // P)], writes=[bv])
                    S.dma("sp", bq, lambda e: [e.dma_start(out=q_[:], in_=qT_s[h, :, 0:NQ])], reads=[self.rb("qT", h)],
                          writes=[bq])
                    qts = [(0, 512, 0, NKEY // P), (512, 512, 0, NKEY // P)]
                    if with_ctx:
                        qts.append((LH, NCTX, L // P, NKEY // P))
                    for (q0, T, kc0, kc1) in qts:
                        esum, besum = es[qi % 2]
                        ab = 4 + 2 * (qi % 2)
                        qi += 1
                        its = list(range(kc0, kc1))
                        slots = {}

                        def emit_qk(i):
                            nonlocal si, ei
                            kc = its[i]
                            p2 = 2 * (si % 2)
                            si += 1
                            e_, be = eb[ei % 4]
                            ei += 1

                            def qk(e):
                                e.matmul(psall[:, p2, :T], k_[0:64, kc * P:(kc + 1) * P], q_[0:64, q0:q0 + T],
                                         start=True, stop=True)
                                return e.matmul(psall[:, p2 + 1, :T], k_[64:128, kc * P:(kc + 1) * P],
                                                q_[64:128, q0:q0 + T], start=True, stop=True)
                            S.op("pe", qk, reads=[bk, bq], writes=[self.pb[p2], self.pb[p2 + 1]])
                            S.op("act", lambda e: e.activation(out=e_[:, :, :T], in_=psall[:, p2:p2 + 2, :T], func=AF.Exp,
                                                               scale=0.125),
                                 reads=[self.pb[p2], self.pb[p2 + 1]], writes=[be])
                            slots[i] = (e_, be)

                        def emit_pv(i):
                            kc = its[i]
                            e_, be = slots.pop(i)

                            def pv(e):
                                e.matmul(self.ps[ab][:, :T], v_[:, kc, :], e_[:, 0, :T],
                                         start=(kc == kc0), stop=(kc == kc1 - 1))
                                return e.matmul(self.ps[ab + 1][:, :T], v_[:, kc, :], e_[:, 1, :T],
                                                start=(kc == kc0), stop=(kc == kc1 - 1))
                            S.op("pe", pv, reads=[be, bv], writes=[self.pb[ab], self.pb[ab + 1]])
                            if kc == kc0:
                                S.op("dve", lambda e: e.tensor_copy(out=esum[:, :, :T], in_=e_[:, :, :T]),
                                     reads=[be], writes=[besum])
                            else:
                                S.op("dve", lambda e: e.tensor_tensor(out=esum[:, :, :T], in0=esum[:, :, :T],
                                                                      in1=e_[:, :, :T], op=ALU.add),
                                     reads=[be, besum], writes=[besum])

                        LA = 1
                        for i in range(min(LA, len(its))):
                            emit_qk(i)
                        for i in range(len(its)):
                            if i + LA < len(its):
                                emit_qk(i + LA)
                            emit_pv(i)
                        zb = 2 * (si % 2)
                        si += 1

                        def zz(e):
                            e.matmul(psall[:, zb, :T], ones32[:], esum[:, 0, :T], start=True, stop=True)
                            return e.matmul(psall[:, zb + 1, :T], ones32[:], esum[:, 1, :T], start=True, stop=True)
                        S.op("pe", zz, reads=[besum, bones32], writes=[self.pb[zb], self.pb[zb + 1]])
                        zs, bzs = zsb[qi % 2]
                        o0, bo0 = fr[1]
                        t1, bt1 = fr[2]
                        oo, boo = fr[3]
                        S.op("act", lambda e: e.activation(out=zs[:, :, :T], in_=psall[:, zb:zb + 2, :T], func=AF.Ln),
                             reads=[self.pb[zb], self.pb[zb + 1]], writes=[bzs])
                        S.op("act", lambda e: e.activation(out=zs[:, :, :T], in_=zs[:, :, :T], func=AF.Exp, scale=-1.0),
                             reads=[bzs], writes=[bzs])
                        S.op("dve", lambda e: e.tensor_tensor(out=o0[:, :T], in0=self.ps[ab][:, :T], in1=zs[:, 0, :T], op=ALU.mult),
                             reads=[self.pb[ab], bzs], writes=[bo0])
                        S.op("dve", lambda e: e.tensor_tensor(out=t1[:, :T], in0=self.ps[ab + 1][:, :T], in1=zs[:, 1, :T], op=ALU.mult),
                             reads=[self.pb[ab + 1], bzs], writes=[bt1])
                        S.op("dve", lambda e: e.scalar_tensor_tensor(out=oo[:, :T], in0=t1[:, :T], scalar=neglam,
                                                                     in1=o0[:, :T], op0=ALU.mult, op1=ALU.add),
                             reads=[bt1, bo0, bcst], writes=[boo])
                        rstd, brstd = self.stats(sc, [(oo[:, :T], boo)], T, P, SUBLN_EPS, zb, 0)
                        S.op("dve", lambda e: e.scalar_tensor_tensor(out=aT[:, h, q0:q0 + T], in0=oo[:, :T], scalar=gsl,
                                                                     in1=rstd[:, :T], op0=ALU.mult, op1=ALU.mult),
                             reads=[boo, brstd, bcst], writes=[baT])
                self.end_phase()

            tiles = [(0, 512, 0, io["x_own"], io["kown"], io["dst_own"], io["kdst_own"], 0),
                     (512, 512, 0, io["x_own"], io["kown"], io["dst_own"], io["kdst_own"], 512)]
            if with_ctx:
                tiles.append((LH, NCTX, 1, io["xc"], io["kxc"], io["dst_c"], io["kdst_c"], 0))
            self.proj_post(aT, baT, w_o, tiles, "gate1", mods)

    def rope_bufs(self, ph, tabs):
        rp = {"i": 0, "banks": [4, 5]}
        rp["qf"] = [self.sb(ph, "qf", [P, 512], F32) for _ in range(2)]
        rp["t1"] = [self.sb(ph, "t1", [P, 512], F32) for _ in range(2)]
        rp["t2"] = [self.sb(ph, "t2", [P, 512], F32) for _ in range(2)]
        rp["perm"], rp["bperm"] = self.sb(ph, "perm", [P, P], F32)
        self.S.dma("sp", rp["bperm"], lambda e: [e.dma_start(out=rp["perm"][:], in_=tabs["perm"])],
                   writes=[rp["bperm"]])
        return rp

    def fourier_layer(self, io, w_f, tabs, scr, mods):
        S = self.S
        fT_s = scr["fT"]
        NT = LH + NCTX
        NLC = NKEY // P
        with ExitStack() as ph:
            sc = self.scratch(ph)
            uT, buT = self.sb(ph, "uTall", [P, KC, NKEY], BF16)
            cc, bcc = self.sb(ph, "dftc", [P, 2, 4, 512], BF16)
            lt_, blt = self.sb(ph, "dftl", [P, 2, KC, 512], BF16)
            c256, bc256 = self.sb(ph, "dft256", [P, 2, 2, NCTX], BF16)
            AB, bAB = self.sb(ph, "AB", [P, 2, NLC, 512], BF16)
            fo = [self.sb(ph, "fo", [P, 512], BF16) for _ in range(2)]
            S.dma("sp", bcc, lambda e: [e.dma_start(out=cc[:, t, :, :], in_=tabs["dftc"][t].rearrange("(j p) m -> p j m", p=P))
                                        for t in range(2)], writes=[bcc])
            S.dma("sp", bc256, lambda e: [e.dma_start(out=c256[:, t, :, :], in_=tabs["dft256"][t].rearrange("(j p) m -> p j m", p=P))
                                          for t in range(2)], writes=[bc256])
            for t in range(4):
                self.prenorm(sc, io["x_full"][t // 2], ("xfull", io["kfull"], t // 2), (t % 2) * 512, 512,
                             mods["gsc1"][:, 0, :], mods["sh1"][:, 0, :], mods["buf"], uT, buT, t * 512)
            self.prenorm(sc, io["xc"], io["kxc"], 0, NCTX, mods["gsc1"][:, 1, :], mods["sh1"][:, 1, :],
                         mods["buf"], uT, buT, L)
            bi = 0
            fi = 0
            for g in range(4):
                for lc in range(NLC):
                    for t in range(2):
                        bk = bi % 4
                        bi += 1
                        ps, pb = self.ps[bk], self.pb[bk]
                        S._deps("pe", [buT, bcc], [pb])
                        for j in range(3):
                            self.nc.tensor.matmul(ps[:, :512], uT[:, g * 4 + j, lc * P:(lc + 1) * P], cc[:, t, j, :],
                                                  start=(j == 0), stop=False)
                        S.op("pe", lambda e: e.matmul(ps[:, :512], uT[:, g * 4 + 3, lc * P:(lc + 1) * P], cc[:, t, 3, :],
                                                      start=False, stop=True), reads=[buT, bcc], writes=[pb])
                        if (lc + t) % 2 == 0:
                            S.op("act", lambda e: e.activation(out=AB[:, t, lc, :], in_=ps[:, :512], func=AF.Copy),
                                 reads=[pb], writes=[bAB])
                        else:
                            S.op("dve", lambda e: e.tensor_copy(out=AB[:, t, lc, :], in_=ps[:, :512]),
                                 reads=[pb], writes=[bAB])
                for lt in range(2):
                    S.dma("sp", blt, lambda e: [e.dma_start(out=lt_[:, t, :, :],
                                                            in_=tabs["dftl"][t, :, lt * 512:(lt + 1) * 512].rearrange(
                                                                "(c p) m -> p c m", p=P)) for t in range(2)],
                          writes=[blt])
                    for mc in range(4):
                        bk = 4 + (fi % 3)
                        ps, pb = self.ps[bk], self.pb[bk]
                        S._deps("pe", [bAB, blt], [pb])
                        n = 0
                        for t in range(2):
                            for lc in range(KC):
                                n += 1
                                if n < 2 * KC:
                                    self.nc.tensor.matmul(ps[:, :512], AB[:, t, lc, mc * P:(mc + 1) * P], lt_[:, t, lc, :],
                                                          start=(n == 1), stop=False)
                                else:
                                    S.op("pe", lambda e: e.matmul(ps[:, :512], AB[:, t, lc, mc * P:(mc + 1) * P],
                                                                  lt_[:, t, lc, :], start=False, stop=True),
                                         reads=[bAB, blt], writes=[pb])
                        f_, bf = fo[fi % 2]
                        fi += 1
                        S.op("act", lambda e: e.activation(out=f_[:, :512], in_=ps[:, :512], func=AF.Copy, scale=1.0 / 1024.0),
                             reads=[pb], writes=[bf])
                        ch = g * 4 + mc
                        S.dma("sp", bf, lambda e: [e.dma_start(out=fT_s[ch * P:(ch + 1) * P, lt * 512:(lt + 1) * 512],
                                                               in_=f_[:, :512])], reads=[bf], writes=[self.rb("fT", ch, lt)])
                for mc in range(4):
                    bk = 4 + (fi % 3)
                    ps, pb = self.ps[bk], self.pb[bk]
                    S._deps("pe", [bAB, bc256], [pb])
                    n = 0
                    for t in range(2):
                        for lc in range(2):
                            n += 1
                            if n < 4:
                                self.nc.tensor.matmul(ps[:, :NCTX], AB[:, t, KC + lc, mc * P:(mc + 1) * P], c256[:, t, lc, :],
                                                      start=(n == 1), stop=False)
                            else:
                                S.op("pe", lambda e: e.matmul(ps[:, :NCTX], AB[:, t, KC + lc, mc * P:(mc + 1) * P],
                                                              c256[:, t, lc, :], start=False, stop=True),
                                     reads=[bAB, bc256], writes=[pb])
                    f_, bf = fo[fi % 2]
                    fi += 1
                    S.op("act", lambda e: e.activation(out=f_[:, :NCTX], in_=ps[:, :NCTX], func=AF.Copy,
                                                       scale=float(1.0 / math.sqrt(NCTX * 512.0))),
                         reads=[pb], writes=[bf])
                    ch = g * 4 + mc
                    S.dma("sp", bf, lambda e: [e.dma_start(out=fT_s[ch * P:(ch + 1) * P, LH:LH + NCTX], in_=f_[:, :NCTX])],
                          reads=[bf], writes=[self.rb("fT", ch, 2)])
            self.end_phase()
        with ExitStack() as lay:
            fT, bfT = self.sb(lay, "fT", [P, KC, NT], BF16)
            S.dma("sp", bfT, lambda e: [e.dma_start(out=fT[:], in_=fT_s.rearrange("(k p) t -> p k t", p=P))],
                  reads=[self.rb("fT", ch, x) for ch in range(KC) for x in range(3)], writes=[bfT])
            tiles = [(0, 512, 0, io["x_own"], io["kown"], io["dst_own"], io["kdst_own"], 0),
                     (512, 512, 0, io["x_own"], io["kown"], io["dst_own"], io["kdst_own"], 512),
                     (LH, NCTX, 1, io["xc"], io["kxc"], io["dst_c"], io["kdst_c"], 0)]
            self.proj_post(fT, bfT, w_f, tiles, "gate1", mods)

    def stats_stream(self, sc, loads, T, dst, bdst, bank=7):
        S = self.S
        ps, pb = self.ps[bank], self.pb[bank]
        nx = len(sc["xk"])
        for k in range(KC):
            xk, bxk = sc["xk"][sc["i"] % nx]
            sc["i"] += 1
            pairs, rds = loads(k, xk)
            S.dma("sp", bxk, lambda e: [e.dma_start(out=o, in_=i_) for (o, i_) in pairs], reads=rds, writes=[bxk])
            sq, bsq = sc["sq"][k % 2]
            S.op("act", lambda e: e.activation(out=sq[:, :T], in_=xk[:, :T], func=AF.Square), reads=[bxk], writes=[bsq])
            S.op("pe", lambda e: e.matmul(ps[:, :T], self.ones[:], sq[:, :T], start=(k == 0), stop=(k == KC - 1)),
                 reads=[bsq, self.b_ones], writes=[pb])
        rt, brt = sc["rt"]
        S.op("act", lambda e: e.activation(out=rt[:, :T], in_=ps[:, :T], func=AF.Ln, bias=self.epsb[:, 0:1], scale=1.0 / D),
             reads=[pb, self.b_eps], writes=[brt])
        S.op("act", lambda e: e.activation(out=dst, in_=rt[:, :T], func=AF.Exp, scale=-0.5), reads=[brt], writes=[bdst])

    def pool_layer(self, li, io, w_pool, tabs, scr, mods):
        S = self.S
        NT = LH + NCTX
        LP = LH + 16
        CP = NCTX + 16
        xf = io["x_full"]
        with ExitStack() as lay:
            mT, bmT = self.sb(lay, "mT", [P, KC, NT], BF16)
            with ExitStack() as ph:
                sc = self.scratch(ph)
                rs, brs = self.sb(ph, "rsall", [P, LH + 16 + NCTX], F32)
                xo = [self.sb(ph, "xo", [P, LH], F32) for _ in range(2)]
                xh = [self.sb(ph, "xh", [P, 16], F32) for _ in range(2)]
                xc_ = [self.sb(ph, "xcc", [P, NCTX], F32) for _ in range(2)]
                th = [self.sb(ph, "th", [P, 16], F32) for _ in range(2)]
                up = [self.sb(ph, "up", [P, LP], F32) for _ in range(2)]
                uc = [self.sb(ph, "uc", [P, CP], F32) for _ in range(2)]
                pa = [self.sb(ph, "pa", [P, LP], F32) for _ in range(2)]
                inv, binv = self.sb(ph, "inv", [P, 4, LH], F32)
                invc, binvc = self.sb(ph, "invc", [P, 4, NCTX], F32)
                msk, bmsk = self.sb(ph, "msk", [P, 2], F32)
                S.dma("sp", binv, lambda e: [e.dma_start(out=inv[:], in_=tabs["pinv"])], writes=[binv])
                S.dma("sp", binvc, lambda e: [e.dma_start(out=invc[:], in_=tabs["pinvc"])], writes=[binvc])
                S.dma("sp", bmsk, lambda e: [e.dma_start(out=msk[:], in_=tabs["pmsk"])], writes=[bmsk])
                for j in range(2):
                    S.op("pool", lambda e: e.memset(uc[j][0][:], 0.0), writes=[uc[j][1]])
                for t in range(2):
                    self.stats_stream(sc, lambda k, xk, t=t: ([(xk[:, :512], io["x_own"][k * P:(k + 1) * P, t * 512:(t + 1) * 512])],
                                                              [self.rb(io["kown"], t, k)]), 512, rs[:, t * 512:(t + 1) * 512], brs)
                self.stats_stream(sc, lambda k, xk: ([(xk[:, 0:8], xf[0](k, LH - 8, LH)),
                                                      (xk[:, 8:16], xf[1](k, 0, 8))],
                                                     [self.rb(("xfull", io["kfull"], 0), 1, k), self.rb(("xfull", io["kfull"], 1), 0, k)]),
                                  16, rs[:, LH:LH + 16], brs)
                self.stats_stream(sc, lambda k, xk: ([(xk[:, :NCTX], io["xc"][k * P:(k + 1) * P, 0:NCTX])],
                                                     [self.rb(io["kxc"], 0, k)]), NCTX, rs[:, LH + 16:LH + 16 + NCTX], brs)
                gsc, sh = mods["gsc1"], mods["sh1"]
                bm = mods["buf"]
                for k in range(KC):
                    g = k // 4
                    w = 2 << g
                    x_, bx = xo[k % 2]
                    h_, bh = xh[k % 2]
                    c_, bc = xc_[k % 2]
                    t_, bt = th[k % 2]
                    u_, bu = up[k % 2]
                    uc_, buc = uc[k % 2]
                    S.dma("sp", bx, lambda e: [e.dma_start(out=x_[:], in_=io["x_own"][k * P:(k + 1) * P, :])],
                          reads=[self.rb(io["kown"], 0, k), self.rb(io["kown"], 1, k)], writes=[bx])
                    S.dma("sp", bh, lambda e: [e.dma_start(out=h_[:, 0:8], in_=xf[0](k, LH - 8, LH)),
                                               e.dma_start(out=h_[:, 8:16], in_=xf[1](k, 0, 8))],
                          reads=[self.rb(("xfull", io["kfull"], 0), 1, k), self.rb(("xfull", io["kfull"], 1), 0, k)], writes=[bh])
                    S.dma("sp", bc, lambda e: [e.dma_start(out=c_[:], in_=io["xc"][k * P:(k + 1) * P, :])],
                          reads=[self.rb(io["kxc"], 0, k)], writes=[bc])
                    S.op("dve", lambda e: e.scalar_tensor_tensor(out=u_[:, 8:8 + LH], in0=x_[:], scalar=gsc[:, 0, k:k + 1],
                                                                 in1=rs[:, 0:LH], op0=ALU.mult, op1=ALU.mult),
                         reads=[bx, brs, bm], writes=[bu])
                    S.op("act", lambda e: e.activation(out=u_[:, 8:8 + LH], in_=u_[:, 8:8 + LH], func=AF.Identity,
                                                       bias=sh[:, 0, k:k + 1], scale=1.0), reads=[bu, bm], writes=[bu])
                    S.op("dve", lambda e: e.scalar_tensor_tensor(out=t_[:], in0=h_[:], scalar=gsc[:, 0, k:k + 1],
                                                                 in1=rs[:, LH:LH + 16], op0=ALU.mult, op1=ALU.mult),
                         reads=[bh, brs, bm], writes=[bt])
                    S.op("act", lambda e: e.activation(out=t_[:], in_=t_[:], func=AF.Identity, bias=sh[:, 0, k:k + 1], scale=1.0),
                         reads=[bt, bm], writes=[bt])
                    S.op("dve", lambda e: e.tensor_scalar(out=u_[:, 0:8], in0=t_[:, 0:8], scalar1=msk[:, 0:1], scalar2=None,
                                                          op0=ALU.mult), reads=[bt, bmsk], writes=[bu])
                    S.op("dve", lambda e: e.tensor_scalar(out=u_[:, 8 + LH:16 + LH], in0=t_[:, 8:16], scalar1=msk[:, 1:2],
                                                          scalar2=None, op0=ALU.mult), reads=[bt, bmsk], writes=[bu])
                    S.op("dve", lambda e: e.scalar_tensor_tensor(out=uc_[:, 8:8 + NCTX], in0=c_[:], scalar=gsc[:, 1, k:k + 1],
                                                                 in1=rs[:, LH + 16:LH + 16 + NCTX], op0=ALU.mult, op1=ALU.mult),
                         reads=[bc, brs, bm], writes=[buc])
                    S.op("act", lambda e: e.activation(out=uc_[:, 8:8 + NCTX], in_=uc_[:, 8:8 + NCTX], func=AF.Identity,
                                                       bias=sh[:, 1, k:k + 1], scale=1.0), reads=[buc, bm], writes=[buc])
                    for (src, bsrc, n_, ln, itab, bit, col0) in ((u_, bu, LH, LP, inv, binv, 0), (uc_, buc, NCTX, CP, invc, binvc, LH)):
                        cur, bcur = src, bsrc
                        clen = ln
                        for s_ in range(g + 1):
                            step = 1 << s_
                            nxt, bnxt = pa[s_ % 2]
                            eng = ("pool", "dve")[s_ % 2]
                            nl = clen - step
                            S.op(eng, lambda e: e.tensor_tensor(out=nxt[:, 0:nl], in0=cur[:, 0:nl], in1=cur[:, step:step + nl],
                                                                op=ALU.add), reads=[bcur], writes=[bnxt])
                            cur, bcur, clen = nxt, bnxt, nl
                        off = 8 - w // 2
                        oth, both = pa[(g + 1) % 2]
                        S.op("dve", lambda e: e.tensor_tensor(out=oth[:, 0:n_], in0=cur[:, off:off + n_], in1=itab[:, g, :],
                                                              op=ALU.mult), reads=[bcur, bit], writes=[both])
                        S.op("pool", lambda e: e.tensor_tensor(out=mT[:, k, col0:col0 + n_], in0=oth[:, 0:n_],
                                                               in1=src[:, 8:8 + n_], op=ALU.subtract),
                             reads=[both, bsrc], writes=[bmT])
                self.end_phase()
            with ExitStack() as ph:
                sc = self.scratch(ph)
                wst = self.wstream_bufs(ph, small=True)
                yT, byT = self.sb(ph, "yTp", [P, KC, NT], F32)
                gv = mods["gv"]
                tl = [(0, 512), (512, 512), (LH, NCTX)]
                for g in range(4):
                    def ev(m, tag, ps, pb):
                        c0, T = tl[tag]
                        ch = g * 4 + m
                        S.op("act", lambda e: e.activation(out=yT[:, ch, c0:c0 + T], in_=ps, func=AF.Copy,
                                                           scale=gv[:, li, 4, ch:ch + 1]),
                             reads=[pb, mods["bgv"]], writes=[byT])
                    tiles = [(lambda k, c0=c0, T=T: mT[:, g * 4 + k, c0:c0 + T], bmT, T, ti) for ti, (c0, T) in enumerate(tl)]
                    self.wstream(wst, w_pool[g], 4, 512, tiles, ev, banks=[0, 1, 2, 3])
                self.postnorm(sc, yT[:, :, 0:512], byT, 512, mods["gate1"][:, 0, :], mods["buf"], io["x_own"], io["kown"],
                              io["dst_own"], io["kdst_own"], 0)
                self.postnorm(sc, yT[:, :, 512:1024], byT, 512, mods["gate1"][:, 0, :], mods["buf"], io["x_own"], io["kown"],
                              io["dst_own"], io["kdst_own"], 512)
                self.postnorm(sc, yT[:, :, LH:NT], byT, NCTX, mods["gate1"][:, 1, :], mods["buf"], io["xc"], io["kxc"],
                              io["dst_c"], io["kdst_c"], 0)
                self.end_phase()

    def layer(self, li, io, Wt, tabs, scr, mods, after_latent=None):
        kind = li % 3
        last = li == DEPTH - 1
        if kind == 0:
            self.attn_layer(li, li // 3, io, Wt["w_qkv"], Wt["w_o"], Wt["lamv"], Wt["gsub"], tabs, scr, mods, not last)
        elif kind == 1:
            self.fourier_layer(io, Wt["w_f"], tabs, scr, mods)
        else:
            self.pool_layer(li, io, Wt["w_pool"], tabs, scr, mods)
        tiles = [(io["dst_own"], io["kdst_own"], io["out_own"], io["kout_own"], 0, 512, 0),
                 (io["dst_own"], io["kdst_own"], io["out_own"], io["kout_own"], 512, 512, 0)]
        if not last:
            tiles.append((io["dst_c"], io["kdst_c"], io["out_c"], io["kout_c"], 0, NCTX, 1))
        self.mlp_phase(tiles, Wt["w1"], Wt["w2"], mods, after_latent)

def host_gvec(inputs):
    gv = np.zeros((P, DEPTH, 5, KC), np.float32)
    for i in range(DEPTH):
        vecs = [inputs["g_mix_pre"][i], inputs["g_mix_post"][i], inputs["g_mlp_pre"][i], inputs["g_mlp_post"][i],
                inputs["pool_scale"][0]]
        for v, vec in enumerate(vecs):
            gv[:, i, v, :] = np.asarray(vec, np.float32).reshape(KC, P).T
    return gv


def _bf16(a):
    return np.asarray(a, np.float32).astype(ml_dtypes.bfloat16)


_TAB_CACHE = {}


def host_tables():
    if _TAB_CACHE:
        return _TAB_CACHE
    T = _TAB_CACHE
    t = np.arange(L)
    row = (t // 64).astype(np.float32)
    col = (t % 64).astype(np.float32)
    inv = (1.0 / (np.float32(10000.0) ** (np.arange(16, dtype=np.float32) / np.float32(16)))).astype(np.float32)
    cos = np.zeros((P, L), np.float32)
    sin = np.zeros((P, L), np.float32)
    perm = np.zeros((P, P), np.float32)
    for p in range(P):
        d = p % 64
        axis = d // 32
        half = (d % 32) // 16
        f = d % 16
        ang = ((row if axis == 0 else col) * inv[f]).astype(np.float32)
        cos[p] = np.cos(ang)
        sin[p] = np.sin(ang) * (-1.0 if half == 0 else 1.0)
        partner = p + 16 if half == 0 else p - 16
        perm[partner, p] = 1.0
    T["rope_cos"], T["rope_sin"], T["perm"] = cos, sin, perm
    c = np.arange(512)
    angc = 2.0 * np.pi * ((c[:, None] * c[None, :]) % 512) / 512.0
    T["dftc"] = _bf16(np.stack([np.cos(angc), np.sin(angc)]))
    l = np.arange(L)
    angl = 2.0 * np.pi * ((l[:, None] * l[None, :]) % L) / float(L)
    T["dftl_full"] = np.stack([np.cos(angl), -np.sin(angl)]).astype(np.float32)
    x = np.arange(NCTX)
    angx = 2.0 * np.pi * ((x[:, None] * x[None, :]) % NCTX) / float(NCTX)
    T["dft256"] = _bf16(np.stack([np.cos(angx), -np.sin(angx)]))

    def invcnt(n, lo_t, num):
        out = np.zeros((4, num), np.float32)
        for g, w in enumerate((2, 4, 8, 16)):
            tt = np.arange(lo_t, lo_t + num)
            lo = np.clip(tt - w // 2, 0, n)
            hi = np.clip(tt - w // 2 + w, 0, n)
            out[g] = 1.0 / (hi - lo).astype(np.float32)
        return out
    T["pinv"] = [np.ascontiguousarray(np.broadcast_to(invcnt(L, hf * LH, LH)[None], (P, 4, LH))) for hf in range(2)]
    T["pinvc"] = np.ascontiguousarray(np.broadcast_to(invcnt(NCTX, 0, NCTX)[None], (P, 4, NCTX)))
    T["dftl"] = [_bf16(T["dftl_full"][:, :, hf * LH:(hf + 1) * LH]) for hf in range(2)]
    return T


def build_fused():
    nc = bass.Bass("TRN2", target_bir_lowering=False)
    di = lambda n, sh, dt=F32: nc.dram_tensor(n, list(sh), dt, kind="ExternalInput").ap()
    do = lambda n, sh, dt=F32: nc.dram_tensor(n, list(sh), dt, kind="ExternalOutput").ap()
    dn = lambda n, sh, dt=F32: nc.dram_tensor(n, list(sh), dt).ap()
    x_full0 = di("x_full", [2, D, LH])
    x_own0 = di("x_own", [D, LH])
    xc0 = di("xc", [D, NCTX])
    cT = di("cT", [P, KC, 2])
    w_mod_h = di("w_mod_h", [DEPTH, D, MSH])
    b_mod_h = di("b_mod_h", [DEPTH * MSH])
    sel = di("sel", [2, 2])
    gvec = di("gvec", [P, DEPTH, 5, KC])
    w1 = di("w_mlp_in", [DEPTH, D, DFF])
    w2 = di("w_mlp_out", [DEPTH, DFF, D])
    w_qkv = di("w_qkv", [2, D, 3 * D])
    w_o = di("w_attn_out", [2, D, D])
    lamv = di("lamv", [2, 4, 64])
    gsub = di("gsub", [2, P])
    w_f = di("w_f", [D, D])
    w_pool = di("w_pool", [4, 512, 512])
    tabs = {"perm": di("perm", [P, P]), "ropek_cos": di("ropek_cos", [P, L]), "ropek_sin": di("ropek_sin", [P, L]),
            "ropeq_cos": di("ropeq_cos", [P, LH]), "ropeq_sin": di("ropeq_sin", [P, LH]),
            "dftc": di("dftc", [2, 512, 512], BF16), "dftl": di("dftl", [2, L, LH], BF16),
            "dft256": di("dft256", [2, NCTX, NCTX], BF16),
            "pinv": di("pinv", [P, 4, LH]), "pinvc": di("pinvc", [P, 4, NCTX]), "pmsk": di("pmsk", [P, 2])}
    out = do("out", [D, LH])
    modh = dn("modh", [2, DEPTH * MSH])
    modall = dn("modall", [2, 2, DEPTH * MSH])
    scr = {"kT": dn("kT_s", [NH, P, NKEY], BF16), "v": dn("v_s", [NKEY, D], BF16),
           "qT": dn("qT_s", [NH, P, LH + NCTX], BF16), "fT": dn("fT_s", [D, LH + NCTX], BF16)}
    NCH = 8
    RPC = D // NCH
    with ExitStack() as st:
        kb = KB(nc, st)
        S = kb.S
        m = kb.alloc_mods()
        kb.load_mod_consts(m, gvec, sel)
        kb.mod_stage(cT, w_mod_h, b_mod_h, modh)
        S.coll([lambda e: e.collective_compute("AllGather", ALU.bypass, replica_groups=PAIRS,
                                               ins=[modh], outs=[modall.rearrange("r a c -> (r a) c")])],
               reads=[kb.rb("modsh")], writes=[kb.rb("modall")])
        x_own, kown = x_own0, "x_own_in"
        xc, kxc = xc0, "xc_in"
        xf = [(lambda k, c0, c1, h=h: x_full0[h, k * P:(k + 1) * P, c0:c1]) for h in range(2)]
        kfull = "in"
        for li in range(DEPTH):
            last = li == DEPTH - 1
            kind = li % 3
            io = {"x_full": xf, "kfull": kfull, "x_own": x_own, "kown": kown, "xc": xc, "kxc": kxc,
                  "dst_own": dn("xmid_own%d" % li, [D, LH]), "kdst_own": "xmid_own%d" % li,
                  "dst_c": dn("xmid_c%d" % li, [D, NCTX]), "kdst_c": "xmid_c%d" % li}
            if last:
                io["out_own"], io["kout_own"] = out, "out"
            else:
                io["out_own"], io["kout_own"] = dn("xo_own%d" % li, [D, LH]), "xo_own%d" % li
                io["out_c"], io["kout_c"] = dn("xo_c%d" % li, [D, NCTX]), "xo_c%d" % li
            Wt = {"w1": w1[li], "w2": w2[li]}
            if kind == 0:
                ai = li // 3
                Wt.update({"w_qkv": w_qkv[ai], "w_o": w_o[ai], "lamv": [lamv[ai, j] for j in range(4)], "gsub": gsub[ai]})
            elif kind == 1:
                Wt["w_f"] = w_f
            else:
                Wt["w_pool"] = [w_pool[g] for g in range(4)]
            kb.mod_prep(m, modall, "modall", li)
            nxt = {}
            if not last:
                xg = dn("xg%d" % li, [NCH, 2, RPC, LH])
                kf = "g%d" % li

                def exchange(io=io, xg=xg, kf=kf):
                    src = io["out_own"]
                    rds = [kb.rb(io["kout_own"], t, k) for t in range(2) for k in range(KC)]
                    wrs = [kb.rb(("xfull", kf, h), t, k) for h in range(2) for t in range(2) for k in range(KC)]
                    S.coll([(lambda e, j=j: e.collective_compute("AllGather", ALU.bypass, replica_groups=PAIRS,
                                                                 ins=[src[j * RPC:(j + 1) * RPC, :]],
                                                                 outs=[xg[j].rearrange("r a c -> (r a) c")]))
                            for j in range(NCH)], reads=rds, writes=wrs)
                kb.layer(li, io, Wt, tabs, scr, m)
                kfull = kf
                if (li + 1) % 3 == 2:
                    hsrc = dn("halo_src%d" % li, [D, 16])
                    hall = dn("halo_all%d" % li, [2, D, 16])
                    src = io["out_own"]
                    bh = Buf("halo%d" % li)
                    rds = [kb.rb(io["kout_own"], t, k) for t in range(2) for k in range(KC)]
                    S.dma("sp", bh, lambda e: [e.dma_start(out=hsrc[:, 0:8], in_=src[:, 0:8]),
                                               e.dma_start(out=hsrc[:, 8:16], in_=src[:, LH - 8:LH])],
                          reads=rds, writes=[kb.rb("halo_src", li)])
                    wrs = [kb.rb(("xfull", kf, h), t, k) for h in range(2) for t in range(2) for k in range(KC)]
                    S.coll([lambda e: e.collective_compute("AllGather", ALU.bypass, replica_groups=PAIRS, ins=[hsrc],
                                                           outs=[hall.rearrange("r d c -> (r d) c")])],
                           reads=[kb.rb("halo_src", li)], writes=wrs)
                    xf = [(lambda k, c0, c1, hall=hall: hall[0, k * P:(k + 1) * P, c0 - (LH - 16):c1 - (LH - 16)]),
                          (lambda k, c0, c1, hall=hall: hall[1, k * P:(k + 1) * P, c0:c1])]
                else:
                    exchange()
                    xf = [(lambda k, c0, c1, h=h, xg=xg: xg[(k * P) // RPC, h, (k * P) % RPC:(k * P) % RPC + P, c0:c1])
                          for h in range(2)]
                x_own, kown = io["out_own"], io["kout_own"]
                xc, kxc = io["out_c"], io["kout_c"]
            else:
                kb.layer(li, io, Wt, tabs, scr, m)
        S.barrier(["sp"])
    return nc


PAIRS = [[0, 1], [2, 3], [4, 5], [6, 7]]
_PROG_CACHE = {}


def kernel(**inputs):
    inputs = {k: np.asarray(v) for k, v in inputs.items()}
    tb = host_tables()
    gv = host_gvec(inputs)
    cores = list(range(NCORES))
    if "fused" not in _PROG_CACHE:
        _PROG_CACHE["fused"] = build_fused()
    x = inputs["x"]
    lamv = np.stack([np.stack([inputs["lambda_q1"][a], inputs["lambda_k1"][a], inputs["lambda_q2"][a],
                               inputs["lambda_k2"][a]]) for a in range(2)]).astype(np.float32)
    shared = {"sel": np.eye(2, dtype=np.float32), "gvec": gv,
              "w_mlp_in": inputs["w_mlp_in"], "w_mlp_out": inputs["w_mlp_out"], "w_qkv": inputs["w_qkv"],
              "w_attn_out": inputs["w_attn_out"], "lamv": lamv, "gsub": np.asarray(inputs["g_subln"], np.float32),
              "w_f": inputs["w_fourier_out"][0], "w_pool": inputs["w_pool"][0],
              "perm": tb["perm"], "ropek_cos": tb["rope_cos"], "ropek_sin": tb["rope_sin"],
              "dftc": tb["dftc"], "dft256": tb["dft256"], "pinvc": tb["pinvc"]}
    wmh = [np.ascontiguousarray(inputs["w_mod"][:, :, h * MSH:(h + 1) * MSH]) for h in range(2)]
    bmh = [np.ascontiguousarray(inputs["b_mod"][:, h * MSH:(h + 1) * MSH]).reshape(-1) for h in range(2)]
    maps = []
    for r in cores:
        b, hf = r // 2, r % 2
        xfull = np.ascontiguousarray(x[b].T.reshape(D, 2, LH).transpose(1, 0, 2))
        c2 = np.stack([inputs["c"][b], inputs["c_ctx"]]).astype(np.float32)
        msk = np.zeros((P, 2), np.float32)
        msk[:, 0] = 1.0 if hf == 1 else 0.0
        msk[:, 1] = 1.0 if hf == 0 else 0.0
        mp = dict(shared)
        mp.update({"x_full": xfull, "x_own": np.ascontiguousarray(xfull[hf]),
                   "xc": np.ascontiguousarray(inputs["ctx"][b].T),
                   "cT": np.ascontiguousarray(c2.T.reshape(KC, P, 2).transpose(1, 0, 2)),
                   "w_mod_h": wmh[hf], "b_mod_h": bmh[hf],
                   "ropeq_cos": np.ascontiguousarray(tb["rope_cos"][:, hf * LH:(hf + 1) * LH]),
                   "ropeq_sin": np.ascontiguousarray(tb["rope_sin"][:, hf * LH:(hf + 1) * LH]),
                   "dftl": tb["dftl"][hf], "pinv": tb["pinv"][hf], "pmsk": msk})
        maps.append(mp)
    res = run_bass_kernel_spmd(_PROG_CACHE["fused"], maps, core_ids=cores)
    out = np.stack([np.concatenate([res.results[2 * b]["out"], res.results[2 * b + 1]["out"]], axis=1).T
                    for b in range(4)])
    return np.ascontiguousarray(out.astype(np.float32))
```

```python
import math
from contextlib import ExitStack

import numpy as np
import ml_dtypes

import concourse.bass as bass
import concourse.mybir as mybir
from concourse.bass_utils import run_bass_kernel_spmd

F32 = mybir.dt.float32
BF16 = mybir.dt.bfloat16
AF = mybir.ActivationFunctionType
ALU = mybir.AluOpType

P = 128
D = 2048
KC = 16
L = 2048
LH = 1024
NCTX = 256
NKEY = L + NCTX
DFF = 8192
FC = DFF // P
NH = 16
DEPTH = 4
NORM_EPS = 1e-6
SUBLN_EPS = 1e-5
NCORES = 8
MODW = 6 * D
MSH = MODW // 2


class Buf:
    __slots__ = ("name", "last_w", "readers", "dsem", "dcnt")

    def __init__(self, name):
        self.name = name
        self.last_w = None
        self.readers = []
        self.dsem = None
        self.dcnt = 0


class Sched:
    ENG = ("pe", "act", "dve", "pool", "sp")

    def __init__(self, nc, stack):
        self.nc = nc
        self.stack = stack
        self.eng = {"pe": nc.tensor, "act": nc.scalar, "dve": nc.vector,
                    "pool": nc.gpsimd, "sp": nc.sync}
        self.esem = {e: stack.enter_context(nc.semaphore("es_" + e)) for e in self.ENG}
        self.ecnt = {e: 0 for e in self.ENG}
        self.known = {e: {} for e in self.ENG}
        self.pool = []
        self.live = []
        self.nsem = 0
        self.ccsem = None
        self.cccnt = 0

    def _need(self, e, ev):
        if ev is None:
            return
        sem, cnt = ev
        k = self.known[e]
        if k.get(id(sem), 0) >= cnt:
            return
        if e == "pe" and sem is self.esem["pe"]:
            return
        self.eng[e].wait_ge(sem, cnt)
        k[id(sem)] = cnt

    def _deps(self, e, reads, writes):
        for b in reads:
            self._need(e, b.last_w)
        for b in writes:
            self._need(e, b.last_w)
            for r in b.readers:
                self._need(e, r)

    @staticmethod
    def _commit(ev, reads, writes):
        for b in reads:
            b.readers.append(ev)
            if len(b.readers) > 64:
                b.readers = b.readers[-48:]
        for b in writes:
            b.last_w = ev
            b.readers = []

    def op(self, e, fn, reads=(), writes=()):
        self._deps(e, reads, writes)
        ins = fn(self.eng[e])
        self.ecnt[e] += 1
        ins.then_inc(self.esem[e], 1)
        ev = (self.esem[e], self.ecnt[e])
        self._commit(ev, reads, writes)
        return ev

    def dma(self, q, owner, fn, reads=(), writes=()):
        if owner.dsem is None:
            if self.pool:
                owner.dsem, owner.dcnt = self.pool.pop()
            else:
                owner.dsem = self.stack.enter_context(self.nc.semaphore("ds%d" % self.nsem))
                owner.dcnt = 0
                self.nsem += 1
            self.live.append(owner)
        self._deps(q, reads, writes)
        lst = fn(self.eng[q])
        for ins in lst:
            ins.then_inc(owner.dsem, 16)
            owner.dcnt += 16
        ev = (owner.dsem, owner.dcnt)
        self._commit(ev, reads, writes)
        return ev

    def coll(self, fns, reads, writes):
        if self.ccsem is None:
            self.ccsem = self.stack.enter_context(self.nc.semaphore("ccsem"))
        self._deps("pool", reads, writes)
        for fn in fns:
            ins = fn(self.eng["pool"])
            self.cccnt += 1
            ins.then_inc(self.ccsem, 1)
        ev = (self.ccsem, self.cccnt)
        self._commit(ev, reads, writes)
        return ev

    def barrier(self, engines=None):
        engines = engines or self.ENG
        for e in engines:
            for f in self.ENG:
                if f != e and self.ecnt[f] > 0:
                    self._need(e, (self.esem[f], self.ecnt[f]))
            for b in self.live:
                if b.dcnt > 0:
                    self._need(e, (b.dsem, b.dcnt))
            if self.cccnt > 0:
                self._need(e, (self.ccsem, self.cccnt))

    def release_dma_sems(self):
        for b in self.live:
            self.pool.append((b.dsem, b.dcnt))
            b.dsem = None
        self.live = []


class KB:
    def __init__(self, nc, st):
        self.nc = nc
        self.st = st
        self.S = Sched(nc, st)
        self.uid = 0
        self.ps = []
        self.pb = []
        self.psall = st.enter_context(nc.psum_tensor("psall", [P, 8, 512], F32))
        for i in range(8):
            self.ps.append(self.psall[:, i, :])
            self.pb.append(Buf("psb%d" % i))
        self.ones, self.b_ones = self.sb(st, "ones", [P, P], BF16)
        self.S.op("pool", lambda e: e.memset(self.ones[:], 1.0), writes=[self.b_ones])
        self.epsb, self.b_eps = self.sb(st, "epsb", [P, 2], F32)
        self.S.op("pool", lambda e: e.memset(self.epsb[:, 0:1], NORM_EPS), writes=[self.b_eps])
        self.S.op("pool", lambda e: e.memset(self.epsb[:, 1:2], SUBLN_EPS), writes=[self.b_eps])
        self.dr = {}
        self.wq = 0

    def sb(self, ph, name, shape, dt):
        self.uid += 1
        nm = "%s_%d" % (name, self.uid)
        t = ph.enter_context(self.nc.sbuf_tensor(nm, list(shape), dt))
        return t, Buf(nm)

    def rb(self, *key):
        b = self.dr.get(key)
        if b is None:
            b = Buf("dr_" + "_".join(str(k) for k in key))
            self.dr[key] = b
        return b

    def end_phase(self):
        self.S.barrier()
        self.S.release_dma_sems()

    def scratch(self, ph, nx=3):
        sc = {}
        sc["sq"] = [self.sb(ph, "sq", [P, 512], BF16) for _ in range(2)]
        sc["rt"] = self.sb(ph, "rt", [P, 512], F32)
        sc["rstd"] = [self.sb(ph, "rstd", [P, 512], F32) for _ in range(2)]
        sc["xk"] = [self.sb(ph, "xk", [P, 512], F32) for _ in range(nx)]
        sc["tmp"] = [self.sb(ph, "tmp", [P, 512], F32) for _ in range(2)]
        sc["ok"] = [self.sb(ph, "ok", [P, 512], F32) for _ in range(2)]
        sc["i"] = 0
        return sc

    def stats(self, sc, chunks, T, n, eps, bank, slot):
        S = self.S
        ps, pb = self.ps[bank], self.pb[bank]
        nk = len(chunks)
        for k, (ap, buf) in enumerate(chunks):
            sq, bsq = sc["sq"][k % 2]
            S.op("act", lambda e: e.activation(out=sq[:, :T], in_=ap, func=AF.Square),
                 reads=[buf], writes=[bsq])
            S.op("pe", lambda e: e.matmul(ps[:, :T], self.ones[:], sq[:, :T],
                                          start=(k == 0), stop=(k == nk - 1)),
                 reads=[bsq, self.b_ones], writes=[pb])
        rt, brt = sc["rt"]
        rstd, brstd = sc["rstd"][slot]
        S.op("act", lambda e: e.activation(out=rt[:, :T], in_=ps[:, :T], func=AF.Ln,
                                           bias=self.epsb[:, 0:1] if eps == NORM_EPS else self.epsb[:, 1:2], scale=1.0 / n),
             reads=[pb, self.b_eps], writes=[brt])
        S.op("act", lambda e: e.activation(out=rstd[:, :T], in_=rt[:, :T], func=AF.Exp, scale=-0.5),
             reads=[brt], writes=[brstd])
        return rstd, brstd

    def prenorm(self, sc, src, skey, t0, T, gsc, sh, bmod, uT, buT, ucol0, bank=7):
        S = self.S
        ps, pb = self.ps[bank], self.pb[bank]
        nx = len(sc["xk"])
        if not callable(src):
            src_ap = src
            src = lambda k, c0, c1: src_ap[k * P:(k + 1) * P, c0:c1]
        for k in range(KC):
            xk, bxk = sc["xk"][sc["i"] % nx]
            sc["i"] += 1
            S.dma("sp", bxk, lambda e: [e.dma_start(out=xk[:, :T], in_=src(k, t0, t0 + T))],
                  reads=[self.rb(skey, t0 // 512, k)], writes=[bxk])
            sq, bsq = sc["sq"][k % 2]
            S.op("act", lambda e: e.activation(out=sq[:, :T], in_=xk[:, :T], func=AF.Square),
                 reads=[bxk], writes=[bsq])
            S.op("pe", lambda e: e.matmul(ps[:, :T], self.ones[:], sq[:, :T],
                                          start=(k == 0), stop=(k == KC - 1)),
                 reads=[bsq, self.b_ones], writes=[pb])
        rt, brt = sc["rt"]
        rstd, brstd = sc["rstd"][0]
        S.op("act", lambda e: e.activation(out=rt[:, :T], in_=ps[:, :T], func=AF.Ln,
                                           bias=self.epsb[:, 0:1], scale=1.0 / D),
             reads=[pb, self.b_eps], writes=[brt])
        S.op("act", lambda e: e.activation(out=rstd[:, :T], in_=rt[:, :T], func=AF.Exp, scale=-0.5),
             reads=[brt], writes=[brstd])
        for k in range(KC):
            xk, bxk = sc["xk"][sc["i"] % nx]
            sc["i"] += 1
            S.dma("sp", bxk, lambda e: [e.dma_start(out=xk[:, :T], in_=src(k, t0, t0 + T))],
                  reads=[self.rb(skey, t0 // 512, k)], writes=[bxk])
            tmp, btmp = sc["tmp"][k % 2]
            S.op("dve", lambda e: e.scalar_tensor_tensor(out=tmp[:, :T], in0=xk[:, :T], scalar=gsc[:, k:k + 1],
                                                         in1=rstd[:, :T], op0=ALU.mult, op1=ALU.mult),
                 reads=[bxk, brstd, bmod], writes=[btmp])
            S.op("act", lambda e: e.activation(out=uT[:, k, ucol0:ucol0 + T], in_=tmp[:, :T], func=AF.Identity,
                                               bias=sh[:, k:k + 1], scale=1.0),
                 reads=[btmp, bmod], writes=[buT])

    def postnorm(self, sc, yT, byT, T, gate, bmod, src, skey, dst, dkey, t0, bank=7, store_q="sp"):
        S = self.S
        chunks = [(yT[:, k, :T], byT) for k in range(KC)]
        rstd, brstd = self.stats(sc, chunks, T, D, NORM_EPS, bank, 1)
        nx = len(sc["xk"])
        base = sc["i"]
        sc["i"] += KC
        LA = max(0, min(2, nx - 1))

        def ld(k):
            xk, bxk = sc["xk"][(base + k) % nx]
            S.dma("sp", bxk, lambda e: [e.dma_start(out=xk[:, :T], in_=src[k * P:(k + 1) * P, t0:t0 + T])],
                  reads=[self.rb(skey, t0 // 512, k)], writes=[bxk])

        for k in range(LA):
            ld(k)
        for k in range(KC):
            if k + LA < KC:
                ld(k + LA)
            xk, bxk = sc["xk"][(base + k) % nx]
            tmp, btmp = sc["tmp"][k % 2]
            S.op("dve", lambda e: e.scalar_tensor_tensor(out=tmp[:, :T], in0=yT[:, k, :T], scalar=gate[:, k:k + 1],
                                                         in1=rstd[:, :T], op0=ALU.mult, op1=ALU.mult),
                 reads=[byT, brstd, bmod], writes=[btmp])
            ok, bok = sc["ok"][k % 2]
            S.op("pool", lambda e: e.tensor_tensor(out=ok[:, :T], in0=tmp[:, :T], in1=xk[:, :T], op=ALU.add),
                 reads=[btmp, bxk], writes=[bok])
            S.dma(store_q, bok, lambda e: [e.dma_start(out=dst[k * P:(k + 1) * P, t0:t0 + T], in_=ok[:, :T])],
                  reads=[bok], writes=[self.rb(dkey, t0 // 512, k)])

    def wstream(self, wst, W, kcn, ncols, tiles, evac, banks, cb=None, act_stationary=False,
                prefetch_only=False, wkey=None, before_last=None):
        S = self.S
        if cb is None:
            cb = 8192 // kcn
        cb = min(cb, ncols)
        npiece = max(1, (kcn * cb) // 4096)
        blocks = list(range(0, ncols, cb))
        loaded = {}
        pre = wst.setdefault("pre", {})
        if prefetch_only:
            blocks = blocks[:1]

        def load_block(c0):
            wi = self.wq
            self.wq += 1
            wb, bwb = wst["wb"][wi % 2]
            wv = wb[:, 0:kcn * cb].rearrange("p (k m) -> p k m", k=kcn)
            for pc in range(npiece):
                si = wst["si"]
                wst["si"] += 1
                stg, bstg = wst["stg"][si % len(wst["stg"])]
                if kcn * cb <= 4096:
                    kk0, kk1, cc0, cc1 = 0, kcn, 0, cb
                else:
                    h = kcn // npiece
                    kk0, kk1, cc0, cc1 = pc * h, (pc + 1) * h, 0, cb
                nk, ncl = kk1 - kk0, cc1 - cc0
                sv = stg[:, 0:nk * ncl].rearrange("p (k m) -> p k m", k=nk)
                src = W[kk0 * P:kk1 * P, c0 + cc0:c0 + cc1].rearrange("(k p) m -> p k m", p=P)
                S.dma("sp", bstg, lambda e: [e.dma_start(out=sv, in_=src)], writes=[bstg])
                ce = ("dve", "act")[si % 2]
                if ce == "dve":
                    S.op("dve", lambda e: e.tensor_copy(out=wv[:, kk0:kk1, cc0:cc1], in_=sv),
                         reads=[bstg], writes=[bwb])
                else:
                    S.op("act", lambda e: e.activation(out=wv[:, kk0:kk1, cc0:cc1], in_=sv, func=AF.Copy),
                         reads=[bstg], writes=[bwb])
            loaded[c0] = (wv, bwb)

        def compute_block(c0):
            wv, bwb = loaded.pop(c0)
            if act_stationary:
                for (lhs_fn, rbuf, T, tag) in tiles:
                    bi = banks[wst["bi"] % len(banks)]
                    wst["bi"] += 1
                    ps, pb = self.ps[bi], self.pb[bi]
                    S._deps("pe", [bwb, rbuf], [pb])
                    for k in range(kcn - 1):
                        self.nc.tensor.matmul(ps[:, :cb], lhs_fn(k), wv[:, k, 0:cb], start=(k == 0), stop=False)
                    S.op("pe", lambda e: e.matmul(ps[:, :cb], lhs_fn(kcn - 1), wv[:, kcn - 1, 0:cb],
                                                  start=(kcn == 1), stop=True),
                         reads=[bwb, rbuf], writes=[pb])
                    evac(c0 // cb, tag, ps[:, :cb], pb)
                return
            for m in range(cb // P):
                for (rhs_fn, rbuf, T, tag) in tiles:
                    bi = banks[wst["bi"] % len(banks)]
                    wst["bi"] += 1
                    ps, pb = self.ps[bi], self.pb[bi]
                    S._deps("pe", [bwb, rbuf], [pb])
                    for k in range(kcn - 1):
                        self.nc.tensor.matmul(ps[:, :T], wv[:, k, m * P:(m + 1) * P], rhs_fn(k),
                                              start=(k == 0), stop=False)
                    S.op("pe", lambda e: e.matmul(ps[:, :T], wv[:, kcn - 1, m * P:(m + 1) * P], rhs_fn(kcn - 1),
                                                  start=(kcn == 1), stop=True),
                         reads=[bwb, rbuf], writes=[pb])
                    evac((c0 // P) + m, tag, ps[:, :T], pb)

        if prefetch_only:
            load_block(blocks[0])
            pre[wkey] = loaded.pop(blocks[0])
            return
        if wkey is not None and wkey in pre:
            loaded[blocks[0]] = pre.pop(wkey)
        else:
            load_block(blocks[0])
        for i, c0 in enumerate(blocks):
            if i + 1 < len(blocks):
                load_block(blocks[i + 1])
            elif before_last is not None:
                before_last()
            compute_block(c0)

    def wstream_kslab(self, wst, W, kct, ncols, rhs_fn, rbuf, T, evac, wkey=None, before_last=None,
                      prefetch_only=False):
        S = self.S
        SL = 16
        nsl = kct // SL
        pre = wst.setdefault("pre", {})
        blocks = [(c0, sl) for c0 in range(0, ncols, 512) for sl in range(nsl)]

        def load(bk):
            c0, sl = bk
            wi = self.wq
            self.wq += 1
            wb, bwb = wst["wb"][wi % 2]
            wv = wb[:, 0:SL * 512].rearrange("p (k m) -> p k m", k=SL)
            for pc in range(2):
                si = wst["si"]
                wst["si"] += 1
                stg, bstg = wst["stg"][si % len(wst["stg"])]
                sv = stg[:, 0:8 * 512].rearrange("p (k m) -> p k m", k=8)
                r0 = (sl * SL + pc * 8) * P
                src = W[r0:r0 + 8 * P, c0:c0 + 512].rearrange("(k p) m -> p k m", p=P)
                S.dma("sp", bstg, lambda e: [e.dma_start(out=sv, in_=src)], writes=[bstg])
                if si % 2 == 0:
                    S.op("dve", lambda e: e.tensor_copy(out=wv[:, pc * 8:(pc + 1) * 8, :], in_=sv),
                         reads=[bstg], writes=[bwb])
                else:
                    S.op("act", lambda e: e.activation(out=wv[:, pc * 8:(pc + 1) * 8, :], in_=sv, func=AF.Copy),
                         reads=[bstg], writes=[bwb])
            return (wv, bwb)

        if prefetch_only:
            pre[wkey] = load(blocks[0])
            return
        loaded = {}
        if wkey is not None and wkey in pre:
            loaded[blocks[0]] = pre.pop(wkey)
        else:
            loaded[blocks[0]] = load(blocks[0])
        for i, bk in enumerate(blocks):
            if i + 1 < len(blocks):
                loaded[blocks[i + 1]] = load(blocks[i + 1])
            elif before_last is not None:
                before_last()
            c0, sl = bk
            wv, bwb = loaded.pop(bk)
            g = c0 // 512
            banks = [0, 1, 2, 3] if g % 2 == 0 else [4, 5, 6, 7]
            pbs = [self.pb[b_] for b_ in banks]
            S._deps("pe", [bwb, rbuf], pbs)
            for m in range(4):
                ps = self.ps[banks[m]]
                for k in range(SL):
                    first = (sl == 0 and k == 0)
                    last = (sl == nsl - 1 and k == SL - 1)
                    if m == 3 and k == SL - 1:
                        S.op("pe", lambda e: e.matmul(ps[:, :T], wv[:, k, m * P:(m + 1) * P], rhs_fn(sl * SL + k),
                                                      start=first, stop=last), reads=[bwb, rbuf], writes=pbs)
                    else:
                        self.nc.tensor.matmul(ps[:, :T], wv[:, k, m * P:(m + 1) * P], rhs_fn(sl * SL + k),
                                              start=first, stop=last)
            if sl == nsl - 1:
                for m in range(4):
                    evac(c0 // P + m, 0, self.ps[banks[m]][:, :T], self.pb[banks[m]])

    def wstream_bufs(self, ph, nstg=2, small=False):
        n1, n2 = (2048, 2048) if small else (4096, 8192)
        return {"stg": [self.sb(ph, "stg", [P, n1], F32) for _ in range(nstg)],
                "wb": [self.sb(ph, "wb", [P, n2], BF16) for _ in range(2)],
                "si": 0, "bi": 0}

    def mlp_phase(self, tiles, w1, w2, mods, after_latent=None):
        S = self.S
        with ExitStack() as ph:
            sc = self.scratch(ph)
            wst = self.wstream_bufs(ph)
            uT, buT = self.sb(ph, "uT", [P, KC, 512], BF16)
            hT, bhT = self.sb(ph, "hT", [P, FC, 512], BF16)
            yT, byT = self.sb(ph, "yT", [P, KC, 512], F32)
            rr = [self.sb(ph, "rr", [P, 512], F32) for _ in range(2)]
            cnt = {"r": 0}
            ti = 0
            ntl = len(tiles)
            self.wstream(wst, w1, KC, DFF, None, None, None, prefetch_only=True, wkey=("w1", 0))
            for tix, (src, skey, dst, dkey, t0, T, which) in enumerate(tiles):
                self.prenorm(sc, src, skey, t0, T, mods["gsc2"][:, which, :], mods["sh2"][:, which, :],
                             mods["buf"], uT, buT, 0)

                def ev1(m, tag, ps, pb):
                    r, br = rr[cnt["r"] % 2]
                    cnt["r"] += 1
                    S.op("act", lambda e: e.activation(out=r[:, :T], in_=ps, func=AF.Relu),
                         reads=[pb], writes=[br])
                    S.op("pool", lambda e: e.tensor_tensor(out=hT[:, m, :T], in0=r[:, :T], in1=r[:, :T], op=ALU.mult),
                         reads=[br], writes=[bhT])

                self.wstream(wst, w1, KC, DFF, [(lambda k: uT[:, k, :T], buT, T, 0)], ev1, banks=[0, 1, 2, 3],
                             wkey=("w1", tix),
                             before_last=lambda: self.wstream_kslab(wst, w2, FC, D, None, None, None, None,
                                                                    prefetch_only=True, wkey=("w2", tix)))

                def ev2(m, tag, ps, pb):
                    S.op("act", lambda e: e.activation(out=yT[:, m, :T], in_=ps, func=AF.Copy),
                         reads=[pb], writes=[byT])

                nxt = None
                if tix + 1 < ntl:
                    nxt = lambda: self.wstream(wst, w1, KC, DFF, None, None, None, prefetch_only=True,
                                               wkey=("w1", tix + 1))
                self.wstream_kslab(wst, w2, FC, D, (lambda k: hT[:, k, :T]), bhT, T, ev2,
                                   wkey=("w2", tix), before_last=nxt)
                self.postnorm(sc, yT, byT, T, mods["gate2"][:, which, :], mods["buf"], src, skey, dst, dkey, t0)
                ti += 1
                if ti == 2 and after_latent is not None:
                    after_latent()
            self.end_phase()

    def mod_stage(self, cT, w_mod_sh, b_mod_sh, modsh):
        S = self.S
        with ExitStack() as ph:
            ct, bct = self.sb(ph, "ct", [P, KC, 2], F32)
            st_, bst = self.sb(ph, "sT", [P, KC, 2], F32)
            bbs = [self.sb(ph, "bb", [2, MSH], F32) for _ in range(2)]
            mos = [self.sb(ph, "mo", [2, MSH], F32) for _ in range(2)]
            wf = [self.sb(ph, "wf", [P, KC, 512], F32) for _ in range(3)]
            S.dma("sp", bct, lambda e: [e.dma_start(out=ct[:], in_=cT)], writes=[bct])
            S.op("act", lambda e: e.activation(out=st_[:], in_=ct[:], func=AF.Silu), reads=[bct], writes=[bst])
            n = 0
            for i in range(DEPTH):
                bb, bbb = bbs[i % 2]
                mo, bmo = mos[i % 2]
                S.dma("sp", bbb, lambda e: [e.dma_start(out=bb[:], in_=b_mod_sh[i * MSH:(i + 1) * MSH].partition_broadcast(2))],
                      writes=[bbb])
                for j in range(MSH // 512):
                    w, bw = wf[n % 3]
                    src = w_mod_sh[i, :, j * 512:(j + 1) * 512].rearrange("(k p) m -> p k m", p=P)
                    S.dma("sp", bw, lambda e: [e.dma_start(out=w[:], in_=src)], writes=[bw])
                    ps, pb = self.ps[n % 2], self.pb[n % 2]
                    S._deps("pe", [bst, bw], [pb])
                    for k in range(KC - 1):
                        self.nc.tensor.matmul(ps[0:2, :], st_[:, k, :], w[:, k, :], start=(k == 0), stop=False)
                    S.op("pe", lambda e: e.matmul(ps[0:2, :], st_[:, KC - 1, :], w[:, KC - 1, :], start=False, stop=True),
                         reads=[bst, bw], writes=[pb])
                    c0 = j * 512
                    S.op("dve", lambda e: e.tensor_tensor(out=mo[:, c0:c0 + 512], in0=ps[0:2, :],
                                                          in1=bb[:, c0:c0 + 512], op=ALU.add),
                         reads=[pb, bbb], writes=[bmo])
                    n += 1
                S.dma("sp", bmo, lambda e: [e.dma_start(out=modsh[:, i * MSH:(i + 1) * MSH], in_=mo[:])], reads=[bmo],
                      writes=[self.rb("modsh")])
            self.end_phase()

    def alloc_mods(self):
        st = self.st
        m = {}
        m["fm"], m["bfm"] = self.sb(st, "modfm", [P, 6, KC, 2], F32)
        m["gv"], m["bgv"] = self.sb(st, "gvec", [P, DEPTH, 5, KC], F32)
        m["sel"], m["bsel"] = self.sb(st, "sel", [2, 2], F32)
        for nm in ("gsc1", "sh1", "gate1", "gsc2", "sh2", "gate2"):
            m[nm], _ = self.sb(st, nm, [P, 2, KC], F32)
        m["buf"] = Buf("mods")
        return m

    def load_mod_consts(self, m, gvec, sel):
        S = self.S
        S.dma("sp", m["bgv"], lambda e: [e.dma_start(out=m["gv"][:], in_=gvec)], writes=[m["bgv"]])
        S.dma("sp", m["bsel"], lambda e: [e.dma_start(out=m["sel"][:], in_=sel)], writes=[m["bsel"]])

    def mod_prep(self, m, modall, mkey, li):
        S = self.S
        with ExitStack() as ph:
            mr, bmr = self.sb(ph, "mrow", [2, MODW], F32)
            S.dma("sp", bmr, lambda e: [e.dma_start(out=mr[:, h * MSH:(h + 1) * MSH],
                                                    in_=modall[h, :, li * MSH:(li + 1) * MSH]) for h in range(2)],
                  reads=[self.rb(mkey)], writes=[bmr])
            ps, pb = self.ps[0], self.pb[0]
            S._deps("pe", [bmr, m["bsel"]], [pb])
            for j in range(6):
                for k in range(KC):
                    c = j * 32 + k * 2
                    ins_fn = lambda e: e.matmul(ps[:, c:c + 2], mr[0:2, j * D + k * P: j * D + (k + 1) * P],
                                                m["sel"][0:2, 0:2], start=True, stop=True)
                    if j == 5 and k == KC - 1:
                        S.op("pe", ins_fn, reads=[bmr, m["bsel"]], writes=[pb])
                    else:
                        ins_fn(self.nc.tensor)
            fm = m["fm"]
            S.op("dve", lambda e: e.tensor_copy(out=fm[:].rearrange("p a k w -> p (a k w)"), in_=ps[:, 0:192]),
                 reads=[pb], writes=[m["bfm"]])
            gv = m["gv"]
            rd = [m["bfm"], m["bgv"]]
            wr = [m["buf"]]
            for w in range(2):
                S.op("dve", lambda e: e.scalar_tensor_tensor(out=m["gsc1"][:, w, :], in0=fm[:, 1, :, w], scalar=1.0,
                                                             in1=gv[:, li, 0, :], op0=ALU.add, op1=ALU.mult),
                     reads=rd, writes=wr)
                S.op("dve", lambda e: e.tensor_copy(out=m["sh1"][:, w, :], in_=fm[:, 0, :, w]), reads=rd, writes=wr)
                S.op("dve", lambda e: e.tensor_tensor(out=m["gate1"][:, w, :], in0=fm[:, 2, :, w],
                                                      in1=gv[:, li, 1, :], op=ALU.mult), reads=rd, writes=wr)
                S.op("dve", lambda e: e.scalar_tensor_tensor(out=m["gsc2"][:, w, :], in0=fm[:, 4, :, w], scalar=1.0,
                                                             in1=gv[:, li, 2, :], op0=ALU.add, op1=ALU.mult),
                     reads=rd, writes=wr)
                S.op("dve", lambda e: e.tensor_copy(out=m["sh2"][:, w, :], in_=fm[:, 3, :, w]), reads=rd, writes=wr)
                S.op("dve", lambda e: e.tensor_tensor(out=m["gate2"][:, w, :], in0=fm[:, 5, :, w],
                                                      in1=gv[:, li, 3, :], op=ALU.mult), reads=rd, writes=wr)
            self.end_phase()


    def proj_post(self, opT, bopT, W, tiles, gate_name, mods):
        S = self.S
        with ExitStack() as ph:
            sc = self.scratch(ph)
            wst = self.wstream_bufs(ph)
            yT, byT = self.sb(ph, "yT", [P, KC, 512], F32)
            for (col0, T, which, src, skey, dst, dkey, t0) in tiles:
                def ev(m, tag, ps, pb):
                    S.op("act", lambda e: e.activation(out=yT[:, m, :T], in_=ps, func=AF.Copy),
                         reads=[pb], writes=[byT])
                self.wstream(wst, W, KC, D, [(lambda k: opT[:, k, col0:col0 + T], bopT, T, 0)], ev,
                             banks=[0, 1, 2, 3])
                self.postnorm(sc, yT, byT, T, mods[gate_name][:, which, :], mods["buf"], src, skey, dst, dkey, t0)
            self.end_phase()

    def rope_evac(self, rp, ps, pb, T, cos_ap, sin_ap, btab, out_ap, bout):
        S = self.S
        i = rp["i"]
        rp["i"] += 1
        qf, bqf = rp["qf"][i % 2]
        t1, bt1 = rp["t1"][i % 2]
        t2, bt2 = rp["t2"][i % 2]
        bank = rp["banks"][i % len(rp["banks"])]
        ps2, pb2 = self.ps[bank], self.pb[bank]
        S.op("act", lambda e: e.activation(out=qf[:, :T], in_=ps, func=AF.Copy), reads=[pb], writes=[bqf])
        S.op("pe", lambda e: e.matmul(ps2[:, :T], rp["perm"][:], qf[:, :T], start=True, stop=True),
             reads=[bqf, rp["bperm"]], writes=[pb2])
        S.op("pool", lambda e: e.tensor_tensor(out=t1[:, :T], in0=qf[:, :T], in1=cos_ap, op=ALU.mult),
             reads=[bqf, btab], writes=[bt1])
        S.op("dve", lambda e: e.tensor_tensor(out=t2[:, :T], in0=ps2[:, :T], in1=sin_ap, op=ALU.mult),
             reads=[pb2, btab], writes=[bt2])
        S.op("dve", lambda e: e.tensor_tensor(out=out_ap, in0=t1[:, :T], in1=t2[:, :T], op=ALU.add),
             reads=[bt1, bt2], writes=[bout])

    def attn_layer(self, li, ai, io, w_qkv, w_o, lamv, gsub, tabs, scr, mods, with_ctx):
        S = self.S
        nc = self.nc
        lambda_init = 0.8 - 0.6 * math.exp(-0.3 * li)
        NQ = LH + (NCTX if with_ctx else 0)
        kT_s, v_s, qT_s = scr["kT"], scr["v"], scr["qT"]
        with ExitStack() as lay:
            cst, bcst = self.sb(lay, "acst", [P, 8], F32)
            with ExitStack() as ph:
                lv, blv = self.sb(ph, "lv", [P, 4, 64], F32)
                gs_, bgs = self.sb(ph, "gsl", [P, 1], F32)
                pr, bpr = self.sb(ph, "pr", [P, 2, 64], F32)
                S.dma("sp", blv, lambda e: [e.dma_start(out=lv[:, j, :], in_=lamv[j].partition_broadcast(P))
                                            for j in range(4)], writes=[blv])
                S.dma("sp", bgs, lambda e: [e.dma_start(out=gs_[:], in_=gsub.rearrange("(p o) -> p o", o=1))],
                      writes=[bgs])
                S.op("dve", lambda e: e.tensor_tensor(out=pr[:, 0, :], in0=lv[:, 0, :], in1=lv[:, 1, :], op=ALU.mult),
                     reads=[blv], writes=[bpr])
                S.op("dve", lambda e: e.tensor_tensor(out=pr[:, 1, :], in0=lv[:, 2, :], in1=lv[:, 3, :], op=ALU.mult),
                     reads=[blv], writes=[bpr])
                S.op("dve", lambda e: e.reduce_sum(out=cst[:, 0:1], in_=pr[:, 0, :], axis=mybir.AxisListType.X),
                     reads=[bpr], writes=[bcst])
                S.op("dve", lambda e: e.reduce_sum(out=cst[:, 1:2], in_=pr[:, 1, :], axis=mybir.AxisListType.X),
                     reads=[bpr], writes=[bcst])
                S.op("act", lambda e: e.activation(out=cst[:, 2:4], in_=cst[:, 0:2], func=AF.Exp),
                     reads=[bcst], writes=[bcst])
                S.op("dve", lambda e: e.tensor_tensor(out=cst[:, 4:5], in0=cst[:, 3:4], in1=cst[:, 2:3], op=ALU.subtract),
                     reads=[bcst], writes=[bcst])
                S.op("dve", lambda e: e.tensor_scalar(out=cst[:, 5:6], in0=cst[:, 4:5], scalar1=-float(lambda_init),
                                                      scalar2=None, op0=ALU.add), reads=[bcst], writes=[bcst])
                S.op("dve", lambda e: e.tensor_scalar(out=cst[:, 6:7], in0=gs_[:, 0:1], scalar1=float(1.0 - lambda_init),
                                                      scalar2=None, op0=ALU.mult), reads=[bgs, bcst], writes=[bcst])
                self.end_phase()
            neglam = cst[:, 5:6]
            gsl = cst[:, 6:7]

            with ExitStack() as ph:
                sc = self.scratch(ph)
                wst = self.wstream_bufs(ph)
                uT, buT = self.sb(ph, "uTall", [P, KC, NKEY], BF16)
                rp = self.rope_bufs(ph, tabs)
                ck, bck = self.sb(ph, "ropek", [P, 2, L], F32)
                S.dma("sp", bck, lambda e: [e.dma_start(out=ck[:, 0, :], in_=tabs["ropek_cos"]),
                                            e.dma_start(out=ck[:, 1, :], in_=tabs["ropek_sin"])], writes=[bck])
                kh = [self.sb(ph, "kh", [P, NKEY], BF16) for _ in range(2)]
                vo = [self.sb(ph, "vo", [P, 512], BF16) for _ in range(2)]
                ktiles = []
                for t in range(4):
                    self.prenorm(sc, io["x_full"][t // 2], ("xfull", io["kfull"], t // 2), (t % 2) * 512, 512,
                                 mods["gsc1"][:, 0, :], mods["sh1"][:, 0, :], mods["buf"], uT, buT, t * 512)
                    ktiles.append((lambda k, t=t: uT[:, k, t * 512:(t + 1) * 512], buT, 512, t))
                self.prenorm(sc, io["xc"], io["kxc"], 0, NCTX, mods["gsc1"][:, 1, :], mods["sh1"][:, 1, :],
                             mods["buf"], uT, buT, L)
                ktiles.append((lambda k: uT[:, k, L:L + NCTX], buT, NCTX, 4))

                def evK(m, tag, ps, pb):
                    kb_, bkb = kh[m % 2]
                    T = 512 if tag < 4 else NCTX
                    c0 = tag * 512
                    if tag < 4:
                        self.rope_evac(rp, ps, pb, T, ck[:, 0, c0:c0 + T], ck[:, 1, c0:c0 + T], bck,
                                       kb_[:, c0:c0 + T], bkb)
                    else:
                        S.op("act", lambda e: e.activation(out=kb_[:, c0:c0 + T], in_=ps, func=AF.Copy),
                             reads=[pb], writes=[bkb])
                        S.dma("sp", bkb, lambda e: [e.dma_start(out=kT_s[m], in_=kb_[:])], reads=[bkb],
                              writes=[self.rb("kT", m)])

                self.wstream(wst, w_qkv[:, D:2 * D], KC, D, ktiles, evK, banks=[0, 1, 2, 3])

                vtiles = [(lambda k, kc=kc: uT[:, k, kc * P:(kc + 1) * P], buT, P, kc) for kc in range(NKEY // P)]

                def evV(cblk, tag, ps, pb):
                    v_, bv = vo[tag % 2]
                    eng = ("act", "dve")[tag % 2]
                    if eng == "act":
                        S.op("act", lambda e: e.activation(out=v_[:], in_=ps, func=AF.Copy), reads=[pb], writes=[bv])
                    else:
                        S.op("dve", lambda e: e.tensor_copy(out=v_[:], in_=ps), reads=[pb], writes=[bv])
                    S.dma("sp", bv, lambda e: [e.dma_start(out=v_s[tag * P:(tag + 1) * P, cblk * 512:(cblk + 1) * 512],
                                                           in_=v_[:])], reads=[bv], writes=[self.rb("v", cblk, tag)])

                self.wstream(wst, w_qkv[:, 2 * D:3 * D], KC, D, vtiles, evV, banks=[4, 5, 6], act_stationary=True)
                self.end_phase()

            with ExitStack() as ph:
                sc = self.scratch(ph)
                wst = self.wstream_bufs(ph)
                uT, buT = self.sb(ph, "uTq", [P, KC, NQ], BF16)
                rp = self.rope_bufs(ph, tabs)
                cq, bcq = self.sb(ph, "ropeq", [P, 2, LH], F32)
                S.dma("sp", bcq, lambda e: [e.dma_start(out=cq[:, 0, :], in_=tabs["ropeq_cos"]),
                                            e.dma_start(out=cq[:, 1, :], in_=tabs["ropeq_sin"])], writes=[bcq])
                qh = [self.sb(ph, "qhb", [P, NQ], BF16) for _ in range(2)]
                qtiles = []
                for t in range(2):
                    self.prenorm(sc, io["x_own"], io["kown"], t * 512, 512,
                                 mods["gsc1"][:, 0, :], mods["sh1"][:, 0, :], mods["buf"], uT, buT, t * 512)
                    qtiles.append((lambda k, t=t: uT[:, k, t * 512:(t + 1) * 512], buT, 512, t))
                if with_ctx:
                    self.prenorm(sc, io["xc"], io["kxc"], 0, NCTX, mods["gsc1"][:, 1, :], mods["sh1"][:, 1, :],
                                 mods["buf"], uT, buT, LH)
                    qtiles.append((lambda k: uT[:, k, LH:LH + NCTX], buT, NCTX, 2))
                nqt = len(qtiles)

                def evQ(m, tag, ps, pb):
                    qb_, bqb = qh[m % 2]
                    T = 512 if tag < 2 else NCTX
                    c0 = tag * 512
                    if tag < 2:
                        self.rope_evac(rp, ps, pb, T, cq[:, 0, c0:c0 + T], cq[:, 1, c0:c0 + T], bcq,
                                       qb_[:, c0:c0 + T], bqb)
                    else:
                        S.op("act", lambda e: e.activation(out=qb_[:, c0:c0 + T], in_=ps, func=AF.Copy),
                             reads=[pb], writes=[bqb])
                    if tag == nqt - 1:
                        S.dma("sp", bqb, lambda e: [e.dma_start(out=qT_s[m, :, 0:NQ], in_=qb_[:])], reads=[bqb],
                              writes=[self.rb("qT", m)])

                self.wstream(wst, w_qkv[:, 0:D], KC, D, qtiles, evQ, banks=[0, 1, 2, 3])
                self.end_phase()

            aT, baT = self.sb(lay, "attnT", [P, KC, NQ], BF16)
            with ExitStack() as ph:
                sc = self.scratch(ph, nx=1)
                khb = [self.sb(ph, "khc", [P, NKEY], BF16) for _ in range(2)]
                vhb = [self.sb(ph, "vhc", [P, NKEY // P, P], BF16) for _ in range(2)]
                qhb = [self.sb(ph, "qhc", [P, NQ], BF16) for _ in range(2)]
                eb = [self.sb(ph, "eb", [P, 2, 512], BF16) for _ in range(4)]
                es = [self.sb(ph, "esum", [P, 2, 512], F32) for _ in range(2)]
                ones32, bones32 = self.sb(ph, "ones32", [P, P], F32)
                S.op("pool", lambda e: e.memset(ones32[:], 1.0), writes=[bones32])
                fr = [self.sb(ph, "fr", [P, 512], F32) for _ in range(4)]
                zsb = [self.sb(ph, "zs", [P, 2, 512], F32) for _ in range(2)]
                ei = 0
                si = 0
                qi = 0
                pend = [None]
                psall = self.psall
                for h in range(NH):
                    k_, bk = khb[h % 2]
                    v_, bv = vhb[h % 2]
                    q_, bq = qhb[h % 2]
                    S.dma("sp", bk, lambda e: [e.dma_start(out=k_[:], in_=kT_s[h])], reads=[self.rb("kT", h)],
                          writes=[bk])
                    S.dma("sp", bv, lambda e: [e.dma_start(out=v_[:], in_=v_s[:, h * P:(h + 1) * P].rearrange(
                        "(c p) e -> p c e", p=P))],
                          reads=[self.rb("v", h // 4, kc) for kc in range(NKEY // P)], writes=[bv])
                    S.dma("sp", bq, lambda e: [e.dma_start(out=q_[:], in_=qT_s[h, :, 0:NQ])], reads=[self.rb("qT", h)],
                          writes=[bq])
                    qts = [(0, 512, 0, NKEY // P), (512, 512, 0, NKEY // P)]
                    if with_ctx:
                        qts.append((LH, NCTX, L // P, NKEY // P))
                    for (q0, T, kc0, kc1) in qts:
                        esum, besum = es[qi % 2]
                        ab = 4 + 2 * (qi % 2)
                        qi += 1
                        its = list(range(kc0, kc1))
                        slots = {}

                        def emit_qk(i):
                            nonlocal si, ei
                            kc = its[i]
                            p2 = 2 * (si % 2)
                            si += 1
                            e_, be = eb[ei % 4]
                            ei += 1

                            def qk(e):
                                e.matmul(psall[:, p2, :T], k_[0:64, kc * P:(kc + 1) * P], q_[0:64, q0:q0 + T],
                                         start=True, stop=True)
                                return e.matmul(psall[:, p2 + 1, :T], k_[64:128, kc * P:(kc + 1) * P],
                                                q_[64:128, q0:q0 + T], start=True, stop=True)
                            S.op("pe", qk, reads=[bk, bq], writes=[self.pb[p2], self.pb[p2 + 1]])
                            S.op("act", lambda e: e.activation(out=e_[:, :, :T], in_=psall[:, p2:p2 + 2, :T], func=AF.Exp,
                                                               scale=0.125),
                                 reads=[self.pb[p2], self.pb[p2 + 1]], writes=[be])
                            slots[i] = (e_, be)

                        def emit_pv(i):
                            kc = its[i]
                            e_, be = slots.pop(i)

                            def pv(e):
                                e.matmul(self.ps[ab][:, :T], v_[:, kc, :], e_[:, 0, :T],
                                         start=(kc == kc0), stop=(kc == kc1 - 1))
                                return e.matmul(self.ps[ab + 1][:, :T], v_[:, kc, :], e_[:, 1, :T],
                                                start=(kc == kc0), stop=(kc == kc1 - 1))
                            S.op("pe", pv, reads=[be, bv], writes=[self.pb[ab], self.pb[ab + 1]])
                            if kc == kc0:
                                S.op("dve", lambda e: e.tensor_copy(out=esum[:, :, :T], in_=e_[:, :, :T]),
                                     reads=[be], writes=[besum])
                            else:
                                S.op("dve", lambda e: e.tensor_tensor(out=esum[:, :, :T], in0=esum[:, :, :T],
                                                                      in1=e_[:, :, :T], op=ALU.add),
                                     reads=[be, besum], writes=[besum])

                        LA = 1
                        for i in range(min(LA, len(its))):
                            emit_qk(i)
                        for i in range(len(its)):
                            if i + LA < len(its):
                                emit_qk(i + LA)
                            emit_pv(i)
                            if i == 4 and pend[0] is not None:
                                pend[0]()
                                pend[0] = None
                        if pend[0] is not None:
                            pend[0]()
                            pend[0] = None

                        def finish(T=T, q0=q0, h=h, esum=esum, besum=besum, ab=ab, zsq=zsb[qi % 2]):
                            nonlocal si
                            zb = 2 * (si % 2)
                            si += 1

                            def zz(e):
                                e.matmul(psall[:, zb, :T], ones32[:], esum[:, 0, :T], start=True, stop=True)
                                return e.matmul(psall[:, zb + 1, :T], ones32[:], esum[:, 1, :T], start=True, stop=True)
                            S.op("pe", zz, reads=[besum, bones32], writes=[self.pb[zb], self.pb[zb + 1]])
                            zs, bzs = zsq
                            o0, bo0 = fr[1]
                            t1, bt1 = fr[2]
                            oo, boo = fr[3]
                            S.op("act", lambda e: e.activation(out=zs[:, :, :T], in_=psall[:, zb:zb + 2, :T], func=AF.Ln),
                                 reads=[self.pb[zb], self.pb[zb + 1]], writes=[bzs])
                            S.op("act", lambda e: e.activation(out=zs[:, :, :T], in_=zs[:, :, :T], func=AF.Exp, scale=-1.0),
                                 reads=[bzs], writes=[bzs])
                            S.op("dve", lambda e: e.tensor_tensor(out=o0[:, :T], in0=self.ps[ab][:, :T], in1=zs[:, 0, :T],
                                                                  op=ALU.mult), reads=[self.pb[ab], bzs], writes=[bo0])
                            S.op("dve", lambda e: e.tensor_tensor(out=t1[:, :T], in0=self.ps[ab + 1][:, :T], in1=zs[:, 1, :T],
                                                                  op=ALU.mult), reads=[self.pb[ab + 1], bzs], writes=[bt1])
                            S.op("dve", lambda e: e.scalar_tensor_tensor(out=oo[:, :T], in0=t1[:, :T], scalar=neglam,
                                                                         in1=o0[:, :T], op0=ALU.mult, op1=ALU.add),
                                 reads=[bt1, bo0, bcst], writes=[boo])
                            rstd, brstd = self.stats(sc, [(oo[:, :T], boo)], T, P, SUBLN_EPS, zb, 0)
                            S.op("dve", lambda e: e.scalar_tensor_tensor(out=aT[:, h, q0:q0 + T], in0=oo[:, :T], scalar=gsl,
                                                                         in1=rstd[:, :T], op0=ALU.mult, op1=ALU.mult),
                                 reads=[boo, brstd, bcst], writes=[baT])
                        pend[0] = finish
                if pend[0] is not None:
                    pend[0]()
                    pend[0] = None
                self.end_phase()

            tiles = [(0, 512, 0, io["x_own"], io["kown"], io["dst_own"], io["kdst_own"], 0),
                     (512, 512, 0, io["x_own"], io["kown"], io["dst_own"], io["kdst_own"], 512)]
            if with_ctx:
                tiles.append((LH, NCTX, 1, io["xc"], io["kxc"], io["dst_c"], io["kdst_c"], 0))
            self.proj_post(aT, baT, w_o, tiles, "gate1", mods)

    def rope_bufs(self, ph, tabs):
        rp = {"i": 0, "banks": [4, 5]}
        rp["qf"] = [self.sb(ph, "qf", [P, 512], F32) for _ in range(2)]
        rp["t1"] = [self.sb(ph, "t1", [P, 512], F32) for _ in range(2)]
        rp["t2"] = [self.sb(ph, "t2", [P, 512], F32) for _ in range(2)]
        rp["perm"], rp["bperm"] = self.sb(ph, "perm", [P, P], F32)
        self.S.dma("sp", rp["bperm"], lambda e: [e.dma_start(out=rp["perm"][:], in_=tabs["perm"])],
                   writes=[rp["bperm"]])
        return rp

    def fourier_layer(self, io, w_f, tabs, scr, mods):
        S = self.S
        fT_s = scr["fT"]
        NT = LH + NCTX
        NLC = NKEY // P
        with ExitStack() as ph:
            sc = self.scratch(ph)
            uT, buT = self.sb(ph, "uTall", [P, KC, NKEY], BF16)
            cc, bcc = self.sb(ph, "dftc", [P, 2, 4, 512], BF16)
            lt_, blt = self.sb(ph, "dftl", [P, 2, KC, 512], BF16)
            c256, bc256 = self.sb(ph, "dft256", [P, 2, 2, NCTX], BF16)
            AB, bAB = self.sb(ph, "AB", [P, 2, NLC, 512], BF16)
            fo = [self.sb(ph, "fo", [P, 512], BF16) for _ in range(2)]
            S.dma("sp", bcc, lambda e: [e.dma_start(out=cc[:, t, :, :], in_=tabs["dftc"][t].rearrange("(j p) m -> p j m", p=P))
                                        for t in range(2)], writes=[bcc])
            S.dma("sp", bc256, lambda e: [e.dma_start(out=c256[:, t, :, :], in_=tabs["dft256"][t].rearrange("(j p) m -> p j m", p=P))
                                          for t in range(2)], writes=[bc256])
            for t in range(4):
                self.prenorm(sc, io["x_full"][t // 2], ("xfull", io["kfull"], t // 2), (t % 2) * 512, 512,
                             mods["gsc1"][:, 0, :], mods["sh1"][:, 0, :], mods["buf"], uT, buT, t * 512)
            self.prenorm(sc, io["xc"], io["kxc"], 0, NCTX, mods["gsc1"][:, 1, :], mods["sh1"][:, 1, :],
                         mods["buf"], uT, buT, L)
            bi = 0
            fi = 0
            for g in range(4):
                for lc in range(NLC):
                    for t in range(2):
                        bk = bi % 4
                        bi += 1
                        ps, pb = self.ps[bk], self.pb[bk]
                        S._deps("pe", [buT, bcc], [pb])
                        for j in range(3):
                            self.nc.tensor.matmul(ps[:, :512], uT[:, g * 4 + j, lc * P:(lc + 1) * P], cc[:, t, j, :],
                                                  start=(j == 0), stop=False)
                        S.op("pe", lambda e: e.matmul(ps[:, :512], uT[:, g * 4 + 3, lc * P:(lc + 1) * P], cc[:, t, 3, :],
                                                      start=False, stop=True), reads=[buT, bcc], writes=[pb])
                        if (lc + t) % 2 == 0:
                            S.op("act", lambda e: e.activation(out=AB[:, t, lc, :], in_=ps[:, :512], func=AF.Copy),
                                 reads=[pb], writes=[bAB])
                        else:
                            S.op("dve", lambda e: e.tensor_copy(out=AB[:, t, lc, :], in_=ps[:, :512]),
                                 reads=[pb], writes=[bAB])
                for lt in range(2):
                    S.dma("sp", blt, lambda e: [e.dma_start(out=lt_[:, t, :, :],
                                                            in_=tabs["dftl"][t, :, lt * 512:(lt + 1) * 512].rearrange(
                                                                "(c p) m -> p c m", p=P)) for t in range(2)],
                          writes=[blt])
                    for mc in range(4):
                        bk = 4 + (fi % 3)
                        ps, pb = self.ps[bk], self.pb[bk]
                        S._deps("pe", [bAB, blt], [pb])
                        n = 0
                        for t in range(2):
                            for lc in range(KC):
                                n += 1
                                if n < 2 * KC:
                                    self.nc.tensor.matmul(ps[:, :512], AB[:, t, lc, mc * P:(mc + 1) * P], lt_[:, t, lc, :],
                                                          start=(n == 1), stop=False)
                                else:
                                    S.op("pe", lambda e: e.matmul(ps[:, :512], AB[:, t, lc, mc * P:(mc + 1) * P],
                                                                  lt_[:, t, lc, :], start=False, stop=True),
                                         reads=[bAB, blt], writes=[pb])
                        f_, bf = fo[fi % 2]
                        fi += 1
                        S.op("act", lambda e: e.activation(out=f_[:, :512], in_=ps[:, :512], func=AF.Copy, scale=1.0 / 1024.0),
                             reads=[pb], writes=[bf])
                        ch = g * 4 + mc
                        S.dma("sp", bf, lambda e: [e.dma_start(out=fT_s[ch * P:(ch + 1) * P, lt * 512:(lt + 1) * 512],
                                                               in_=f_[:, :512])], reads=[bf], writes=[self.rb("fT", ch, lt)])
                for mc in range(4):
                    bk = 4 + (fi % 3)
                    ps, pb = self.ps[bk], self.pb[bk]
                    S._deps("pe", [bAB, bc256], [pb])
                    n = 0
                    for t in range(2):
                        for lc in range(2):
                            n += 1
                            if n < 4:
                                self.nc.tensor.matmul(ps[:, :NCTX], AB[:, t, KC + lc, mc * P:(mc + 1) * P], c256[:, t, lc, :],
                                                      start=(n == 1), stop=False)
                            else:
                                S.op("pe", lambda e: e.matmul(ps[:, :NCTX], AB[:, t, KC + lc, mc * P:(mc + 1) * P],
                                                              c256[:, t, lc, :], start=False, stop=True),
                                     reads=[bAB, bc256], writes=[pb])
                    f_, bf = fo[fi % 2]
                    fi += 1
                    S.op("act", lambda e: e.activation(out=f_[:, :NCTX], in_=ps[:, :NCTX], func=AF.Copy,
                                                       scale=float(1.0 / math.sqrt(NCTX * 512.0))),
                         reads=[pb], writes=[bf])
                    ch = g * 4 + mc
                    S.dma("sp", bf, lambda e: [e.dma_start(out=fT_s[ch * P:(ch + 1) * P, LH:LH + NCTX], in_=f_[:, :NCTX])],
                          reads=[bf], writes=[self.rb("fT", ch, 2)])
            self.end_phase()
        with ExitStack() as lay:
            fT, bfT = self.sb(lay, "fT", [P, KC, NT], BF16)
            S.dma("sp", bfT, lambda e: [e.dma_start(out=fT[:], in_=fT_s.rearrange("(k p) t -> p k t", p=P))],
                  reads=[self.rb("fT", ch, x) for ch in range(KC) for x in range(3)], writes=[bfT])
            tiles = [(0, 512, 0, io["x_own"], io["kown"], io["dst_own"], io["kdst_own"], 0),
                     (512, 512, 0, io["x_own"], io["kown"], io["dst_own"], io["kdst_own"], 512),
                     (LH, NCTX, 1, io["xc"], io["kxc"], io["dst_c"], io["kdst_c"], 0)]
            self.proj_post(fT, bfT, w_f, tiles, "gate1", mods)

    def stats_stream(self, sc, loads, T, dst, bdst, bank=7):
        S = self.S
        ps, pb = self.ps[bank], self.pb[bank]
        nx = len(sc["xk"])
        for k in range(KC):
            xk, bxk = sc["xk"][sc["i"] % nx]
            sc["i"] += 1
            pairs, rds = loads(k, xk)
            S.dma("sp", bxk, lambda e: [e.dma_start(out=o, in_=i_) for (o, i_) in pairs], reads=rds, writes=[bxk])
            sq, bsq = sc["sq"][k % 2]
            S.op("act", lambda e: e.activation(out=sq[:, :T], in_=xk[:, :T], func=AF.Square), reads=[bxk], writes=[bsq])
            S.op("pe", lambda e: e.matmul(ps[:, :T], self.ones[:], sq[:, :T], start=(k == 0), stop=(k == KC - 1)),
                 reads=[bsq, self.b_ones], writes=[pb])
        rt, brt = sc["rt"]
        S.op("act", lambda e: e.activation(out=rt[:, :T], in_=ps[:, :T], func=AF.Ln, bias=self.epsb[:, 0:1], scale=1.0 / D),
             reads=[pb, self.b_eps], writes=[brt])
        S.op("act", lambda e: e.activation(out=dst, in_=rt[:, :T], func=AF.Exp, scale=-0.5), reads=[brt], writes=[bdst])

    def pool_layer(self, li, io, w_pool, tabs, scr, mods):
        S = self.S
        NT = LH + NCTX
        LP = LH + 16
        CP = NCTX + 16
        xf = io["x_full"]
        with ExitStack() as lay:
            mT, bmT = self.sb(lay, "mT", [P, KC, NT], BF16)
            with ExitStack() as ph:
                sc = self.scratch(ph)
                rs, brs = self.sb(ph, "rsall", [P, LH + 16 + NCTX], F32)
                xo = [self.sb(ph, "xo", [P, LH], F32) for _ in range(2)]
                xh = [self.sb(ph, "xh", [P, 16], F32) for _ in range(2)]
                xc_ = [self.sb(ph, "xcc", [P, NCTX], F32) for _ in range(2)]
                th = [self.sb(ph, "th", [P, 16], F32) for _ in range(2)]
                up = [self.sb(ph, "up", [P, LP], F32) for _ in range(2)]
                uc = [self.sb(ph, "uc", [P, CP], F32) for _ in range(2)]
                pa = [self.sb(ph, "pa", [P, LP], F32) for _ in range(2)]
                inv, binv = self.sb(ph, "inv", [P, 4, LH], F32)
                invc, binvc = self.sb(ph, "invc", [P, 4, NCTX], F32)
                msk, bmsk = self.sb(ph, "msk", [P, 2], F32)
                S.dma("sp", binv, lambda e: [e.dma_start(out=inv[:], in_=tabs["pinv"])], writes=[binv])
                S.dma("sp", binvc, lambda e: [e.dma_start(out=invc[:], in_=tabs["pinvc"])], writes=[binvc])
                S.dma("sp", bmsk, lambda e: [e.dma_start(out=msk[:], in_=tabs["pmsk"])], writes=[bmsk])
                for j in range(2):
                    S.op("pool", lambda e: e.memset(uc[j][0][:], 0.0), writes=[uc[j][1]])
                for t in range(2):
                    self.stats_stream(sc, lambda k, xk, t=t: ([(xk[:, :512], io["x_own"][k * P:(k + 1) * P, t * 512:(t + 1) * 512])],
                                                              [self.rb(io["kown"], t, k)]), 512, rs[:, t * 512:(t + 1) * 512], brs)
                self.stats_stream(sc, lambda k, xk: ([(xk[:, 0:8], xf[0](k, LH - 8, LH)),
                                                      (xk[:, 8:16], xf[1](k, 0, 8))],
                                                     [self.rb(("xfull", io["kfull"], 0), 1, k), self.rb(("xfull", io["kfull"], 1), 0, k)]),
                                  16, rs[:, LH:LH + 16], brs)
                self.stats_stream(sc, lambda k, xk: ([(xk[:, :NCTX], io["xc"][k * P:(k + 1) * P, 0:NCTX])],
                                                     [self.rb(io["kxc"], 0, k)]), NCTX, rs[:, LH + 16:LH + 16 + NCTX], brs)
                gsc, sh = mods["gsc1"], mods["sh1"]
                bm = mods["buf"]
                for k in range(KC):
                    g = k // 4
                    w = 2 << g
                    x_, bx = xo[k % 2]
                    h_, bh = xh[k % 2]
                    c_, bc = xc_[k % 2]
                    t_, bt = th[k % 2]
                    u_, bu = up[k % 2]
                    uc_, buc = uc[k % 2]
                    S.dma("sp", bx, lambda e: [e.dma_start(out=x_[:], in_=io["x_own"][k * P:(k + 1) * P, :])],
                          reads=[self.rb(io["kown"], 0, k), self.rb(io["kown"], 1, k)], writes=[bx])
                    S.dma("sp", bh, lambda e: [e.dma_start(out=h_[:, 0:8], in_=xf[0](k, LH - 8, LH)),
                                               e.dma_start(out=h_[:, 8:16], in_=xf[1](k, 0, 8))],
                          reads=[self.rb(("xfull", io["kfull"], 0), 1, k), self.rb(("xfull", io["kfull"], 1), 0, k)], writes=[bh])
                    S.dma("sp", bc, lambda e: [e.dma_start(out=c_[:], in_=io["xc"][k * P:(k + 1) * P, :])],
                          reads=[self.rb(io["kxc"], 0, k)], writes=[bc])
                    S.op("dve", lambda e: e.scalar_tensor_tensor(out=u_[:, 8:8 + LH], in0=x_[:], scalar=gsc[:, 0, k:k + 1],
                                                                 in1=rs[:, 0:LH], op0=ALU.mult, op1=ALU.mult),
                         reads=[bx, brs, bm], writes=[bu])
                    S.op("act", lambda e: e.activation(out=u_[:, 8:8 + LH], in_=u_[:, 8:8 + LH], func=AF.Identity,
                                                       bias=sh[:, 0, k:k + 1], scale=1.0), reads=[bu, bm], writes=[bu])
                    S.op("dve", lambda e: e.scalar_tensor_tensor(out=t_[:], in0=h_[:], scalar=gsc[:, 0, k:k + 1],
                                                                 in1=rs[:, LH:LH + 16], op0=ALU.mult, op1=ALU.mult),
                         reads=[bh, brs, bm], writes=[bt])
                    S.op("act", lambda e: e.activation(out=t_[:], in_=t_[:], func=AF.Identity, bias=sh[:, 0, k:k + 1], scale=1.0),
                         reads=[bt, bm], writes=[bt])
                    S.op("dve", lambda e: e.tensor_scalar(out=u_[:, 0:8], in0=t_[:, 0:8], scalar1=msk[:, 0:1], scalar2=None,
                                                          op0=ALU.mult), reads=[bt, bmsk], writes=[bu])
                    S.op("dve", lambda e: e.tensor_scalar(out=u_[:, 8 + LH:16 + LH], in0=t_[:, 8:16], scalar1=msk[:, 1:2],
                                                          scalar2=None, op0=ALU.mult), reads=[bt, bmsk], writes=[bu])
                    S.op("dve", lambda e: e.scalar_tensor_tensor(out=uc_[:, 8:8 + NCTX], in0=c_[:], scalar=gsc[:, 1, k:k + 1],
                                                                 in1=rs[:, LH + 16:LH + 16 + NCTX], op0=ALU.mult, op1=ALU.mult),
                         reads=[bc, brs, bm], writes=[buc])
                    S.op("act", lambda e: e.activation(out=uc_[:, 8:8 + NCTX], in_=uc_[:, 8:8 + NCTX], func=AF.Identity,
                                                       bias=sh[:, 1, k:k + 1], scale=1.0), reads=[buc, bm], writes=[buc])
                    for (src, bsrc, n_, ln, itab, bit, col0) in ((u_, bu, LH, LP, inv, binv, 0), (uc_, buc, NCTX, CP, invc, binvc, LH)):
                        cur, bcur = src, bsrc
                        clen = ln
                        for s_ in range(g + 1):
                            step = 1 << s_
                            nxt, bnxt = pa[s_ % 2]
                            eng = ("pool", "dve")[s_ % 2]
                            nl = clen - step
                            S.op(eng, lambda e: e.tensor_tensor(out=nxt[:, 0:nl], in0=cur[:, 0:nl], in1=cur[:, step:step + nl],
                                                                op=ALU.add), reads=[bcur], writes=[bnxt])
                            cur, bcur, clen = nxt, bnxt, nl
                        off = 8 - w // 2
                        oth, both = pa[(g + 1) % 2]
                        S.op("dve", lambda e: e.tensor_tensor(out=oth[:, 0:n_], in0=cur[:, off:off + n_], in1=itab[:, g, :],
                                                              op=ALU.mult), reads=[bcur, bit], writes=[both])
                        S.op("pool", lambda e: e.tensor_tensor(out=mT[:, k, col0:col0 + n_], in0=oth[:, 0:n_],
                                                               in1=src[:, 8:8 + n_], op=ALU.subtract),
                             reads=[both, bsrc], writes=[bmT])
                self.end_phase()
            with ExitStack() as ph:
                sc = self.scratch(ph)
                wst = self.wstream_bufs(ph, small=True)
                yT, byT = self.sb(ph, "yTp", [P, KC, NT], F32)
                gv = mods["gv"]
                tl = [(0, 512), (512, 512), (LH, NCTX)]
                for g in range(4):
                    def ev(m, tag, ps, pb):
                        c0, T = tl[tag]
                        ch = g * 4 + m
                        S.op("act", lambda e: e.activation(out=yT[:, ch, c0:c0 + T], in_=ps, func=AF.Copy,
                                                           scale=gv[:, li, 4, ch:ch + 1]),
                             reads=[pb, mods["bgv"]], writes=[byT])
                    tiles = [(lambda k, c0=c0, T=T: mT[:, g * 4 + k, c0:c0 + T], bmT, T, ti) for ti, (c0, T) in enumerate(tl)]
                    self.wstream(wst, w_pool[g], 4, 512, tiles, ev, banks=[0, 1, 2, 3])
                self.postnorm(sc, yT[:, :, 0:512], byT, 512, mods["gate1"][:, 0, :], mods["buf"], io["x_own"], io["kown"],
                              io["dst_own"], io["kdst_own"], 0)
                self.postnorm(sc, yT[:, :, 512:1024], byT, 512, mods["gate1"][:, 0, :], mods["buf"], io["x_own"], io["kown"],
                              io["dst_own"], io["kdst_own"], 512)
                self.postnorm(sc, yT[:, :, LH:NT], byT, NCTX, mods["gate1"][:, 1, :], mods["buf"], io["xc"], io["kxc"],
                              io["dst_c"], io["kdst_c"], 0)
                self.end_phase()

    def layer(self, li, io, Wt, tabs, scr, mods, after_latent=None):
        kind = li % 3
        last = li == DEPTH - 1
        if kind == 0:
            self.attn_layer(li, li // 3, io, Wt["w_qkv"], Wt["w_o"], Wt["lamv"], Wt["gsub"], tabs, scr, mods, not last)
        elif kind == 1:
            self.fourier_layer(io, Wt["w_f"], tabs, scr, mods)
        else:
            self.pool_layer(li, io, Wt["w_pool"], tabs, scr, mods)
        tiles = [(io["dst_own"], io["kdst_own"], io["out_own"], io["kout_own"], 0, 512, 0),
                 (io["dst_own"], io["kdst_own"], io["out_own"], io["kout_own"], 512, 512, 0)]
        if not last:
            tiles.append((io["dst_c"], io["kdst_c"], io["out_c"], io["kout_c"], 0, NCTX, 1))
        self.mlp_phase(tiles, Wt["w1"], Wt["w2"], mods, after_latent)

def host_gvec(inputs):
    gv = np.zeros((P, DEPTH, 5, KC), np.float32)
    for i in range(DEPTH):
        vecs = [inputs["g_mix_pre"][i], inputs["g_mix_post"][i], inputs["g_mlp_pre"][i], inputs["g_mlp_post"][i],
                inputs["pool_scale"][0]]
        for v, vec in enumerate(vecs):
            gv[:, i, v, :] = np.asarray(vec, np.float32).reshape(KC, P).T
    return gv


def _bf16(a):
    return np.asarray(a, np.float32).astype(ml_dtypes.bfloat16)


_TAB_CACHE = {}


def host_tables():
    if _TAB_CACHE:
        return _TAB_CACHE
    T = _TAB_CACHE
    t = np.arange(L)
    row = (t // 64).astype(np.float32)
    col = (t % 64).astype(np.float32)
    inv = (1.0 / (np.float32(10000.0) ** (np.arange(16, dtype=np.float32) / np.float32(16)))).astype(np.float32)
    cos = np.zeros((P, L), np.float32)
    sin = np.zeros((P, L), np.float32)
    perm = np.zeros((P, P), np.float32)
    for p in range(P):
        d = p % 64
        axis = d // 32
        half = (d % 32) // 16
        f = d % 16
        ang = ((row if axis == 0 else col) * inv[f]).astype(np.float32)
        cos[p] = np.cos(ang)
        sin[p] = np.sin(ang) * (-1.0 if half == 0 else 1.0)
        partner = p + 16 if half == 0 else p - 16
        perm[partner, p] = 1.0
    T["rope_cos"], T["rope_sin"], T["perm"] = cos, sin, perm
    c = np.arange(512)
    angc = 2.0 * np.pi * ((c[:, None] * c[None, :]) % 512) / 512.0
    T["dftc"] = _bf16(np.stack([np.cos(angc), np.sin(angc)]))
    l = np.arange(L)
    angl = 2.0 * np.pi * ((l[:, None] * l[None, :]) % L) / float(L)
    T["dftl_full"] = np.stack([np.cos(angl), -np.sin(angl)]).astype(np.float32)
    x = np.arange(NCTX)
    angx = 2.0 * np.pi * ((x[:, None] * x[None, :]) % NCTX) / float(NCTX)
    T["dft256"] = _bf16(np.stack([np.cos(angx), -np.sin(angx)]))

    def invcnt(n, lo_t, num):
        out = np.zeros((4, num), np.float32)
        for g, w in enumerate((2, 4, 8, 16)):
            tt = np.arange(lo_t, lo_t + num)
            lo = np.clip(tt - w // 2, 0, n)
            hi = np.clip(tt - w // 2 + w, 0, n)
            out[g] = 1.0 / (hi - lo).astype(np.float32)
        return out
    T["pinv"] = [np.ascontiguousarray(np.broadcast_to(invcnt(L, hf * LH, LH)[None], (P, 4, LH))) for hf in range(2)]
    T["pinvc"] = np.ascontiguousarray(np.broadcast_to(invcnt(NCTX, 0, NCTX)[None], (P, 4, NCTX)))
    T["dftl"] = [_bf16(T["dftl_full"][:, :, hf * LH:(hf + 1) * LH]) for hf in range(2)]
    return T


def build_fused():
    nc = bass.Bass("TRN2", target_bir_lowering=False)
    di = lambda n, sh, dt=F32: nc.dram_tensor(n, list(sh), dt, kind="ExternalInput").ap()
    do = lambda n, sh, dt=F32: nc.dram_tensor(n, list(sh), dt, kind="ExternalOutput").ap()
    dn = lambda n, sh, dt=F32: nc.dram_tensor(n, list(sh), dt).ap()
    x_full0 = di("x_full", [2, D, LH])
    x_own0 = di("x_own", [D, LH])
    xc0 = di("xc", [D, NCTX])
    cT = di("cT", [P, KC, 2])
    w_mod_h = di("w_mod_h", [DEPTH, D, MSH])
    b_mod_h = di("b_mod_h", [DEPTH * MSH])
    sel = di("sel", [2, 2])
    gvec = di("gvec", [P, DEPTH, 5, KC])
    w1 = di("w_mlp_in", [DEPTH, D, DFF])
    w2 = di("w_mlp_out", [DEPTH, DFF, D])
    w_qkv = di("w_qkv", [2, D, 3 * D])
    w_o = di("w_attn_out", [2, D, D])
    lamv = di("lamv", [2, 4, 64])
    gsub = di("gsub", [2, P])
    w_f = di("w_f", [D, D])
    w_pool = di("w_pool", [4, 512, 512])
    tabs = {"perm": di("perm", [P, P]), "ropek_cos": di("ropek_cos", [P, L]), "ropek_sin": di("ropek_sin", [P, L]),
            "ropeq_cos": di("ropeq_cos", [P, LH]), "ropeq_sin": di("ropeq_sin", [P, LH]),
            "dftc": di("dftc", [2, 512, 512], BF16), "dftl": di("dftl", [2, L, LH], BF16),
            "dft256": di("dft256", [2, NCTX, NCTX], BF16),
            "pinv": di("pinv", [P, 4, LH]), "pinvc": di("pinvc", [P, 4, NCTX]), "pmsk": di("pmsk", [P, 2])}
    out = do("out", [D, LH])
    modh = dn("modh", [2, DEPTH * MSH])
    modall = dn("modall", [2, 2, DEPTH * MSH])
    scr = {"kT": dn("kT_s", [NH, P, NKEY], BF16), "v": dn("v_s", [NKEY, D], BF16),
           "qT": dn("qT_s", [NH, P, LH + NCTX], BF16), "fT": dn("fT_s", [D, LH + NCTX], BF16)}
    NCH = 8
    RPC = D // NCH
    with ExitStack() as st:
        kb = KB(nc, st)
        S = kb.S
        m = kb.alloc_mods()
        kb.load_mod_consts(m, gvec, sel)
        kb.mod_stage(cT, w_mod_h, b_mod_h, modh)
        S.coll([lambda e: e.collective_compute("AllGather", ALU.bypass, replica_groups=PAIRS,
                                               ins=[modh], outs=[modall.rearrange("r a c -> (r a) c")])],
               reads=[kb.rb("modsh")], writes=[kb.rb("modall")])
        x_own, kown = x_own0, "x_own_in"
        xc, kxc = xc0, "xc_in"
        xf = [(lambda k, c0, c1, h=h: x_full0[h, k * P:(k + 1) * P, c0:c1]) for h in range(2)]
        kfull = "in"
        for li in range(DEPTH):
            last = li == DEPTH - 1
            kind = li % 3
            io = {"x_full": xf, "kfull": kfull, "x_own": x_own, "kown": kown, "xc": xc, "kxc": kxc,
                  "dst_own": dn("xmid_own%d" % li, [D, LH]), "kdst_own": "xmid_own%d" % li,
                  "dst_c": dn("xmid_c%d" % li, [D, NCTX]), "kdst_c": "xmid_c%d" % li}
            if last:
                io["out_own"], io["kout_own"] = out, "out"
            else:
                io["out_own"], io["kout_own"] = dn("xo_own%d" % li, [D, LH]), "xo_own%d" % li
                io["out_c"], io["kout_c"] = dn("xo_c%d" % li, [D, NCTX]), "xo_c%d" % li
            Wt = {"w1": w1[li], "w2": w2[li]}
            if kind == 0:
                ai = li // 3
                Wt.update({"w_qkv": w_qkv[ai], "w_o": w_o[ai], "lamv": [lamv[ai, j] for j in range(4)], "gsub": gsub[ai]})
            elif kind == 1:
                Wt["w_f"] = w_f
            else:
                Wt["w_pool"] = [w_pool[g] for g in range(4)]
            kb.mod_prep(m, modall, "modall", li)
            nxt = {}
            if not last:
                xg = dn("xg%d" % li, [NCH, 2, RPC, LH])
                kf = "g%d" % li

                def exchange(io=io, xg=xg, kf=kf):
                    src = io["out_own"]
                    rds = [kb.rb(io["kout_own"], t, k) for t in range(2) for k in range(KC)]
                    wrs = [kb.rb(("xfull", kf, h), t, k) for h in range(2) for t in range(2) for k in range(KC)]
                    S.coll([(lambda e, j=j: e.collective_compute("AllGather", ALU.bypass, replica_groups=PAIRS,
                                                                 ins=[src[j * RPC:(j + 1) * RPC, :]],
                                                                 outs=[xg[j].rearrange("r a c -> (r a) c")]))
                            for j in range(NCH)], reads=rds, writes=wrs)
                kb.layer(li, io, Wt, tabs, scr, m)
                kfull = kf
                if (li + 1) % 3 == 2:
                    hsrc = dn("halo_src%d" % li, [D, 16])
                    hall = dn("halo_all%d" % li, [2, D, 16])
                    src = io["out_own"]
                    bh = Buf("halo%d" % li)
                    rds = [kb.rb(io["kout_own"], t, k) for t in range(2) for k in range(KC)]
                    S.dma("sp", bh, lambda e: [e.dma_start(out=hsrc[:, 0:8], in_=src[:, 0:8]),
                                               e.dma_start(out=hsrc[:, 8:16], in_=src[:, LH - 8:LH])],
                          reads=rds, writes=[kb.rb("halo_src", li)])
                    wrs = [kb.rb(("xfull", kf, h), t, k) for h in range(2) for t in range(2) for k in range(KC)]
                    S.coll([lambda e: e.collective_compute("AllGather", ALU.bypass, replica_groups=PAIRS, ins=[hsrc],
                                                           outs=[hall.rearrange("r d c -> (r d) c")])],
                           reads=[kb.rb("halo_src", li)], writes=wrs)
                    xf = [(lambda k, c0, c1, hall=hall: hall[0, k * P:(k + 1) * P, c0 - (LH - 16):c1 - (LH - 16)]),
                          (lambda k, c0, c1, hall=hall: hall[1, k * P:(k + 1) * P, c0:c1])]
                else:
                    exchange()
                    xf = [(lambda k, c0, c1, h=h, xg=xg: xg[(k * P) // RPC, h, (k * P) % RPC:(k * P) % RPC + P, c0:c1])
                          for h in range(2)]
                x_own, kown = io["out_own"], io["kout_own"]
                xc, kxc = io["out_c"], io["kout_c"]
            else:
                kb.layer(li, io, Wt, tabs, scr, m)
        S.barrier(["sp"])
    return nc


PAIRS = [[0, 1], [2, 3], [4, 5], [6, 7]]
_PROG_CACHE = {}


def kernel(**inputs):
    inputs = {k: np.asarray(v) for k, v in inputs.items()}
    tb = host_tables()
    gv = host_gvec(inputs)
    cores = list(range(NCORES))
    if "fused" not in _PROG_CACHE:
        _PROG_CACHE["fused"] = build_fused()
    x = inputs["x"]
    lamv = np.stack([np.stack([inputs["lambda_q1"][a], inputs["lambda_k1"][a], inputs["lambda_q2"][a],
                               inputs["lambda_k2"][a]]) for a in range(2)]).astype(np.float32)
    shared = {"sel": np.eye(2, dtype=np.float32), "gvec": gv,
              "w_mlp_in": inputs["w_mlp_in"], "w_mlp_out": inputs["w_mlp_out"], "w_qkv": inputs["w_qkv"],
              "w_attn_out": inputs["w_attn_out"], "lamv": lamv, "gsub": np.asarray(inputs["g_subln"], np.float32),
              "w_f": inputs["w_fourier_out"][0], "w_pool": inputs["w_pool"][0],
              "perm": tb["perm"], "ropek_cos": tb["rope_cos"], "ropek_sin": tb["rope_sin"],
              "dftc": tb["dftc"], "dft256": tb["dft256"], "pinvc": tb["pinvc"]}
    wmh = [np.ascontiguousarray(inputs["w_mod"][:, :, h * MSH:(h + 1) * MSH]) for h in range(2)]
    bmh = [np.ascontiguousarray(inputs["b_mod"][:, h * MSH:(h + 1) * MSH]).reshape(-1) for h in range(2)]
    maps = []
    for r in cores:
        b, hf = r // 2, r % 2
        xfull = np.ascontiguousarray(x[b].T.reshape(D, 2, LH).transpose(1, 0, 2))
        c2 = np.stack([inputs["c"][b], inputs["c_ctx"]]).astype(np.float32)
        msk = np.zeros((P, 2), np.float32)
        msk[:, 0] = 1.0 if hf == 1 else 0.0
        msk[:, 1] = 1.0 if hf == 0 else 0.0
        mp = dict(shared)
        mp.update({"x_full": xfull, "x_own": np.ascontiguousarray(xfull[hf]),
                   "xc": np.ascontiguousarray(inputs["ctx"][b].T),
                   "cT": np.ascontiguousarray(c2.T.reshape(KC, P, 2).transpose(1, 0, 2)),
                   "w_mod_h": wmh[hf], "b_mod_h": bmh[hf],
                   "ropeq_cos": np.ascontiguousarray(tb["rope_cos"][:, hf * LH:(hf + 1) * LH]),
                   "ropeq_sin": np.ascontiguousarray(tb["rope_sin"][:, hf * LH:(hf + 1) * LH]),
                   "dftl": tb["dftl"][hf], "pinv": tb["pinv"][hf], "pmsk": msk})
        maps.append(mp)
    res = run_bass_kernel_spmd(_PROG_CACHE["fused"], maps, core_ids=cores)
    out = np.stack([np.concatenate([res.results[2 * b]["out"], res.results[2 * b + 1]["out"]], axis=1).T
                    for b in range(4)])
    return np.ascontiguousarray(out.astype(np.float32))
```

```python
import math
from contextlib import ExitStack

import numpy as np
import ml_dtypes

import concourse.bass as bass
import concourse.mybir as mybir
from concourse.bass_utils import run_bass_kernel_spmd

F32 = mybir.dt.float32
BF16 = mybir.dt.bfloat16
AF = mybir.ActivationFunctionType
ALU = mybir.AluOpType

P = 128
D = 2048
KC = 16
L = 2048
LH = 1024
NCTX = 256
NKEY = L + NCTX
DFF = 8192
FC = DFF // P
NH = 16
DEPTH = 4
NORM_EPS = 1e-6
SUBLN_EPS = 1e-5
NCORES = 8
MODW = 6 * D
MSH = MODW // 2


class Buf:
    __slots__ = ("name", "last_w", "readers", "dsem", "dcnt")

    def __init__(self, name):
        self.name = name
        self.last_w = None
        self.readers = []
        self.dsem = None
        self.dcnt = 0


class Sched:
    ENG = ("pe", "act", "dve", "pool", "sp")

    def __init__(self, nc, stack):
        self.nc = nc
        self.stack = stack
        self.eng = {"pe": nc.tensor, "act": nc.scalar, "dve": nc.vector,
                    "pool": nc.gpsimd, "sp": nc.sync}
        self.esem = {e: stack.enter_context(nc.semaphore("es_" + e)) for e in self.ENG}
        self.ecnt = {e: 0 for e in self.ENG}
        self.known = {e: {} for e in self.ENG}
        self.pool = []
        self.live = []
        self.nsem = 0
        self.ccsem = None
        self.cccnt = 0

    def _need(self, e, ev):
        if ev is None:
            return
        sem, cnt = ev
        k = self.known[e]
        if k.get(id(sem), 0) >= cnt:
            return
        if e == "pe" and sem is self.esem["pe"]:
            return
        self.eng[e].wait_ge(sem, cnt)
        k[id(sem)] = cnt

    def _deps(self, e, reads, writes):
        for b in reads:
            self._need(e, b.last_w)
        for b in writes:
            self._need(e, b.last_w)
            for r in b.readers:
                self._need(e, r)

    @staticmethod
    def _commit(ev, reads, writes):
        for b in reads:
            b.readers.append(ev)
            if len(b.readers) > 64:
                b.readers = b.readers[-48:]
        for b in writes:
            b.last_w = ev
            b.readers = []

    def op(self, e, fn, reads=(), writes=()):
        self._deps(e, reads, writes)
        ins = fn(self.eng[e])
        self.ecnt[e] += 1
        ins.then_inc(self.esem[e], 1)
        ev = (self.esem[e], self.ecnt[e])
        self._commit(ev, reads, writes)
        return ev

    def dma(self, q, owner, fn, reads=(), writes=()):
        if owner.dsem is None:
            if self.pool:
                owner.dsem, owner.dcnt = self.pool.pop()
            else:
                owner.dsem = self.stack.enter_context(self.nc.semaphore("ds%d" % self.nsem))
                owner.dcnt = 0
                self.nsem += 1
            self.live.append(owner)
        self._deps(q, reads, writes)
        lst = fn(self.eng[q])
        for ins in lst:
            ins.then_inc(owner.dsem, 16)
            owner.dcnt += 16
        ev = (owner.dsem, owner.dcnt)
        self._commit(ev, reads, writes)
        return ev

    def coll(self, fns, reads, writes):
        if self.ccsem is None:
            self.ccsem = self.stack.enter_context(self.nc.semaphore("ccsem"))
        self._deps("pool", reads, writes)
        for fn in fns:
            ins = fn(self.eng["pool"])
            self.cccnt += 1
            ins.then_inc(self.ccsem, 1)
        ev = (self.ccsem, self.cccnt)
        self._commit(ev, reads, writes)
        return ev

    def barrier(self, engines=None):
        engines = engines or self.ENG
        for e in engines:
            for f in self.ENG:
                if f != e and self.ecnt[f] > 0:
                    self._need(e, (self.esem[f], self.ecnt[f]))
            for b in self.live:
                if b.dcnt > 0:
                    self._need(e, (b.dsem, b.dcnt))
            if self.cccnt > 0:
                self._need(e, (self.ccsem, self.cccnt))

    def release_dma_sems(self):
        for b in self.live:
            self.pool.append((b.dsem, b.dcnt))
            b.dsem = None
        self.live = []


class KB:
    def __init__(self, nc, st):
        self.nc = nc
        self.st = st
        self.S = Sched(nc, st)
        self.uid = 0
        self.ps = []
        self.pb = []
        self.psall = st.enter_context(nc.psum_tensor("psall", [P, 8, 512], F32))
        for i in range(8):
            self.ps.append(self.psall[:, i, :])
            self.pb.append(Buf("psb%d" % i))
        self.ones, self.b_ones = self.sb(st, "ones", [P, P], BF16)
        self.S.op("pool", lambda e: e.memset(self.ones[:], 1.0), writes=[self.b_ones])
        self.epsb, self.b_eps = self.sb(st, "epsb", [P, 2], F32)
        self.S.op("pool", lambda e: e.memset(self.epsb[:, 0:1], NORM_EPS), writes=[self.b_eps])
        self.S.op("pool", lambda e: e.memset(self.epsb[:, 1:2], SUBLN_EPS), writes=[self.b_eps])
        self.dr = {}
        self.wq = 0

    def sb(self, ph, name, shape, dt):
        self.uid += 1
        nm = "%s_%d" % (name, self.uid)
        t = ph.enter_context(self.nc.sbuf_tensor(nm, list(shape), dt))
        return t, Buf(nm)

    def rb(self, *key):
        b = self.dr.get(key)
        if b is None:
            b = Buf("dr_" + "_".join(str(k) for k in key))
            self.dr[key] = b
        return b

    def end_phase(self):
        self.S.barrier()
        self.S.release_dma_sems()

    def scratch(self, ph, nx=3):
        sc = {}
        sc["sq"] = [self.sb(ph, "sq", [P, 512], BF16) for _ in range(2)]
        sc["rt"] = self.sb(ph, "rt", [P, 512], F32)
        sc["rstd"] = [self.sb(ph, "rstd", [P, 512], F32) for _ in range(2)]
        sc["xk"] = [self.sb(ph, "xk", [P, 512], F32) for _ in range(nx)]
        sc["tmp"] = [self.sb(ph, "tmp", [P, 512], F32) for _ in range(2)]
        sc["ok"] = [self.sb(ph, "ok", [P, 512], F32) for _ in range(2)]
        sc["i"] = 0
        return sc

    def stats(self, sc, chunks, T, n, eps, bank, slot):
        S = self.S
        ps, pb = self.ps[bank], self.pb[bank]
        nk = len(chunks)
        for k, (ap, buf) in enumerate(chunks):
            sq, bsq = sc["sq"][k % 2]
            S.op("act", lambda e: e.activation(out=sq[:, :T], in_=ap, func=AF.Square),
                 reads=[buf], writes=[bsq])
            S.op("pe", lambda e: e.matmul(ps[:, :T], self.ones[:], sq[:, :T],
                                          start=(k == 0), stop=(k == nk - 1)),
                 reads=[bsq, self.b_ones], writes=[pb])
        rt, brt = sc["rt"]
        rstd, brstd = sc["rstd"][slot]
        S.op("act", lambda e: e.activation(out=rt[:, :T], in_=ps[:, :T], func=AF.Ln,
                                           bias=self.epsb[:, 0:1] if eps == NORM_EPS else self.epsb[:, 1:2], scale=1.0 / n),
             reads=[pb, self.b_eps], writes=[brt])
        S.op("act", lambda e: e.activation(out=rstd[:, :T], in_=rt[:, :T], func=AF.Exp, scale=-0.5),
             reads=[brt], writes=[brstd])
        return rstd, brstd

    def prenorm(self, sc, src, skey, t0, T, gsc, sh, bmod, uT, buT, ucol0, bank=7):
        S = self.S
        ps, pb = self.ps[bank], self.pb[bank]
        nx = len(sc["xk"])
        if not callable(src):
            src_ap = src
            src = lambda k, c0, c1: src_ap[k * P:(k + 1) * P, c0:c1]
        for k in range(KC):
            xk, bxk = sc["xk"][sc["i"] % nx]
            sc["i"] += 1
            S.dma("sp", bxk, lambda e: [e.dma_start(out=xk[:, :T], in_=src(k, t0, t0 + T))],
                  reads=[self.rb(skey, t0 // 512, k)], writes=[bxk])
            sq, bsq = sc["sq"][k % 2]
            S.op("act", lambda e: e.activation(out=sq[:, :T], in_=xk[:, :T], func=AF.Square),
                 reads=[bxk], writes=[bsq])
            S.op("pe", lambda e: e.matmul(ps[:, :T], self.ones[:], sq[:, :T],
                                          start=(k == 0), stop=(k == KC - 1)),
                 reads=[bsq, self.b_ones], writes=[pb])
        rt, brt = sc["rt"]
        rstd, brstd = sc["rstd"][0]
        S.op("act", lambda e: e.activation(out=rt[:, :T], in_=ps[:, :T], func=AF.Ln,
                                           bias=self.epsb[:, 0:1], scale=1.0 / D),
             reads=[pb, self.b_eps], writes=[brt])
        S.op("act", lambda e: e.activation(out=rstd[:, :T], in_=rt[:, :T], func=AF.Exp, scale=-0.5),
             reads=[brt], writes=[brstd])
        for k in range(KC):
            xk, bxk = sc["xk"][sc["i"] % nx]
            sc["i"] += 1
            S.dma("sp", bxk, lambda e: [e.dma_start(out=xk[:, :T], in_=src(k, t0, t0 + T))],
                  reads=[self.rb(skey, t0 // 512, k)], writes=[bxk])
            tmp, btmp = sc["tmp"][k % 2]
            S.op("dve", lambda e: e.scalar_tensor_tensor(out=tmp[:, :T], in0=xk[:, :T], scalar=gsc[:, k:k + 1],
                                                         in1=rstd[:, :T], op0=ALU.mult, op1=ALU.mult),
                 reads=[bxk, brstd, bmod], writes=[btmp])
            S.op("act", lambda e: e.activation(out=uT[:, k, ucol0:ucol0 + T], in_=tmp[:, :T], func=AF.Identity,
                                               bias=sh[:, k:k + 1], scale=1.0),
                 reads=[btmp, bmod], writes=[buT])

    def postnorm(self, sc, yT, byT, T, gate, bmod, src, skey, dst, dkey, t0, bank=7, store_q="sp"):
        S = self.S
        chunks = [(yT[:, k, :T], byT) for k in range(KC)]
        rstd, brstd = self.stats(sc, chunks, T, D, NORM_EPS, bank, 1)
        nx = len(sc["xk"])
        base = sc["i"]
        sc["i"] += KC
        LA = max(0, min(2, nx - 1))

        def ld(k):
            xk, bxk = sc["xk"][(base + k) % nx]
            S.dma("sp", bxk, lambda e: [e.dma_start(out=xk[:, :T], in_=src[k * P:(k + 1) * P, t0:t0 + T])],
                  reads=[self.rb(skey, t0 // 512, k)], writes=[bxk])

        for k in range(LA):
            ld(k)
        for k in range(KC):
            if k + LA < KC:
                ld(k + LA)
            xk, bxk = sc["xk"][(base + k) % nx]
            tmp, btmp = sc["tmp"][k % 2]
            S.op("dve", lambda e: e.scalar_tensor_tensor(out=tmp[:, :T], in0=yT[:, k, :T], scalar=gate[:, k:k + 1],
                                                         in1=rstd[:, :T], op0=ALU.mult, op1=ALU.mult),
                 reads=[byT, brstd, bmod], writes=[btmp])
            ok, bok = sc["ok"][k % 2]
            S.op("pool", lambda e: e.tensor_tensor(out=ok[:, :T], in0=tmp[:, :T], in1=xk[:, :T], op=ALU.add),
                 reads=[btmp, bxk], writes=[bok])
            S.dma(store_q, bok, lambda e: [e.dma_start(out=dst[k * P:(k + 1) * P, t0:t0 + T], in_=ok[:, :T])],
                  reads=[bok], writes=[self.rb(dkey, t0 // 512, k)])

    def wstream(self, wst, W, kcn, ncols, tiles, evac, banks, cb=None, act_stationary=False,
                prefetch_only=False, wkey=None, before_last=None, wcache=None):
        S = self.S
        if cb is None:
            cb = 8192 // kcn
        cb = min(cb, ncols)
        npiece = max(1, (kcn * cb) // 4096)
        blocks = list(range(0, ncols, cb))
        loaded = {}
        pre = wst.setdefault("pre", {})
        if prefetch_only:
            blocks = blocks[:1]

        def load_block(c0):
            wi = self.wq
            self.wq += 1
            wb, bwb = wst["wb"][wi % 2]
            wv = wb[:, 0:kcn * cb].rearrange("p (k m) -> p k m", k=kcn)
            if wcache is not None and wcache[2] == "r":
                blk = wcache[1] + c0 // cb
                S.dma("sp", bwb, lambda e: [e.dma_start(out=wb[:, 0:kcn * cb], in_=wcache[0][blk, :, 0:kcn * cb])],
                      reads=[self.rb("wc", blk)], writes=[bwb])
                loaded[c0] = (wv, bwb)
                return
            for pc in range(npiece):
                si = wst["si"]
                wst["si"] += 1
                stg, bstg = wst["stg"][si % len(wst["stg"])]
                if kcn * cb <= 4096:
                    kk0, kk1, cc0, cc1 = 0, kcn, 0, cb
                else:
                    h = kcn // npiece
                    kk0, kk1, cc0, cc1 = pc * h, (pc + 1) * h, 0, cb
                nk, ncl = kk1 - kk0, cc1 - cc0
                sv = stg[:, 0:nk * ncl].rearrange("p (k m) -> p k m", k=nk)
                src = W[kk0 * P:kk1 * P, c0 + cc0:c0 + cc1].rearrange("(k p) m -> p k m", p=P)
                S.dma("sp", bstg, lambda e: [e.dma_start(out=sv, in_=src)], writes=[bstg])
                if pc == npiece - 1:
                    self.flush_wstore(wst)
                ce = ("dve", "act")[si % 2]
                if ce == "dve":
                    S.op("dve", lambda e: e.tensor_copy(out=wv[:, kk0:kk1, cc0:cc1], in_=sv),
                         reads=[bstg], writes=[bwb])
                else:
                    S.op("act", lambda e: e.activation(out=wv[:, kk0:kk1, cc0:cc1], in_=sv, func=AF.Copy),
                         reads=[bstg], writes=[bwb])
            if wcache is not None and wcache[2] == "w":
                blk = wcache[1] + c0 // cb
                wst["pstore"] = (wb, bwb, wcache[0], blk, kcn * cb)
            loaded[c0] = (wv, bwb)

        def compute_block(c0):
            wv, bwb = loaded.pop(c0)
            if act_stationary:
                for (lhs_fn, rbuf, T, tag) in tiles:
                    bi = banks[wst["bi"] % len(banks)]
                    wst["bi"] += 1
                    ps, pb = self.ps[bi], self.pb[bi]
                    S._deps("pe", [bwb, rbuf], [pb])
                    for k in range(kcn - 1):
                        self.nc.tensor.matmul(ps[:, :cb], lhs_fn(k), wv[:, k, 0:cb], start=(k == 0), stop=False)
                    S.op("pe", lambda e: e.matmul(ps[:, :cb], lhs_fn(kcn - 1), wv[:, kcn - 1, 0:cb],
                                                  start=(kcn == 1), stop=True),
                         reads=[bwb, rbuf], writes=[pb])
                    evac(c0 // cb, tag, ps[:, :cb], pb)
                return
            for m in range(cb // P):
                for (rhs_fn, rbuf, T, tag) in tiles:
                    bi = banks[wst["bi"] % len(banks)]
                    wst["bi"] += 1
                    ps, pb = self.ps[bi], self.pb[bi]
                    S._deps("pe", [bwb, rbuf], [pb])
                    for k in range(kcn - 1):
                        self.nc.tensor.matmul(ps[:, :T], wv[:, k, m * P:(m + 1) * P], rhs_fn(k),
                                              start=(k == 0), stop=False)
                    S.op("pe", lambda e: e.matmul(ps[:, :T], wv[:, kcn - 1, m * P:(m + 1) * P], rhs_fn(kcn - 1),
                                                  start=(kcn == 1), stop=True),
                         reads=[bwb, rbuf], writes=[pb])
                    evac((c0 // P) + m, tag, ps[:, :T], pb)

        if prefetch_only:
            load_block(blocks[0])
            pre[wkey] = loaded.pop(blocks[0])
            return
        if wkey is not None and wkey in pre:
            loaded[blocks[0]] = pre.pop(wkey)
        else:
            load_block(blocks[0])
        for i, c0 in enumerate(blocks):
            if i + 1 < len(blocks):
                load_block(blocks[i + 1])
            elif before_last is not None:
                before_last()
            compute_block(c0)

    def flush_wstore(self, wst):
        ps_ = wst.pop("pstore", None)
        if ps_ is None:
            return
        wb, bwb, cache, blk, n = ps_
        self.S.dma("sp", bwb, lambda e: [e.dma_start(out=cache[blk, :, 0:n], in_=wb[:, 0:n])], reads=[bwb],
                   writes=[self.rb("wc", blk)])

    def wstream_kslab(self, wst, W, kct, ncols, rhs_fn, rbuf, T, evac, wkey=None, before_last=None,
                      prefetch_only=False, wcache=None):
        S = self.S
        SL = 16
        nsl = kct // SL
        pre = wst.setdefault("pre", {})
        blocks = [(c0, sl) for c0 in range(0, ncols, 512) for sl in range(nsl)]

        def load(bk):
            c0, sl = bk
            wi = self.wq
            self.wq += 1
            wb, bwb = wst["wb"][wi % 2]
            wv = wb[:, 0:SL * 512].rearrange("p (k m) -> p k m", k=SL)
            blk = None
            if wcache is not None:
                blk = wcache[1] + (c0 // 512) * nsl + sl
                if wcache[2] == "r":
                    S.dma("sp", bwb, lambda e: [e.dma_start(out=wb[:, 0:SL * 512], in_=wcache[0][blk, :, 0:SL * 512])],
                          reads=[self.rb("wc", blk)], writes=[bwb])
                    return (wv, bwb)
            for pc in range(2):
                si = wst["si"]
                wst["si"] += 1
                stg, bstg = wst["stg"][si % len(wst["stg"])]
                sv = stg[:, 0:8 * 512].rearrange("p (k m) -> p k m", k=8)
                r0 = (sl * SL + pc * 8) * P
                src = W[r0:r0 + 8 * P, c0:c0 + 512].rearrange("(k p) m -> p k m", p=P)
                S.dma("sp", bstg, lambda e: [e.dma_start(out=sv, in_=src)], writes=[bstg])
                if pc == 1:
                    self.flush_wstore(wst)
                if si % 2 == 0:
                    S.op("dve", lambda e: e.tensor_copy(out=wv[:, pc * 8:(pc + 1) * 8, :], in_=sv),
                         reads=[bstg], writes=[bwb])
                else:
                    S.op("act", lambda e: e.activation(out=wv[:, pc * 8:(pc + 1) * 8, :], in_=sv, func=AF.Copy),
                         reads=[bstg], writes=[bwb])
            if wcache is not None and wcache[2] == "w":
                wst["pstore"] = (wb, bwb, wcache[0], blk, SL * 512)
            return (wv, bwb)

        if prefetch_only:
            pre[wkey] = load(blocks[0])
            return
        loaded = {}
        if wkey is not None and wkey in pre:
            loaded[blocks[0]] = pre.pop(wkey)
        else:
            loaded[blocks[0]] = load(blocks[0])
        for i, bk in enumerate(blocks):
            if i + 1 < len(blocks):
                loaded[blocks[i + 1]] = load(blocks[i + 1])
            elif before_last is not None:
                before_last()
            c0, sl = bk
            wv, bwb = loaded.pop(bk)
            g = c0 // 512
            banks = [0, 1, 2, 3] if g % 2 == 0 else [4, 5, 6, 7]
            pbs = [self.pb[b_] for b_ in banks]
            S._deps("pe", [bwb, rbuf], pbs)
            for m in range(4):
                ps = self.ps[banks[m]]
                for k in range(SL):
                    first = (sl == 0 and k == 0)
                    last = (sl == nsl - 1 and k == SL - 1)
                    if m == 3 and k == SL - 1:
                        S.op("pe", lambda e: e.matmul(ps[:, :T], wv[:, k, m * P:(m + 1) * P], rhs_fn(sl * SL + k),
                                                      start=first, stop=last), reads=[bwb, rbuf], writes=pbs)
                    else:
                        self.nc.tensor.matmul(ps[:, :T], wv[:, k, m * P:(m + 1) * P], rhs_fn(sl * SL + k),
                                              start=first, stop=last)
            if sl == nsl - 1:
                for m in range(4):
                    evac(c0 // P + m, 0, self.ps[banks[m]][:, :T], self.pb[banks[m]])

    def wstream_bufs(self, ph, nstg=2, small=False):
        n1, n2 = (2048, 2048) if small else (4096, 8192)
        return {"stg": [self.sb(ph, "stg", [P, n1], F32) for _ in range(nstg)],
                "wb": [self.sb(ph, "wb", [P, n2], BF16) for _ in range(2)],
                "si": 0, "bi": 0}

    def mlp_phase(self, tiles, w1, w2, mods, after_latent=None, wc=None):
        S = self.S
        with ExitStack() as ph:
            sc = self.scratch(ph)
            wst = self.wstream_bufs(ph)
            uT, buT = self.sb(ph, "uT", [P, KC, 512], BF16)
            hT, bhT = self.sb(ph, "hT", [P, FC, 512], BF16)
            yT, byT = self.sb(ph, "yT", [P, KC, 512], F32)
            rr = [self.sb(ph, "rr", [P, 512], F32) for _ in range(2)]
            cnt = {"r": 0}
            ti = 0
            ntl = len(tiles)
            wcm = lambda t_, base: None if wc is None else (wc, base, "w" if t_ == 0 else "r")
            self.wstream(wst, w1, KC, DFF, None, None, None, prefetch_only=True, wkey=("w1", 0), wcache=wcm(0, 0))
            for tix, (src, skey, dst, dkey, t0, T, which) in enumerate(tiles):
                self.prenorm(sc, src, skey, t0, T, mods["gsc2"][:, which, :], mods["sh2"][:, which, :],
                             mods["buf"], uT, buT, 0)

                def ev1(m, tag, ps, pb):
                    r, br = rr[cnt["r"] % 2]
                    cnt["r"] += 1
                    S.op("act", lambda e: e.activation(out=r[:, :T], in_=ps, func=AF.Relu),
                         reads=[pb], writes=[br])
                    S.op("pool", lambda e: e.tensor_tensor(out=hT[:, m, :T], in0=r[:, :T], in1=r[:, :T], op=ALU.mult),
                         reads=[br], writes=[bhT])

                self.wstream(wst, w1, KC, DFF, [(lambda k: uT[:, k, :T], buT, T, 0)], ev1, banks=[0, 1, 2, 3],
                             wkey=("w1", tix), wcache=wcm(tix, 0),
                             before_last=lambda: self.wstream_kslab(wst, w2, FC, D, None, None, None, None,
                                                                    prefetch_only=True, wkey=("w2", tix),
                                                                    wcache=wcm(tix, 16)))

                def ev2(m, tag, ps, pb):
                    S.op("act", lambda e: e.activation(out=yT[:, m, :T], in_=ps, func=AF.Copy),
                         reads=[pb], writes=[byT])

                nxt = None
                if tix + 1 < ntl:
                    nxt = lambda: self.wstream(wst, w1, KC, DFF, None, None, None, prefetch_only=True,
                                               wkey=("w1", tix + 1), wcache=wcm(tix + 1, 0))
                self.wstream_kslab(wst, w2, FC, D, (lambda k: hT[:, k, :T]), bhT, T, ev2,
                                   wkey=("w2", tix), before_last=nxt, wcache=wcm(tix, 16))
                self.flush_wstore(wst)
                self.postnorm(sc, yT, byT, T, mods["gate2"][:, which, :], mods["buf"], src, skey, dst, dkey, t0)
                ti += 1
                if ti == 2 and after_latent is not None:
                    after_latent()
            self.end_phase()

    def mod_stage(self, cT, w_mod_sh, b_mod_sh, modsh):
        S = self.S
        with ExitStack() as ph:
            ct, bct = self.sb(ph, "ct", [P, KC, 2], F32)
            st_, bst = self.sb(ph, "sT", [P, KC, 2], F32)
            bbs = [self.sb(ph, "bb", [2, MSH], F32) for _ in range(2)]
            mos = [self.sb(ph, "mo", [2, MSH], F32) for _ in range(2)]
            wf = [self.sb(ph, "wf", [P, KC, 512], F32) for _ in range(3)]
            S.dma("sp", bct, lambda e: [e.dma_start(out=ct[:], in_=cT)], writes=[bct])
            S.op("act", lambda e: e.activation(out=st_[:], in_=ct[:], func=AF.Silu), reads=[bct], writes=[bst])
            n = 0
            for i in range(DEPTH):
                bb, bbb = bbs[i % 2]
                mo, bmo = mos[i % 2]
                S.dma("sp", bbb, lambda e: [e.dma_start(out=bb[:], in_=b_mod_sh[i * MSH:(i + 1) * MSH].partition_broadcast(2))],
                      writes=[bbb])
                for j in range(MSH // 512):
                    w, bw = wf[n % 3]
                    src = w_mod_sh[i, :, j * 512:(j + 1) * 512].rearrange("(k p) m -> p k m", p=P)
                    S.dma("sp", bw, lambda e: [e.dma_start(out=w[:], in_=src)], writes=[bw])
                    ps, pb = self.ps[n % 2], self.pb[n % 2]
                    S._deps("pe", [bst, bw], [pb])
                    for k in range(KC - 1):
                        self.nc.tensor.matmul(ps[0:2, :], st_[:, k, :], w[:, k, :], start=(k == 0), stop=False)
                    S.op("pe", lambda e: e.matmul(ps[0:2, :], st_[:, KC - 1, :], w[:, KC - 1, :], start=False, stop=True),
                         reads=[bst, bw], writes=[pb])
                    c0 = j * 512
                    S.op("dve", lambda e: e.tensor_tensor(out=mo[:, c0:c0 + 512], in0=ps[0:2, :],
                                                          in1=bb[:, c0:c0 + 512], op=ALU.add),
                         reads=[pb, bbb], writes=[bmo])
                    n += 1
                S.dma("sp", bmo, lambda e: [e.dma_start(out=modsh[:, i * MSH:(i + 1) * MSH], in_=mo[:])], reads=[bmo],
                      writes=[self.rb("modsh")])
            self.end_phase()

    def alloc_mods(self):
        st = self.st
        m = {}
        m["fm"], m["bfm"] = self.sb(st, "modfm", [P, 6, KC, 2], F32)
        m["gv"], m["bgv"] = self.sb(st, "gvec", [P, DEPTH, 5, KC], F32)
        m["sel"], m["bsel"] = self.sb(st, "sel", [2, 2], F32)
        for nm in ("gsc1", "sh1", "gate1", "gsc2", "sh2", "gate2"):
            m[nm], _ = self.sb(st, nm, [P, 2, KC], F32)
        m["buf"] = Buf("mods")
        return m

    def load_mod_consts(self, m, gvec, sel):
        S = self.S
        S.dma("sp", m["bgv"], lambda e: [e.dma_start(out=m["gv"][:], in_=gvec)], writes=[m["bgv"]])
        S.dma("sp", m["bsel"], lambda e: [e.dma_start(out=m["sel"][:], in_=sel)], writes=[m["bsel"]])

    def mod_prep(self, m, modall, mkey, li):
        S = self.S
        with ExitStack() as ph:
            mr, bmr = self.sb(ph, "mrow", [2, MODW], F32)
            S.dma("sp", bmr, lambda e: [e.dma_start(out=mr[:, h * MSH:(h + 1) * MSH],
                                                    in_=modall[h, :, li * MSH:(li + 1) * MSH]) for h in range(2)],
                  reads=[self.rb(mkey)], writes=[bmr])
            ps, pb = self.ps[0], self.pb[0]
            S._deps("pe", [bmr, m["bsel"]], [pb])
            for j in range(6):
                for k in range(KC):
                    c = j * 32 + k * 2
                    ins_fn = lambda e: e.matmul(ps[:, c:c + 2], mr[0:2, j * D + k * P: j * D + (k + 1) * P],
                                                m["sel"][0:2, 0:2], start=True, stop=True)
                    if j == 5 and k == KC - 1:
                        S.op("pe", ins_fn, reads=[bmr, m["bsel"]], writes=[pb])
                    else:
                        ins_fn(self.nc.tensor)
            fm = m["fm"]
            S.op("dve", lambda e: e.tensor_copy(out=fm[:].rearrange("p a k w -> p (a k w)"), in_=ps[:, 0:192]),
                 reads=[pb], writes=[m["bfm"]])
            gv = m["gv"]
            rd = [m["bfm"], m["bgv"]]
            wr = [m["buf"]]
            for w in range(2):
                S.op("dve", lambda e: e.scalar_tensor_tensor(out=m["gsc1"][:, w, :], in0=fm[:, 1, :, w], scalar=1.0,
                                                             in1=gv[:, li, 0, :], op0=ALU.add, op1=ALU.mult),
                     reads=rd, writes=wr)
                S.op("dve", lambda e: e.tensor_copy(out=m["sh1"][:, w, :], in_=fm[:, 0, :, w]), reads=rd, writes=wr)
                S.op("dve", lambda e: e.tensor_tensor(out=m["gate1"][:, w, :], in0=fm[:, 2, :, w],
                                                      in1=gv[:, li, 1, :], op=ALU.mult), reads=rd, writes=wr)
                S.op("dve", lambda e: e.scalar_tensor_tensor(out=m["gsc2"][:, w, :], in0=fm[:, 4, :, w], scalar=1.0,
                                                             in1=gv[:, li, 2, :], op0=ALU.add, op1=ALU.mult),
                     reads=rd, writes=wr)
                S.op("dve", lambda e: e.tensor_copy(out=m["sh2"][:, w, :], in_=fm[:, 3, :, w]), reads=rd, writes=wr)
                S.op("dve", lambda e: e.tensor_tensor(out=m["gate2"][:, w, :], in0=fm[:, 5, :, w],
                                                      in1=gv[:, li, 3, :], op=ALU.mult), reads=rd, writes=wr)
            self.end_phase()


    def proj_post(self, opT, bopT, W, tiles, gate_name, mods):
        S = self.S
        with ExitStack() as ph:
            sc = self.scratch(ph)
            wst = self.wstream_bufs(ph)
            yT, byT = self.sb(ph, "yT", [P, KC, 512], F32)
            for (col0, T, which, src, skey, dst, dkey, t0) in tiles:
                def ev(m, tag, ps, pb):
                    S.op("act", lambda e: e.activation(out=yT[:, m, :T], in_=ps, func=AF.Copy),
                         reads=[pb], writes=[byT])
                self.wstream(wst, W, KC, D, [(lambda k: opT[:, k, col0:col0 + T], bopT, T, 0)], ev,
                             banks=[0, 1, 2, 3])
                self.postnorm(sc, yT, byT, T, mods[gate_name][:, which, :], mods["buf"], src, skey, dst, dkey, t0)
            self.end_phase()

    def rope_evac(self, rp, ps, pb, T, cos_ap, sin_ap, btab, out_ap, bout):
        S = self.S
        i = rp["i"]
        rp["i"] += 1
        qf, bqf = rp["qf"][i % 2]
        t1, bt1 = rp["t1"][i % 2]
        t2, bt2 = rp["t2"][i % 2]
        bank = rp["banks"][i % len(rp["banks"])]
        ps2, pb2 = self.ps[bank], self.pb[bank]
        S.op("act", lambda e: e.activation(out=qf[:, :T], in_=ps, func=AF.Copy), reads=[pb], writes=[bqf])
        S.op("pe", lambda e: e.matmul(ps2[:, :T], rp["perm"][:], qf[:, :T], start=True, stop=True),
             reads=[bqf, rp["bperm"]], writes=[pb2])
        S.op("pool", lambda e: e.tensor_tensor(out=t1[:, :T], in0=qf[:, :T], in1=cos_ap, op=ALU.mult),
             reads=[bqf, btab], writes=[bt1])
        S.op("dve", lambda e: e.tensor_tensor(out=t2[:, :T], in0=ps2[:, :T], in1=sin_ap, op=ALU.mult),
             reads=[pb2, btab], writes=[bt2])
        S.op("dve", lambda e: e.tensor_tensor(out=out_ap, in0=t1[:, :T], in1=t2[:, :T], op=ALU.add),
             reads=[bt1, bt2], writes=[bout])

    def attn_layer(self, li, ai, io, w_qkv, w_o, lamv, gsub, tabs, scr, mods, with_ctx):
        S = self.S
        nc = self.nc
        lambda_init = 0.8 - 0.6 * math.exp(-0.3 * li)
        NQ = LH + (NCTX if with_ctx else 0)
        kT_s, v_s, qT_s = scr["kT"], scr["v"], scr["qT"]
        with ExitStack() as lay:
            cst, bcst = self.sb(lay, "acst", [P, 8], F32)
            with ExitStack() as ph:
                lv, blv = self.sb(ph, "lv", [P, 4, 64], F32)
                gs_, bgs = self.sb(ph, "gsl", [P, 1], F32)
                pr, bpr = self.sb(ph, "pr", [P, 2, 64], F32)
                S.dma("sp", blv, lambda e: [e.dma_start(out=lv[:, j, :], in_=lamv[j].partition_broadcast(P))
                                            for j in range(4)], writes=[blv])
                S.dma("sp", bgs, lambda e: [e.dma_start(out=gs_[:], in_=gsub.rearrange("(p o) -> p o", o=1))],
                      writes=[bgs])
                S.op("dve", lambda e: e.tensor_tensor(out=pr[:, 0, :], in0=lv[:, 0, :], in1=lv[:, 1, :], op=ALU.mult),
                     reads=[blv], writes=[bpr])
                S.op("dve", lambda e: e.tensor_tensor(out=pr[:, 1, :], in0=lv[:, 2, :], in1=lv[:, 3, :], op=ALU.mult),
                     reads=[blv], writes=[bpr])
                S.op("dve", lambda e: e.reduce_sum(out=cst[:, 0:1], in_=pr[:, 0, :], axis=mybir.AxisListType.X),
                     reads=[bpr], writes=[bcst])
                S.op("dve", lambda e: e.reduce_sum(out=cst[:, 1:2], in_=pr[:, 1, :], axis=mybir.AxisListType.X),
                     reads=[bpr], writes=[bcst])
                S.op("act", lambda e: e.activation(out=cst[:, 2:4], in_=cst[:, 0:2], func=AF.Exp),
                     reads=[bcst], writes=[bcst])
                S.op("dve", lambda e: e.tensor_tensor(out=cst[:, 4:5], in0=cst[:, 3:4], in1=cst[:, 2:3], op=ALU.subtract),
                     reads=[bcst], writes=[bcst])
                S.op("dve", lambda e: e.tensor_scalar(out=cst[:, 5:6], in0=cst[:, 4:5], scalar1=-float(lambda_init),
                                                      scalar2=None, op0=ALU.add), reads=[bcst], writes=[bcst])
                S.op("dve", lambda e: e.tensor_scalar(out=cst[:, 6:7], in0=gs_[:, 0:1], scalar1=float(1.0 - lambda_init),
                                                      scalar2=None, op0=ALU.mult), reads=[bgs, bcst], writes=[bcst])
                self.end_phase()
            neglam = cst[:, 5:6]
            gsl = cst[:, 6:7]

            with ExitStack() as ph:
                sc = self.scratch(ph)
                wst = self.wstream_bufs(ph)
                uT, buT = self.sb(ph, "uTall", [P, KC, NKEY], BF16)
                rp = self.rope_bufs(ph, tabs)
                ck, bck = self.sb(ph, "ropek", [P, 2, L], F32)
                S.dma("sp", bck, lambda e: [e.dma_start(out=ck[:, 0, :], in_=tabs["ropek_cos"]),
                                            e.dma_start(out=ck[:, 1, :], in_=tabs["ropek_sin"])], writes=[bck])
                kh = [self.sb(ph, "kh", [P, NKEY], BF16) for _ in range(2)]
                vo = [self.sb(ph, "vo", [P, 512], BF16) for _ in range(2)]
                ktiles = []
                for t in range(4):
                    self.prenorm(sc, io["x_full"][t // 2], ("xfull", io["kfull"], t // 2), (t % 2) * 512, 512,
                                 mods["gsc1"][:, 0, :], mods["sh1"][:, 0, :], mods["buf"], uT, buT, t * 512)
                    ktiles.append((lambda k, t=t: uT[:, k, t * 512:(t + 1) * 512], buT, 512, t))
                self.prenorm(sc, io["xc"], io["kxc"], 0, NCTX, mods["gsc1"][:, 1, :], mods["sh1"][:, 1, :],
                             mods["buf"], uT, buT, L)
                ktiles.append((lambda k: uT[:, k, L:L + NCTX], buT, NCTX, 4))

                def evK(m, tag, ps, pb):
                    kb_, bkb = kh[m % 2]
                    T = 512 if tag < 4 else NCTX
                    c0 = tag * 512
                    if tag < 4:
                        self.rope_evac(rp, ps, pb, T, ck[:, 0, c0:c0 + T], ck[:, 1, c0:c0 + T], bck,
                                       kb_[:, c0:c0 + T], bkb)
                    else:
                        S.op("act", lambda e: e.activation(out=kb_[:, c0:c0 + T], in_=ps, func=AF.Copy),
                             reads=[pb], writes=[bkb])
                        S.dma("sp", bkb, lambda e: [e.dma_start(out=kT_s[m], in_=kb_[:])], reads=[bkb],
                              writes=[self.rb("kT", m)])

                self.wstream(wst, w_qkv[:, D:2 * D], KC, D, ktiles, evK, banks=[0, 1, 2, 3])

                vtiles = [(lambda k, kc=kc: uT[:, k, kc * P:(kc + 1) * P], buT, P, kc) for kc in range(NKEY // P)]

                def evV(cblk, tag, ps, pb):
                    v_, bv = vo[tag % 2]
                    eng = ("act", "dve")[tag % 2]
                    if eng == "act":
                        S.op("act", lambda e: e.activation(out=v_[:], in_=ps, func=AF.Copy), reads=[pb], writes=[bv])
                    else:
                        S.op("dve", lambda e: e.tensor_copy(out=v_[:], in_=ps), reads=[pb], writes=[bv])
                    S.dma("sp", bv, lambda e: [e.dma_start(out=v_s[tag * P:(tag + 1) * P, cblk * 512:(cblk + 1) * 512],
                                                           in_=v_[:])], reads=[bv], writes=[self.rb("v", cblk, tag)])

                self.wstream(wst, w_qkv[:, 2 * D:3 * D], KC, D, vtiles, evV, banks=[4, 5, 6], act_stationary=True)
                self.end_phase()

            with ExitStack() as ph:
                sc = self.scratch(ph)
                wst = self.wstream_bufs(ph)
                uT, buT = self.sb(ph, "uTq", [P, KC, NQ], BF16)
                rp = self.rope_bufs(ph, tabs)
                cq, bcq = self.sb(ph, "ropeq", [P, 2, LH], F32)
                S.dma("sp", bcq, lambda e: [e.dma_start(out=cq[:, 0, :], in_=tabs["ropeq_cos"]),
                                            e.dma_start(out=cq[:, 1, :], in_=tabs["ropeq_sin"])], writes=[bcq])
                qh = [self.sb(ph, "qhb", [P, NQ], BF16) for _ in range(2)]
                qtiles = []
                for t in range(2):
                    self.prenorm(sc, io["x_own"], io["kown"], t * 512, 512,
                                 mods["gsc1"][:, 0, :], mods["sh1"][:, 0, :], mods["buf"], uT, buT, t * 512)
                    qtiles.append((lambda k, t=t: uT[:, k, t * 512:(t + 1) * 512], buT, 512, t))
                if with_ctx:
                    self.prenorm(sc, io["xc"], io["kxc"], 0, NCTX, mods["gsc1"][:, 1, :], mods["sh1"][:, 1, :],
                                 mods["buf"], uT, buT, LH)
                    qtiles.append((lambda k: uT[:, k, LH:LH + NCTX], buT, NCTX, 2))
                nqt = len(qtiles)

                def evQ(m, tag, ps, pb):
                    qb_, bqb = qh[m % 2]
                    T = 512 if tag < 2 else NCTX
                    c0 = tag * 512
                    if tag < 2:
                        self.rope_evac(rp, ps, pb, T, cq[:, 0, c0:c0 + T], cq[:, 1, c0:c0 + T], bcq,
                                       qb_[:, c0:c0 + T], bqb)
                    else:
                        S.op("act", lambda e: e.activation(out=qb_[:, c0:c0 + T], in_=ps, func=AF.Copy),
                             reads=[pb], writes=[bqb])
                    if tag == nqt - 1:
                        S.dma("sp", bqb, lambda e: [e.dma_start(out=qT_s[m, :, 0:NQ], in_=qb_[:])], reads=[bqb],
                              writes=[self.rb("qT", m)])

                self.wstream(wst, w_qkv[:, 0:D], KC, D, qtiles, evQ, banks=[0, 1, 2, 3])
                self.end_phase()

            aT, baT = self.sb(lay, "attnT", [P, KC, NQ], BF16)
            with ExitStack() as ph:
                sc = self.scratch(ph, nx=1)
                khb = [self.sb(ph, "khc", [P, NKEY], BF16) for _ in range(2)]
                vhb = [self.sb(ph, "vhc", [P, NKEY // P, P], BF16) for _ in range(2)]
                qhb = [self.sb(ph, "qhc", [P, NQ], BF16) for _ in range(2)]
                eb = [self.sb(ph, "eb", [P, 2, 512], BF16) for _ in range(4)]
                es = [self.sb(ph, "esum", [P, 2, 512], F32) for _ in range(2)]
                ones32, bones32 = self.sb(ph, "ones32", [P, P], F32)
                S.op("pool", lambda e: e.memset(ones32[:], 1.0), writes=[bones32])
                fr = [self.sb(ph, "fr", [P, 512], F32) for _ in range(4)]
                zsb = [self.sb(ph, "zs", [P, 2, 512], F32) for _ in range(2)]
                ei = 0
                si = 0
                qi = 0
                pend = [None]
                psall = self.psall
                for h in range(NH):
                    k_, bk = khb[h % 2]
                    v_, bv = vhb[h % 2]
                    q_, bq = qhb[h % 2]
                    S.dma("sp", bk, lambda e: [e.dma_start(out=k_[:], in_=kT_s[h])], reads=[self.rb("kT", h)],
                          writes=[bk])
                    S.dma("sp", bv, lambda e: [e.dma_start(out=v_[:], in_=v_s[:, h * P:(h + 1) * P].rearrange(
                        "(c p) e -> p c e", p=P))],
                          reads=[self.rb("v", h // 4, kc) for kc in range(NKEY // P)], writes=[bv])
                    S.dma("sp", bq, lambda e: [e.dma_start(out=q_[:], in_=qT_s[h, :, 0:NQ])], reads=[self.rb("qT", h)],
                          writes=[bq])
                    qts = [(0, 512, 0, NKEY // P), (512, 512, 0, NKEY // P)]
                    if with_ctx:
                        qts.append((LH, NCTX, L // P, NKEY // P))
                    for (q0, T, kc0, kc1) in qts:
                        esum, besum = es[qi % 2]
                        ab = 4 + 2 * (qi % 2)
                        qi += 1
                        its = list(range(kc0, kc1))
                        slots = {}

                        def emit_qk(i):
                            nonlocal si, ei
                            kc = its[i]
                            p2 = 2 * (si % 2)
                            si += 1
                            e_, be = eb[ei % 4]
                            ei += 1

                            def qk(e):
                                e.matmul(psall[:, p2, :T], k_[0:64, kc * P:(kc + 1) * P], q_[0:64, q0:q0 + T],
                                         start=True, stop=True)
                                return e.matmul(psall[:, p2 + 1, :T], k_[64:128, kc * P:(kc + 1) * P],
                                                q_[64:128, q0:q0 + T], start=True, stop=True)
                            S.op("pe", qk, reads=[bk, bq], writes=[self.pb[p2], self.pb[p2 + 1]])
                            S.op("act", lambda e: e.activation(out=e_[:, :, :T], in_=psall[:, p2:p2 + 2, :T], func=AF.Exp,
                                                               scale=0.125),
                                 reads=[self.pb[p2], self.pb[p2 + 1]], writes=[be])
                            slots[i] = (e_, be)

                        def emit_pv(i):
                            kc = its[i]
                            e_, be = slots.pop(i)

                            def pv(e):
                                e.matmul(self.ps[ab][:, :T], v_[:, kc, :], e_[:, 0, :T],
                                         start=(kc == kc0), stop=(kc == kc1 - 1))
                                return e.matmul(self.ps[ab + 1][:, :T], v_[:, kc, :], e_[:, 1, :T],
                                                start=(kc == kc0), stop=(kc == kc1 - 1))
                            S.op("pe", pv, reads=[be, bv], writes=[self.pb[ab], self.pb[ab + 1]])
                            if kc == kc0:
                                S.op("dve", lambda e: e.tensor_copy(out=esum[:, :, :T], in_=e_[:, :, :T]),
                                     reads=[be], writes=[besum])
                            else:
                                S.op("dve", lambda e: e.tensor_tensor(out=esum[:, :, :T], in0=esum[:, :, :T],
                                                                      in1=e_[:, :, :T], op=ALU.add),
                                     reads=[be, besum], writes=[besum])

                        LA = 1
                        for i in range(min(LA, len(its))):
                            emit_qk(i)
                        for i in range(len(its)):
                            if i + LA < len(its):
                                emit_qk(i + LA)
                            emit_pv(i)
                            if i == 4 and pend[0] is not None:
                                pend[0]()
                                pend[0] = None
                        if pend[0] is not None:
                            pend[0]()
                            pend[0] = None

                        def finish(T=T, q0=q0, h=h, esum=esum, besum=besum, ab=ab, zsq=zsb[qi % 2]):
                            nonlocal si
                            zb = 2 * (si % 2)
                            si += 1

                            def zz(e):
                                e.matmul(psall[:, zb, :T], ones32[:], esum[:, 0, :T], start=True, stop=True)
                                return e.matmul(psall[:, zb + 1, :T], ones32[:], esum[:, 1, :T], start=True, stop=True)
                            S.op("pe", zz, reads=[besum, bones32], writes=[self.pb[zb], self.pb[zb + 1]])
                            zs, bzs = zsq
                            o0, bo0 = fr[1]
                            t1, bt1 = fr[2]
                            oo, boo = fr[3]
                            S.op("act", lambda e: e.activation(out=zs[:, :, :T], in_=psall[:, zb:zb + 2, :T], func=AF.Ln),
                                 reads=[self.pb[zb], self.pb[zb + 1]], writes=[bzs])
                            S.op("act", lambda e: e.activation(out=zs[:, :, :T], in_=zs[:, :, :T], func=AF.Exp, scale=-1.0),
                                 reads=[bzs], writes=[bzs])
                            S.op("dve", lambda e: e.tensor_tensor(out=o0[:, :T], in0=self.ps[ab][:, :T], in1=zs[:, 0, :T],
                                                                  op=ALU.mult), reads=[self.pb[ab], bzs], writes=[bo0])
                            S.op("dve", lambda e: e.tensor_tensor(out=t1[:, :T], in0=self.ps[ab + 1][:, :T], in1=zs[:, 1, :T],
                                                                  op=ALU.mult), reads=[self.pb[ab + 1], bzs], writes=[bt1])
                            S.op("dve", lambda e: e.scalar_tensor_tensor(out=oo[:, :T], in0=t1[:, :T], scalar=neglam,
                                                                         in1=o0[:, :T], op0=ALU.mult, op1=ALU.add),
                                 reads=[bt1, bo0, bcst], writes=[boo])
                            rstd, brstd = self.stats(sc, [(oo[:, :T], boo)], T, P, SUBLN_EPS, zb, 0)
                            S.op("dve", lambda e: e.scalar_tensor_tensor(out=aT[:, h, q0:q0 + T], in0=oo[:, :T], scalar=gsl,
                                                                         in1=rstd[:, :T], op0=ALU.mult, op1=ALU.mult),
                                 reads=[boo, brstd, bcst], writes=[baT])
                        pend[0] = finish
                if pend[0] is not None:
                    pend[0]()
                    pend[0] = None
                self.end_phase()

            tiles = [(0, 512, 0, io["x_own"], io["kown"], io["dst_own"], io["kdst_own"], 0),
                     (512, 512, 0, io["x_own"], io["kown"], io["dst_own"], io["kdst_own"], 512)]
            if with_ctx:
                tiles.append((LH, NCTX, 1, io["xc"], io["kxc"], io["dst_c"], io["kdst_c"], 0))
            self.proj_post(aT, baT, w_o, tiles, "gate1", mods)

    def rope_bufs(self, ph, tabs):
        rp = {"i": 0, "banks": [4, 5]}
        rp["qf"] = [self.sb(ph, "qf", [P, 512], F32) for _ in range(2)]
        rp["t1"] = [self.sb(ph, "t1", [P, 512], F32) for _ in range(2)]
        rp["t2"] = [self.sb(ph, "t2", [P, 512], F32) for _ in range(2)]
        rp["perm"], rp["bperm"] = self.sb(ph, "perm", [P, P], F32)
        self.S.dma("sp", rp["bperm"], lambda e: [e.dma_start(out=rp["perm"][:], in_=tabs["perm"])],
                   writes=[rp["bperm"]])
        return rp

    def fourier_layer(self, io, w_f, tabs, scr, mods):
        S = self.S
        fT_s = scr["fT"]
        NT = LH + NCTX
        NLC = NKEY // P
        with ExitStack() as ph:
            sc = self.scratch(ph)
            uT, buT = self.sb(ph, "uTall", [P, KC, NKEY], BF16)
            cc, bcc = self.sb(ph, "dftc", [P, 2, 4, 512], BF16)
            lt_, blt = self.sb(ph, "dftl", [P, 2, KC, 512], BF16)
            c256, bc256 = self.sb(ph, "dft256", [P, 2, 2, NCTX], BF16)
            AB, bAB = self.sb(ph, "AB", [P, 2, NLC, 512], BF16)
            fo = [self.sb(ph, "fo", [P, 512], BF16) for _ in range(2)]
            S.dma("sp", bcc, lambda e: [e.dma_start(out=cc[:, t, :, :], in_=tabs["dftc"][t].rearrange("(j p) m -> p j m", p=P))
                                        for t in range(2)], writes=[bcc])
            S.dma("sp", bc256, lambda e: [e.dma_start(out=c256[:, t, :, :], in_=tabs["dft256"][t].rearrange("(j p) m -> p j m", p=P))
                                          for t in range(2)], writes=[bc256])
            for t in range(4):
                self.prenorm(sc, io["x_full"][t // 2], ("xfull", io["kfull"], t // 2), (t % 2) * 512, 512,
                             mods["gsc1"][:, 0, :], mods["sh1"][:, 0, :], mods["buf"], uT, buT, t * 512)
            self.prenorm(sc, io["xc"], io["kxc"], 0, NCTX, mods["gsc1"][:, 1, :], mods["sh1"][:, 1, :],
                         mods["buf"], uT, buT, L)
            bi = 0
            fi = 0
            for g in range(4):
                for lc in range(NLC):
                    for t in range(2):
                        bk = bi % 4
                        bi += 1
                        ps, pb = self.ps[bk], self.pb[bk]
                        S._deps("pe", [buT, bcc], [pb])
                        for j in range(3):
                            self.nc.tensor.matmul(ps[:, :512], uT[:, g * 4 + j, lc * P:(lc + 1) * P], cc[:, t, j, :],
                                                  start=(j == 0), stop=False)
                        S.op("pe", lambda e: e.matmul(ps[:, :512], uT[:, g * 4 + 3, lc * P:(lc + 1) * P], cc[:, t, 3, :],
                                                      start=False, stop=True), reads=[buT, bcc], writes=[pb])
                        if (lc + t) % 2 == 0:
                            S.op("act", lambda e: e.activation(out=AB[:, t, lc, :], in_=ps[:, :512], func=AF.Copy),
                                 reads=[pb], writes=[bAB])
                        else:
                            S.op("dve", lambda e: e.tensor_copy(out=AB[:, t, lc, :], in_=ps[:, :512]),
                                 reads=[pb], writes=[bAB])
                for lt in range(2):
                    S.dma("sp", blt, lambda e: [e.dma_start(out=lt_[:, t, :, :],
                                                            in_=tabs["dftl"][t, :, lt * 512:(lt + 1) * 512].rearrange(
                                                                "(c p) m -> p c m", p=P)) for t in range(2)],
                          writes=[blt])
                    for mc in range(4):
                        bk = 4 + (fi % 3)
                        ps, pb = self.ps[bk], self.pb[bk]
                        S._deps("pe", [bAB, blt], [pb])
                        n = 0
                        for t in range(2):
                            for lc in range(KC):
                                n += 1
                                if n < 2 * KC:
                                    self.nc.tensor.matmul(ps[:, :512], AB[:, t, lc, mc * P:(mc + 1) * P], lt_[:, t, lc, :],
                                                          start=(n == 1), stop=False)
                                else:
                                    S.op("pe", lambda e: e.matmul(ps[:, :512], AB[:, t, lc, mc * P:(mc + 1) * P],
                                                                  lt_[:, t, lc, :], start=False, stop=True),
                                         reads=[bAB, blt], writes=[pb])
                        f_, bf = fo[fi % 2]
                        fi += 1
                        S.op("act", lambda e: e.activation(out=f_[:, :512], in_=ps[:, :512], func=AF.Copy, scale=1.0 / 1024.0),
                             reads=[pb], writes=[bf])
                        ch = g * 4 + mc
                        S.dma("sp", bf, lambda e: [e.dma_start(out=fT_s[ch * P:(ch + 1) * P, lt * 512:(lt + 1) * 512],
                                                               in_=f_[:, :512])], reads=[bf], writes=[self.rb("fT", ch, lt)])
                for mc in range(4):
                    bk = 4 + (fi % 3)
                    ps, pb = self.ps[bk], self.pb[bk]
                    S._deps("pe", [bAB, bc256], [pb])
                    n = 0
                    for t in range(2):
                        for lc in range(2):
                            n += 1
                            if n < 4:
                                self.nc.tensor.matmul(ps[:, :NCTX], AB[:, t, KC + lc, mc * P:(mc + 1) * P], c256[:, t, lc, :],
                                                      start=(n == 1), stop=False)
                            else:
                                S.op("pe", lambda e: e.matmul(ps[:, :NCTX], AB[:, t, KC + lc, mc * P:(mc + 1) * P],
                                                              c256[:, t, lc, :], start=False, stop=True),
                                     reads=[bAB, bc256], writes=[pb])
                    f_, bf = fo[fi % 2]
                    fi += 1
                    S.op("act", lambda e: e.activation(out=f_[:, :NCTX], in_=ps[:, :NCTX], func=AF.Copy,
                                                       scale=float(1.0 / math.sqrt(NCTX * 512.0))),
                         reads=[pb], writes=[bf])
                    ch = g * 4 + mc
                    S.dma("sp", bf, lambda e: [e.dma_start(out=fT_s[ch * P:(ch + 1) * P, LH:LH + NCTX], in_=f_[:, :NCTX])],
                          reads=[bf], writes=[self.rb("fT", ch, 2)])
            self.end_phase()
        with ExitStack() as lay:
            fT, bfT = self.sb(lay, "fT", [P, KC, NT], BF16)
            S.dma("sp", bfT, lambda e: [e.dma_start(out=fT[:], in_=fT_s.rearrange("(k p) t -> p k t", p=P))],
                  reads=[self.rb("fT", ch, x) for ch in range(KC) for x in range(3)], writes=[bfT])
            tiles = [(0, 512, 0, io["x_own"], io["kown"], io["dst_own"], io["kdst_own"], 0),
                     (512, 512, 0, io["x_own"], io["kown"], io["dst_own"], io["kdst_own"], 512),
                     (LH, NCTX, 1, io["xc"], io["kxc"], io["dst_c"], io["kdst_c"], 0)]
            self.proj_post(fT, bfT, w_f, tiles, "gate1", mods)

    def stats_stream(self, sc, loads, T, dst, bdst, bank=7):
        S = self.S
        ps, pb = self.ps[bank], self.pb[bank]
        nx = len(sc["xk"])
        for k in range(KC):
            xk, bxk = sc["xk"][sc["i"] % nx]
            sc["i"] += 1
            pairs, rds = loads(k, xk)
            S.dma("sp", bxk, lambda e: [e.dma_start(out=o, in_=i_) for (o, i_) in pairs], reads=rds, writes=[bxk])
            sq, bsq = sc["sq"][k % 2]
            S.op("act", lambda e: e.activation(out=sq[:, :T], in_=xk[:, :T], func=AF.Square), reads=[bxk], writes=[bsq])
            S.op("pe", lambda e: e.matmul(ps[:, :T], self.ones[:], sq[:, :T], start=(k == 0), stop=(k == KC - 1)),
                 reads=[bsq, self.b_ones], writes=[pb])
        rt, brt = sc["rt"]
        S.op("act", lambda e: e.activation(out=rt[:, :T], in_=ps[:, :T], func=AF.Ln, bias=self.epsb[:, 0:1], scale=1.0 / D),
             reads=[pb, self.b_eps], writes=[brt])
        S.op("act", lambda e: e.activation(out=dst, in_=rt[:, :T], func=AF.Exp, scale=-0.5), reads=[brt], writes=[bdst])

    def pool_layer(self, li, io, w_pool, tabs, scr, mods):
        S = self.S
        NT = LH + NCTX
        LP = LH + 16
        CP = NCTX + 16
        xf = io["x_full"]
        with ExitStack() as lay:
            mT, bmT = self.sb(lay, "mT", [P, KC, NT], BF16)
            with ExitStack() as ph:
                sc = self.scratch(ph)
                rs, brs = self.sb(ph, "rsall", [P, LH + 16 + NCTX], F32)
                xo = [self.sb(ph, "xo", [P, LH], F32) for _ in range(2)]
                xh = [self.sb(ph, "xh", [P, 16], F32) for _ in range(2)]
                xc_ = [self.sb(ph, "xcc", [P, NCTX], F32) for _ in range(2)]
                th = [self.sb(ph, "th", [P, 16], F32) for _ in range(2)]
                up = [self.sb(ph, "up", [P, LP], F32) for _ in range(2)]
                uc = [self.sb(ph, "uc", [P, CP], F32) for _ in range(2)]
                pa = [self.sb(ph, "pa", [P, LP], F32) for _ in range(2)]
                inv, binv = self.sb(ph, "inv", [P, 4, LH], F32)
                invc, binvc = self.sb(ph, "invc", [P, 4, NCTX], F32)
                msk, bmsk = self.sb(ph, "msk", [P, 2], F32)
                S.dma("sp", binv, lambda e: [e.dma_start(out=inv[:], in_=tabs["pinv"])], writes=[binv])
                S.dma("sp", binvc, lambda e: [e.dma_start(out=invc[:], in_=tabs["pinvc"])], writes=[binvc])
                S.dma("sp", bmsk, lambda e: [e.dma_start(out=msk[:], in_=tabs["pmsk"])], writes=[bmsk])
                for j in range(2):
                    S.op("pool", lambda e: e.memset(uc[j][0][:], 0.0), writes=[uc[j][1]])
                for t in range(2):
                    self.stats_stream(sc, lambda k, xk, t=t: ([(xk[:, :512], io["x_own"][k * P:(k + 1) * P, t * 512:(t + 1) * 512])],
                                                              [self.rb(io["kown"], t, k)]), 512, rs[:, t * 512:(t + 1) * 512], brs)
                self.stats_stream(sc, lambda k, xk: ([(xk[:, 0:8], xf[0](k, LH - 8, LH)),
                                                      (xk[:, 8:16], xf[1](k, 0, 8))],
                                                     [self.rb(("xfull", io["kfull"], 0), 1, k), self.rb(("xfull", io["kfull"], 1), 0, k)]),
                                  16, rs[:, LH:LH + 16], brs)
                self.stats_stream(sc, lambda k, xk: ([(xk[:, :NCTX], io["xc"][k * P:(k + 1) * P, 0:NCTX])],
                                                     [self.rb(io["kxc"], 0, k)]), NCTX, rs[:, LH + 16:LH + 16 + NCTX], brs)
                gsc, sh = mods["gsc1"], mods["sh1"]
                bm = mods["buf"]
                for k in range(KC):
                    g = k // 4
                    w = 2 << g
                    x_, bx = xo[k % 2]
                    h_, bh = xh[k % 2]
                    c_, bc = xc_[k % 2]
                    t_, bt = th[k % 2]
                    u_, bu = up[k % 2]
                    uc_, buc = uc[k % 2]
                    S.dma("sp", bx, lambda e: [e.dma_start(out=x_[:], in_=io["x_own"][k * P:(k + 1) * P, :])],
                          reads=[self.rb(io["kown"], 0, k), self.rb(io["kown"], 1, k)], writes=[bx])
                    S.dma("sp", bh, lambda e: [e.dma_start(out=h_[:, 0:8], in_=xf[0](k, LH - 8, LH)),
                                               e.dma_start(out=h_[:, 8:16], in_=xf[1](k, 0, 8))],
                          reads=[self.rb(("xfull", io["kfull"], 0), 1, k), self.rb(("xfull", io["kfull"], 1), 0, k)], writes=[bh])
                    S.dma("sp", bc, lambda e: [e.dma_start(out=c_[:], in_=io["xc"][k * P:(k + 1) * P, :])],
                          reads=[self.rb(io["kxc"], 0, k)], writes=[bc])
                    S.op("dve", lambda e: e.scalar_tensor_tensor(out=u_[:, 8:8 + LH], in0=x_[:], scalar=gsc[:, 0, k:k + 1],
                                                                 in1=rs[:, 0:LH], op0=ALU.mult, op1=ALU.mult),
                         reads=[bx, brs, bm], writes=[bu])
                    S.op("act", lambda e: e.activation(out=u_[:, 8:8 + LH], in_=u_[:, 8:8 + LH], func=AF.Identity,
                                                       bias=sh[:, 0, k:k + 1], scale=1.0), reads=[bu, bm], writes=[bu])
                    S.op("dve", lambda e: e.scalar_tensor_tensor(out=t_[:], in0=h_[:], scalar=gsc[:, 0, k:k + 1],
                                                                 in1=rs[:, LH:LH + 16], op0=ALU.mult, op1=ALU.mult),
                         reads=[bh, brs, bm], writes=[bt])
                    S.op("act", lambda e: e.activation(out=t_[:], in_=t_[:], func=AF.Identity, bias=sh[:, 0, k:k + 1], scale=1.0),
                         reads=[bt, bm], writes=[bt])
                    S.op("dve", lambda e: e.tensor_scalar(out=u_[:, 0:8], in0=t_[:, 0:8], scalar1=msk[:, 0:1], scalar2=None,
                                                          op0=ALU.mult), reads=[bt, bmsk], writes=[bu])
                    S.op("dve", lambda e: e.tensor_scalar(out=u_[:, 8 + LH:16 + LH], in0=t_[:, 8:16], scalar1=msk[:, 1:2],
                                                          scalar2=None, op0=ALU.mult), reads=[bt, bmsk], writes=[bu])
                    S.op("dve", lambda e: e.scalar_tensor_tensor(out=uc_[:, 8:8 + NCTX], in0=c_[:], scalar=gsc[:, 1, k:k + 1],
                                                                 in1=rs[:, LH + 16:LH + 16 + NCTX], op0=ALU.mult, op1=ALU.mult),
                         reads=[bc, brs, bm], writes=[buc])
                    S.op("act", lambda e: e.activation(out=uc_[:, 8:8 + NCTX], in_=uc_[:, 8:8 + NCTX], func=AF.Identity,
                                                       bias=sh[:, 1, k:k + 1], scale=1.0), reads=[buc, bm], writes=[buc])
                    for (src, bsrc, n_, ln, itab, bit, col0) in ((u_, bu, LH, LP, inv, binv, 0), (uc_, buc, NCTX, CP, invc, binvc, LH)):
                        cur, bcur = src, bsrc
                        clen = ln
                        for s_ in range(g + 1):
                            step = 1 << s_
                            nxt, bnxt = pa[s_ % 2]
                            eng = ("pool", "dve")[s_ % 2]
                            nl = clen - step
                            S.op(eng, lambda e: e.tensor_tensor(out=nxt[:, 0:nl], in0=cur[:, 0:nl], in1=cur[:, step:step + nl],
                                                                op=ALU.add), reads=[bcur], writes=[bnxt])
                            cur, bcur, clen = nxt, bnxt, nl
                        off = 8 - w // 2
                        oth, both = pa[(g + 1) % 2]
                        S.op("dve", lambda e: e.tensor_tensor(out=oth[:, 0:n_], in0=cur[:, off:off + n_], in1=itab[:, g, :],
                                                              op=ALU.mult), reads=[bcur, bit], writes=[both])
                        S.op("pool", lambda e: e.tensor_tensor(out=mT[:, k, col0:col0 + n_], in0=oth[:, 0:n_],
                                                               in1=src[:, 8:8 + n_], op=ALU.subtract),
                             reads=[both, bsrc], writes=[bmT])
                self.end_phase()
            with ExitStack() as ph:
                sc = self.scratch(ph)
                wst = self.wstream_bufs(ph, small=True)
                yT, byT = self.sb(ph, "yTp", [P, KC, NT], F32)
                gv = mods["gv"]
                tl = [(0, 512), (512, 512), (LH, NCTX)]
                for g in range(4):
                    def ev(m, tag, ps, pb):
                        c0, T = tl[tag]
                        ch = g * 4 + m
                        S.op("act", lambda e: e.activation(out=yT[:, ch, c0:c0 + T], in_=ps, func=AF.Copy,
                                                           scale=gv[:, li, 4, ch:ch + 1]),
                             reads=[pb, mods["bgv"]], writes=[byT])
                    tiles = [(lambda k, c0=c0, T=T: mT[:, g * 4 + k, c0:c0 + T], bmT, T, ti) for ti, (c0, T) in enumerate(tl)]
                    self.wstream(wst, w_pool[g], 4, 512, tiles, ev, banks=[0, 1, 2, 3])
                self.postnorm(sc, yT[:, :, 0:512], byT, 512, mods["gate1"][:, 0, :], mods["buf"], io["x_own"], io["kown"],
                              io["dst_own"], io["kdst_own"], 0)
                self.postnorm(sc, yT[:, :, 512:1024], byT, 512, mods["gate1"][:, 0, :], mods["buf"], io["x_own"], io["kown"],
                              io["dst_own"], io["kdst_own"], 512)
                self.postnorm(sc, yT[:, :, LH:NT], byT, NCTX, mods["gate1"][:, 1, :], mods["buf"], io["xc"], io["kxc"],
                              io["dst_c"], io["kdst_c"], 0)
                self.end_phase()

    def layer(self, li, io, Wt, tabs, scr, mods, after_latent=None):
        kind = li % 3
        last = li == DEPTH - 1
        if kind == 0:
            self.attn_layer(li, li // 3, io, Wt["w_qkv"], Wt["w_o"], Wt["lamv"], Wt["gsub"], tabs, scr, mods, not last)
        elif kind == 1:
            self.fourier_layer(io, Wt["w_f"], tabs, scr, mods)
        else:
            self.pool_layer(li, io, Wt["w_pool"], tabs, scr, mods)
        tiles = [(io["dst_own"], io["kdst_own"], io["out_own"], io["kout_own"], 0, 512, 0),
                 (io["dst_own"], io["kdst_own"], io["out_own"], io["kout_own"], 512, 512, 0)]
        if not last:
            tiles.append((io["dst_c"], io["kdst_c"], io["out_c"], io["kout_c"], 0, NCTX, 1))
        self.mlp_phase(tiles, Wt["w1"], Wt["w2"], mods, after_latent, wc=scr.get("wc"))

def host_gvec(inputs):
    gv = np.zeros((P, DEPTH, 5, KC), np.float32)
    for i in range(DEPTH):
        vecs = [inputs["g_mix_pre"][i], inputs["g_mix_post"][i], inputs["g_mlp_pre"][i], inputs["g_mlp_post"][i],
                inputs["pool_scale"][0]]
        for v, vec in enumerate(vecs):
            gv[:, i, v, :] = np.asarray(vec, np.float32).reshape(KC, P).T
    return gv


def _bf16(a):
    return np.asarray(a, np.float32).astype(ml_dtypes.bfloat16)


_TAB_CACHE = {}


def host_tables():
    if _TAB_CACHE:
        return _TAB_CACHE
    T = _TAB_CACHE
    t = np.arange(L)
    row = (t // 64).astype(np.float32)
    col = (t % 64).astype(np.float32)
    inv = (1.0 / (np.float32(10000.0) ** (np.arange(16, dtype=np.float32) / np.float32(16)))).astype(np.float32)
    cos = np.zeros((P, L), np.float32)
    sin = np.zeros((P, L), np.float32)
    perm = np.zeros((P, P), np.float32)
    for p in range(P):
        d = p % 64
        axis = d // 32
        half = (d % 32) // 16
        f = d % 16
        ang = ((row if axis == 0 else col) * inv[f]).astype(np.float32)
        cos[p] = np.cos(ang)
        sin[p] = np.sin(ang) * (-1.0 if half == 0 else 1.0)
        partner = p + 16 if half == 0 else p - 16
        perm[partner, p] = 1.0
    T["rope_cos"], T["rope_sin"], T["perm"] = cos, sin, perm
    c = np.arange(512)
    angc = 2.0 * np.pi * ((c[:, None] * c[None, :]) % 512) / 512.0
    T["dftc"] = _bf16(np.stack([np.cos(angc), np.sin(angc)]))
    l = np.arange(L)
    angl = 2.0 * np.pi * ((l[:, None] * l[None, :]) % L) / float(L)
    T["dftl_full"] = np.stack([np.cos(angl), -np.sin(angl)]).astype(np.float32)
    x = np.arange(NCTX)
    angx = 2.0 * np.pi * ((x[:, None] * x[None, :]) % NCTX) / float(NCTX)
    T["dft256"] = _bf16(np.stack([np.cos(angx), -np.sin(angx)]))

    def invcnt(n, lo_t, num):
        out = np.zeros((4, num), np.float32)
        for g, w in enumerate((2, 4, 8, 16)):
            tt = np.arange(lo_t, lo_t + num)
            lo = np.clip(tt - w // 2, 0, n)
            hi = np.clip(tt - w // 2 + w, 0, n)
            out[g] = 1.0 / (hi - lo).astype(np.float32)
        return out
    T["pinv"] = [np.ascontiguousarray(np.broadcast_to(invcnt(L, hf * LH, LH)[None], (P, 4, LH))) for hf in range(2)]
    T["pinvc"] = np.ascontiguousarray(np.broadcast_to(invcnt(NCTX, 0, NCTX)[None], (P, 4, NCTX)))
    T["dftl"] = [_bf16(T["dftl_full"][:, :, hf * LH:(hf + 1) * LH]) for hf in range(2)]
    return T


def build_fused():
    nc = bass.Bass("TRN2", target_bir_lowering=False)
    di = lambda n, sh, dt=F32: nc.dram_tensor(n, list(sh), dt, kind="ExternalInput").ap()
    do = lambda n, sh, dt=F32: nc.dram_tensor(n, list(sh), dt, kind="ExternalOutput").ap()
    dn = lambda n, sh, dt=F32: nc.dram_tensor(n, list(sh), dt).ap()
    x_full0 = di("x_full", [2, D, LH])
    x_own0 = di("x_own", [D, LH])
    xc0 = di("xc", [D, NCTX])
    cT = di("cT", [P, KC, 2])
    w_mod_h = di("w_mod_h", [DEPTH, D, MSH])
    b_mod_h = di("b_mod_h", [DEPTH * MSH])
    sel = di("sel", [2, 2])
    gvec = di("gvec", [P, DEPTH, 5, KC])
    w1 = di("w_mlp_in", [DEPTH, D, DFF])
    w2 = di("w_mlp_out", [DEPTH, DFF, D])
    w_qkv = di("w_qkv", [2, D, 3 * D])
    w_o = di("w_attn_out", [2, D, D])
    lamv = di("lamv", [2, 4, 64])
    gsub = di("gsub", [2, P])
    w_f = di("w_f", [D, D])
    w_pool = di("w_pool", [4, 512, 512])
    tabs = {"perm": di("perm", [P, P]), "ropek_cos": di("ropek_cos", [P, L]), "ropek_sin": di("ropek_sin", [P, L]),
            "ropeq_cos": di("ropeq_cos", [P, LH]), "ropeq_sin": di("ropeq_sin", [P, LH]),
            "dftc": di("dftc", [2, 512, 512], BF16), "dftl": di("dftl", [2, L, LH], BF16),
            "dft256": di("dft256", [2, NCTX, NCTX], BF16),
            "pinv": di("pinv", [P, 4, LH]), "pinvc": di("pinvc", [P, 4, NCTX]), "pmsk": di("pmsk", [P, 2])}
    out = do("out", [D, LH])
    modh = dn("modh", [2, DEPTH * MSH])
    modall = dn("modall", [2, 2, DEPTH * MSH])
    scr = {"kT": dn("kT_s", [NH, P, NKEY], BF16), "v": dn("v_s", [NKEY, D], BF16),
           "qT": dn("qT_s", [NH, P, LH + NCTX], BF16), "fT": dn("fT_s", [D, LH + NCTX], BF16),
           "wc": dn("wcache", [32, P, 8192], BF16)}
    NCH = 8
    RPC = D // NCH
    with ExitStack() as st:
        kb = KB(nc, st)
        S = kb.S
        m = kb.alloc_mods()
        kb.load_mod_consts(m, gvec, sel)
        kb.mod_stage(cT, w_mod_h, b_mod_h, modh)
        S.coll([lambda e: e.collective_compute("AllGather", ALU.bypass, replica_groups=PAIRS,
                                               ins=[modh], outs=[modall.rearrange("r a c -> (r a) c")])],
               reads=[kb.rb("modsh")], writes=[kb.rb("modall")])
        x_own, kown = x_own0, "x_own_in"
        xc, kxc = xc0, "xc_in"
        xf = [(lambda k, c0, c1, h=h: x_full0[h, k * P:(k + 1) * P, c0:c1]) for h in range(2)]
        kfull = "in"
        for li in range(DEPTH):
            last = li == DEPTH - 1
            kind = li % 3
            io = {"x_full": xf, "kfull": kfull, "x_own": x_own, "kown": kown, "xc": xc, "kxc": kxc,
                  "dst_own": dn("xmid_own%d" % li, [D, LH]), "kdst_own": "xmid_own%d" % li,
                  "dst_c": dn("xmid_c%d" % li, [D, NCTX]), "kdst_c": "xmid_c%d" % li}
            if last:
                io["out_own"], io["kout_own"] = out, "out"
            else:
                io["out_own"], io["kout_own"] = dn("xo_own%d" % li, [D, LH]), "xo_own%d" % li
                io["out_c"], io["kout_c"] = dn("xo_c%d" % li, [D, NCTX]), "xo_c%d" % li
            Wt = {"w1": w1[li], "w2": w2[li]}
            if kind == 0:
                ai = li // 3
                Wt.update({"w_qkv": w_qkv[ai], "w_o": w_o[ai], "lamv": [lamv[ai, j] for j in range(4)], "gsub": gsub[ai]})
            elif kind == 1:
                Wt["w_f"] = w_f
            else:
                Wt["w_pool"] = [w_pool[g] for g in range(4)]
            kb.mod_prep(m, modall, "modall", li)
            nxt = {}
            if not last:
                xg = dn("xg%d" % li, [NCH, 2, RPC, LH])
                kf = "g%d" % li

                def exchange(io=io, xg=xg, kf=kf):
                    src = io["out_own"]
                    rds = [kb.rb(io["kout_own"], t, k) for t in range(2) for k in range(KC)]
                    wrs = [kb.rb(("xfull", kf, h), t, k) for h in range(2) for t in range(2) for k in range(KC)]
                    S.coll([(lambda e, j=j: e.collective_compute("AllGather", ALU.bypass, replica_groups=PAIRS,
                                                                 ins=[src[j * RPC:(j + 1) * RPC, :]],
                                                                 outs=[xg[j].rearrange("r a c -> (r a) c")]))
                            for j in range(NCH)], reads=rds, writes=wrs)
                kb.layer(li, io, Wt, tabs, scr, m)
                kfull = kf
                if (li + 1) % 3 == 2:
                    hsrc = dn("halo_src%d" % li, [D, 16])
                    hall = dn("halo_all%d" % li, [2, D, 16])
                    src = io["out_own"]
                    bh = Buf("halo%d" % li)
                    rds = [kb.rb(io["kout_own"], t, k) for t in range(2) for k in range(KC)]
                    S.dma("sp", bh, lambda e: [e.dma_start(out=hsrc[:, 0:8], in_=src[:, 0:8]),
                                               e.dma_start(out=hsrc[:, 8:16], in_=src[:, LH - 8:LH])],
                          reads=rds, writes=[kb.rb("halo_src", li)])
                    wrs = [kb.rb(("xfull", kf, h), t, k) for h in range(2) for t in range(2) for k in range(KC)]
                    S.coll([lambda e: e.collective_compute("AllGather", ALU.bypass, replica_groups=PAIRS, ins=[hsrc],
                                                           outs=[hall.rearrange("r d c -> (r d) c")])],
                           reads=[kb.rb("halo_src", li)], writes=wrs)
                    xf = [(lambda k, c0, c1, hall=hall: hall[0, k * P:(k + 1) * P, c0 - (LH - 16):c1 - (LH - 16)]),
                          (lambda k, c0, c1, hall=hall: hall[1, k * P:(k + 1) * P, c0:c1])]
                else:
                    exchange()
                    xf = [(lambda k, c0, c1, h=h, xg=xg: xg[(k * P) // RPC, h, (k * P) % RPC:(k * P) % RPC + P, c0:c1])
                          for h in range(2)]
                x_own, kown = io["out_own"], io["kout_own"]
                xc, kxc = io["out_c"], io["kout_c"]
            else:
                kb.layer(li, io, Wt, tabs, scr, m)
        S.barrier(["sp"])
    return nc


PAIRS = [[0, 1], [2, 3], [4, 5], [6, 7]]
_PROG_CACHE = {}


def kernel(**inputs):
    inputs = {k: np.asarray(v) for k, v in inputs.items()}
    tb = host_tables()
    gv = host_gvec(inputs)
    cores = list(range(NCORES))
    if "fused" not in _PROG_CACHE:
        _PROG_CACHE["fused"] = build_fused()
    x = inputs["x"]
    lamv = np.stack([np.stack([inputs["lambda_q1"][a], inputs["lambda_k1"][a], inputs["lambda_q2"][a],
                               inputs["lambda_k2"][a]]) for a in range(2)]).astype(np.float32)
    shared = {"sel": np.eye(2, dtype=np.float32), "gvec": gv,
              "w_mlp_in": inputs["w_mlp_in"], "w_mlp_out": inputs["w_mlp_out"], "w_qkv": inputs["w_qkv"],
              "w_attn_out": inputs["w_attn_out"], "lamv": lamv, "gsub": np.asarray(inputs["g_subln"], np.float32),
              "w_f": inputs["w_fourier_out"][0], "w_pool": inputs["w_pool"][0],
              "perm": tb["perm"], "ropek_cos": tb["rope_cos"], "ropek_sin": tb["rope_sin"],
              "dftc": tb["dftc"], "dft256": tb["dft256"], "pinvc": tb["pinvc"]}
    wmh = [np.ascontiguousarray(inputs["w_mod"][:, :, h * MSH:(h + 1) * MSH]) for h in range(2)]
    bmh = [np.ascontiguousarray(inputs["b_mod"][:, h * MSH:(h + 1) * MSH]).reshape(-1) for h in range(2)]
    maps = []
    for r in cores:
        b, hf = r // 2, r % 2
        xfull = np.ascontiguousarray(x[b].T.reshape(D, 2, LH).transpose(1, 0, 2))
        c2 = np.stack([inputs["c"][b], inputs["c_ctx"]]).astype(np.float32)
        msk = np.zeros((P, 2), np.float32)
        msk[:, 0] = 1.0 if hf == 1 else 0.0
        msk[:, 1] = 1.0 if hf == 0 else 0.0
        mp = dict(shared)
        mp.update({"x_full": xfull, "x_own": np.ascontiguousarray(xfull[hf]),
                   "xc": np.ascontiguousarray(inputs["ctx"][b].T),
                   "cT": np.ascontiguousarray(c2.T.reshape(KC, P, 2).transpose(1, 0, 2)),
                   "w_mod_h": wmh[hf], "b_mod_h": bmh[hf],
                   "ropeq_cos": np.ascontiguousarray(tb["rope_cos"][:, hf * LH:(hf + 1) * LH]),
                   "ropeq_sin": np.ascontiguousarray(tb["rope_sin"][:, hf * LH:(hf + 1) * LH]),
                   "dftl": tb["dftl"][hf], "pinv": tb["pinv"][hf], "pmsk": msk})
        maps.append(mp)
    res = run_bass_kernel_spmd(_PROG_CACHE["fused"], maps, core_ids=cores)
    out = np.stack([np.concatenate([res.results[2 * b]["out"], res.results[2 * b + 1]["out"]], axis=1).T
                    for b in range(4)])
    return np.ascontiguousarray(out.astype(np.float32))
```
